# Optimizing a Trainium2 kernel written in Bass

```python
import jax, jax.numpy as jnp
from jax import lax
import numpy as np

D_MODEL = 1024
BATCH = 16
SEQ = 2048
DEPTH = 1

GRID_W = 64
CTX_LEN = 256
HEAD_DIM = 128
N_Q_HEADS = 8
N_KV_HEADS = 2
Q_PER_KV = N_Q_HEADS // N_KV_HEADS
ROPE_AXIS_DIM = HEAD_DIM // 2
ROPE_THETA = 10000.0
Q_BLOCK = 128
GLA_HEADS = 4
GLA_DK = (D_MODEL // 2) // GLA_HEADS
GLA_DV = D_MODEL // GLA_HEADS
GLA_LOWRANK = 16
GLA_GATE_NORM = 16.0
GLA_CHUNK = 64
N_BRANCH = 2
D_FF = ((8 * D_MODEL + 3 * 256 - 1) // (3 * 256)) * 256
EPS = 1e-6

ATTN_Q_W = N_Q_HEADS * HEAD_DIM
ATTN_KV_W = N_KV_HEADS * HEAD_DIM
GLA_QK_W = GLA_HEADS * GLA_DK
GLA_V_W = GLA_HEADS * GLA_DV
IN_WIDTHS = (ATTN_Q_W, ATTN_KV_W, ATTN_KV_W, GLA_QK_W, GLA_QK_W, GLA_V_W, GLA_V_W, 2 * GLA_LOWRANK, N_BRANCH * D_MODEL)
D_IN = sum(IN_WIDTHS)

kernel_name = "hybrid_gqa_gla_adaln_prefix_block"


def rms_norm(x, g):
    x32 = x.astype(jnp.float32)
    y = x32 * lax.rsqrt(jnp.mean(x32 * x32, axis=-1, keepdims=True) + EPS)
    return y.astype(x.dtype) * g


def modulate(h, shift, scale):
    return h * (1 + scale) + shift


def adaln(cvec, w_ada, b_ada):
    mod = jax.nn.silu(cvec) @ w_ada + b_ada
    return jnp.split(mod, 6, axis=-1)


def split_in(p):
    idx = [int(i) for i in np.cumsum(IN_WIDTHS)[:-1]]
    return jnp.split(p, idx, axis=-1)


def rope_2d_tables(rows, dtype):
    row = jnp.repeat(jnp.arange(rows, dtype=jnp.float32), GRID_W)
    col = jnp.tile(jnp.arange(GRID_W, dtype=jnp.float32), rows)
    inv_freq = 1.0 / (ROPE_THETA ** (jnp.arange(0, ROPE_AXIS_DIM, 2, dtype=jnp.float32) / ROPE_AXIS_DIM))
    ang = jnp.concatenate([row[:, None] * inv_freq[None], col[:, None] * inv_freq[None]], axis=-1)
    return jnp.cos(ang).astype(dtype), jnp.sin(ang).astype(dtype)


def apply_rope(x, cos, sin):
    xr = x.reshape(x.shape[:-1] + (HEAD_DIM // 2, 2))
    x0, x1 = xr[..., 0], xr[..., 1]
    c, s = cos[None, :, None, :], sin[None, :, None, :]
    out = jnp.stack([x0 * c - x1 * s, x0 * s + x1 * c], axis=-1)
    return out.reshape(x.shape)


def grouped_attention(qg, k, v):
    s = jnp.einsum('bqhgd,bkhd->bhgqk', qg, k).astype(jnp.float32) * (HEAD_DIM ** -0.5)
    p = jax.nn.softmax(s, axis=-1).astype(v.dtype)
    return jnp.einsum('bhgqk,bkhd->bqhgd', p, v)


def latent_attention(q, k_all, v_all):
    B, T = q.shape[0], q.shape[1]
    nb = T // Q_BLOCK
    qb = jnp.moveaxis(q.reshape(B, nb, Q_BLOCK, N_KV_HEADS, Q_PER_KV, HEAD_DIM), 1, 0)
    o = lax.map(lambda qi: grouped_attention(qi, k_all, v_all), qb)
    return jnp.moveaxis(o, 0, 1).reshape(B, T, ATTN_Q_W)


def gla_chunked(q, k, v, log_a, s0):
    B, T, H, dk = q.shape
    dv = v.shape[-1]
    C = GLA_CHUNK
    N = T // C
    f32 = jnp.float32
    qc = q.astype(f32).reshape(B, N, C, H, dk) * (dk ** -0.5)
    kc = k.astype(f32).reshape(B, N, C, H, dk)
    vc = v.astype(f32).reshape(B, N, C, H, dv)
    b = jnp.cumsum(log_a.astype(f32).reshape(B, N, C, H, dk), axis=2)
    b_last = b[:, :, -1]
    qe = qc * jnp.exp(b)
    ke = kc * jnp.exp(-b)
    kd = kc * jnp.exp(b_last[:, :, None] - b)
    mask = jnp.tril(jnp.ones((C, C), dtype=bool))
    a_intra = jnp.where(mask, jnp.einsum('bnthk,bnshk->bnhts', qe, ke), 0.0)
    o_intra = jnp.einsum('bnhts,bnshv->bnthv', a_intra, vc)

    def step(state, inp):
        qe_c, kd_c, v_c, dl_c = inp
        o = jnp.einsum('bchk,bhkv->bchv', qe_c, state)
        state = state * dl_c[..., None] + jnp.einsum('bchk,bchv->bhkv', kd_c, v_c)
        return state, o

    xs = (jnp.moveaxis(qe, 1, 0), jnp.moveaxis(kd, 1, 0), jnp.moveaxis(vc, 1, 0), jnp.moveaxis(jnp.exp(b_last), 1, 0))
    s_fin, o_inter = lax.scan(step, s0.astype(f32), xs)
    o = o_intra + jnp.moveaxis(o_inter, 0, 1)
    return o.reshape(B, T, H, dv).astype(q.dtype), s_fin


def gla_bidir(q, k, v, la_f, la_b, s0_f, s0_b):
    o_f, s_f = gla_chunked(q, k, v, la_f, s0_f)
    flip = lambda t: jnp.flip(t, axis=1)
    o_b, s_b = gla_chunked(flip(q), flip(k), flip(v), flip(la_b), s0_b)
    return o_f + flip(o_b), s_f, s_b


def gla_decays(lowrank, up_f, up_f_b, up_b, up_b_b):
    B, T = lowrank.shape[0], lowrank.shape[1]
    lr_f, lr_b = jnp.split(lowrank, 2, axis=-1)
    la_f = jax.nn.log_sigmoid((lr_f @ up_f + up_f_b).astype(jnp.float32)) / GLA_GATE_NORM
    la_b = jax.nn.log_sigmoid((lr_b @ up_b + up_b_b).astype(jnp.float32)) / GLA_GATE_NORM
    return la_f.reshape(B, T, GLA_HEADS, GLA_DK), la_b.reshape(B, T, GLA_HEADS, GLA_DK)


def gla_heads(gq, gk, gv):
    B, T = gq.shape[0], gq.shape[1]
    return (gq.reshape(B, T, GLA_HEADS, GLA_DK), gk.reshape(B, T, GLA_HEADS, GLA_DK),
            gv.reshape(B, T, GLA_HEADS, GLA_DV))


def branch_merge(attn_o, gla_o, gla_g, merge_g, gla_norm_g, w_attn_proj, w_gla_proj, w_out):
    B, T = attn_o.shape[0], attn_o.shape[1]
    go = rms_norm(gla_o, gla_norm_g).reshape(B, T, GLA_V_W) * jax.nn.silu(gla_g)
    ya = attn_o @ w_attn_proj
    yg = go @ w_gla_proj
    ga, gb = jnp.split(merge_g, 2, axis=-1)
    return (jax.nn.sigmoid(ga) * ya + jax.nn.sigmoid(gb) * yg) @ w_out


def swiglu(h, w_ffn_in, w_ffn_out):
    a, b = jnp.split(h @ w_ffn_in, 2, axis=-1)
    return (jax.nn.silu(a) * b) @ w_ffn_out


def setup_inputs(seed: int = 0) -> dict:
    key = jax.random.key(seed)
    ks = jax.random.split(key, 21)
    nrm = jax.random.normal
    f32 = jnp.float32

    def dense(k, shape, fan_in, gain=1.0):
        return nrm(k, shape, f32) * (gain * fan_in ** -0.5)

    return {
        "x": nrm(ks[0], (BATCH, SEQ, D_MODEL), f32),
        "c": nrm(ks[1], (BATCH, D_MODEL), f32),
        "ctx": nrm(ks[2], (BATCH, CTX_LEN, D_MODEL), f32),
        "c_ctx": nrm(ks[3], (D_MODEL,), f32),
        "w_ada": dense(ks[4], (DEPTH, D_MODEL, 6 * D_MODEL), D_MODEL, 0.5),
        "b_ada": 0.02 * nrm(ks[5], (DEPTH, 6 * D_MODEL), f32),
        "norm1_g": 1.0 + 0.05 * nrm(ks[6], (DEPTH, D_MODEL), f32),
        "w_in": dense(ks[7], (DEPTH, D_MODEL, D_IN), D_MODEL),
        "q_norm_g": 1.0 + 0.05 * nrm(ks[8], (DEPTH, HEAD_DIM), f32),
        "k_norm_g": 1.0 + 0.05 * nrm(ks[9], (DEPTH, HEAD_DIM), f32),
        "gk_up_f": dense(ks[10], (DEPTH, GLA_LOWRANK, GLA_QK_W), GLA_LOWRANK),
        "gk_up_f_b": 0.1 * nrm(ks[11], (DEPTH, GLA_QK_W), f32),
        "gk_up_b": dense(ks[12], (DEPTH, GLA_LOWRANK, GLA_QK_W), GLA_LOWRANK),
        "gk_up_b_b": 0.1 * nrm(ks[13], (DEPTH, GLA_QK_W), f32),
        "gla_norm_g": 1.0 + 0.05 * nrm(ks[14], (DEPTH, GLA_DV), f32),
        "w_attn_proj": dense(ks[15], (DEPTH, ATTN_Q_W, D_MODEL), ATTN_Q_W),
        "w_gla_proj": dense(ks[16], (DEPTH, GLA_V_W, D_MODEL), GLA_V_W),
        "w_out": dense(ks[17], (DEPTH, D_MODEL, D_MODEL), D_MODEL),
        "norm2_g": 1.0 + 0.05 * nrm(ks[18], (DEPTH, D_MODEL), f32),
        "w_ffn_in": dense(ks[19], (DEPTH, D_MODEL, 2 * D_FF), D_MODEL),
        "w_ffn_out": dense(ks[20], (DEPTH, D_FF, D_MODEL), D_FF),
    }


def reference(x, c, ctx, c_ctx, w_ada, b_ada, norm1_g, w_in, q_norm_g, k_norm_g, gk_up_f, gk_up_f_b,
              gk_up_b, gk_up_b_b, gla_norm_g, w_attn_proj, w_gla_proj, w_out, norm2_g, w_ffn_in, w_ffn_out):
    B, T, _ = x.shape
    Tc = ctx.shape[1]
    ROWS = T // GRID_W
    cos, sin = rope_2d_tables(ROWS, x.dtype)

    for l in range(DEPTH):
        sh1, sc1, g1, sh2, sc2, g2 = [m[:, None, :] for m in adaln(c, w_ada[l], b_ada[l])]
        sh1c, sc1c, g1c, sh2c, sc2c, g2c = adaln(c_ctx, w_ada[l], b_ada[l])

        hc = modulate(rms_norm(ctx, norm1_g[l]), sh1c, sc1c)
        aq_c, ak_c, av_c, gq_c, gk_c, gv_c, gg_c, glr_c, mg_c = split_in(hc @ w_in[l])
        k_c = rms_norm(ak_c.reshape(B, Tc, N_KV_HEADS, HEAD_DIM), k_norm_g[l])
        v_c = av_c.reshape(B, Tc, N_KV_HEADS, HEAD_DIM)
        la_f_c, la_b_c = gla_decays(glr_c, gk_up_f[l], gk_up_f_b[l], gk_up_b[l], gk_up_b_b[l])
        s_zero = jnp.zeros((B, GLA_HEADS, GLA_DK, GLA_DV), jnp.float32)
        gla_c, s_f, s_b = gla_bidir(*gla_heads(gq_c, gk_c, gv_c), la_f_c, la_b_c, s_zero, s_zero)

        hx = modulate(rms_norm(x, norm1_g[l]), sh1, sc1)
        aq, ak, av, gq, gk, gv, gg, glr, mg = split_in(hx @ w_in[l])
        q_x = apply_rope(rms_norm(aq.reshape(B, T, N_Q_HEADS, HEAD_DIM), q_norm_g[l]), cos, sin)
        k_x = apply_rope(rms_norm(ak.reshape(B, T, N_KV_HEADS, HEAD_DIM), k_norm_g[l]), cos, sin)
        v_x = av.reshape(B, T, N_KV_HEADS, HEAD_DIM)
        attn_x = latent_attention(q_x, jnp.concatenate([k_c, k_x], axis=1), jnp.concatenate([v_c, v_x], axis=1))
        la_f, la_b = gla_decays(glr, gk_up_f[l], gk_up_f_b[l], gk_up_b[l], gk_up_b_b[l])
        gla_x, _, _ = gla_bidir(*gla_heads(gq, gk, gv), la_f, la_b, s_f, s_b)
        x = x + g1 * branch_merge(attn_x, gla_x, gg, mg, gla_norm_g[l], w_attn_proj[l], w_gla_proj[l], w_out[l])
        x = x + g2 * swiglu(modulate(rms_norm(x, norm2_g[l]), sh2, sc2), w_ffn_in[l], w_ffn_out[l])

        if l < DEPTH - 1:
            q_c = rms_norm(aq_c.reshape(B, Tc, N_Q_HEADS, HEAD_DIM), q_norm_g[l])
            attn_c = grouped_attention(q_c.reshape(B, Tc, N_KV_HEADS, Q_PER_KV, HEAD_DIM), k_c, v_c).reshape(B, Tc, ATTN_Q_W)
            ctx = ctx + g1c * branch_merge(attn_c, gla_c, gg_c, mg_c, gla_norm_g[l], w_attn_proj[l], w_gla_proj[l], w_out[l])
            ctx = ctx + g2c * swiglu(modulate(rms_norm(ctx, norm2_g[l]), sh2c, sc2c), w_ffn_in[l], w_ffn_out[l])

    return x
```

```python
import numpy as np
import concourse.bass as bass
import concourse.mybir as mybir
from concourse.bass_utils import run_bass_kernel_spmd

F32 = mybir.dt.float32
BF16 = mybir.dt.bfloat16
AF = mybir.ActivationFunctionType
ALU = mybir.AluOpType
AX = mybir.AxisListType

D = 1024
T = 2048
TC = 256
NB = 2
NCORES = 8
DFF = 2816
NJ = DFF // 128
EPS = 1e-6
KT = 8
ATT_SCALE = 128.0 ** -0.5
ATT_SHIFT = 30.0
GLA_QSCALE = 128.0 ** -0.5

WB_Q = 0
WB_K = 8
WB_GQK = 10
WB_GA = 18
WB_GB = 26
N_WB = 34


class Tk:
    __slots__ = ("w", "r", "dsem", "dcnt", "name", "excl", "ws")

    def __init__(self, name="", excl=False):
        self.excl = excl
        self.w = None
        self.ws = {}
        self.r = {}
        self.dsem = None
        self.dcnt = 0
        self.name = name


class _Eng:
    def __init__(self, h, sem):
        self.h = h
        self.sem = sem
        self.cnt = 0
        self.known = {}
        self.is_pe = False


class Sched:
    def __init__(self, nc):
        self.nc = nc
        self.E = {
            "pe": _Eng(nc.tensor, nc.alloc_semaphore("s_pe")),
            "act": _Eng(nc.scalar, nc.alloc_semaphore("s_act")),
            "dve": _Eng(nc.vector, nc.alloc_semaphore("s_dve")),
            "pool": _Eng(nc.gpsimd, nc.alloc_semaphore("s_pool")),
            "sp": _Eng(nc.sync, nc.alloc_semaphore("s_sp")),
        }
        self.E["pe"].is_pe = True
        self.nsem = 5
        self.dsems = []
        self.final = {}

    def _wait(self, E, reads, writes, wpart=()):
        need = {}

        def add(sp):
            if sp is None:
                return
            k = id(sp[0])
            if k not in need or need[k][1] < sp[1]:
                need[k] = sp

        for t in reads:
            add(t.w)
            for sp in t.ws.values():
                add(sp)
            if t.excl:
                for sp in t.r.values():
                    add(sp)
        for t in writes:
            add(t.w)
            for sp in t.ws.values():
                add(sp)
            for sp in t.r.values():
                add(sp)
        for t in wpart:
            add(t.w)
            for sp in t.r.values():
                add(sp)
        for k, (sem, val) in need.items():
            if sem is E.sem and E.is_pe:
                continue
            if E.known.get(k, 0) >= val:
                continue
            E.h.wait_ge(sem, val)
            E.known[k] = val

    def op(self, e, fn, reads=(), writes=(), wpart=()):
        E = self.E[e]
        self._wait(E, reads, writes, wpart)
        inst = fn(E.h)
        inst.then_inc(E.sem, 1)
        E.cnt += 1
        sp = (E.sem, E.cnt)
        for t in writes:
            t.w = sp
            t.ws = {}
            t.r = {}
        for t in wpart:
            t.ws[id(E.sem)] = sp
        k = id(E.sem)
        for t in reads:
            t.r[k] = sp
        return sp

    def dma(self, q, out, in_, owner, reads=(), writes=(), final=False, **kw):
        E = self.E[q]
        self._wait(E, reads, writes)
        if owner.dsem is None:
            owner.dsem = self.nc.alloc_semaphore("s_d%d" % self.nsem)
            self.nsem += 1
            self.dsems.append(owner)
        owner.dcnt += 16
        E.h.dma_start(out=out, in_=in_, **kw).then_inc(owner.dsem, 16)
        sp = (owner.dsem, owner.dcnt)
        for t in writes:
            t.w = sp
            t.ws = {}
            t.r = {}
        k = id(owner.dsem)
        for t in reads:
            t.r[k] = sp
        if final:
            self.final[k] = sp
        return sp

    def barrier(self):
        sps = [(E.sem, E.cnt) for E in self.E.values() if E.cnt > 0]
        sps += [(o.dsem, o.dcnt) for o in self.dsems]
        for E in self.E.values():
            for sem, val in sps:
                if sem is E.sem:
                    continue
                k = id(sem)
                if E.known.get(k, 0) >= val:
                    continue
                E.h.wait_ge(sem, val)
                E.known[k] = val

    def finish(self):
        E = self.E["sp"]
        for k, (sem, val) in self.final.items():
            E.h.wait_ge(sem, val)


class Ring:
    def __init__(self, items):
        self.items = items
        self.i = 0

    def next(self):
        it = self.items[self.i % len(self.items)]
        self.i += 1
        return it


def _bstyle(w_cols):
    M = w_cols.shape[1]
    return np.ascontiguousarray(w_cols.reshape(KT, 128, M).transpose(1, 0, 2))


def _deint():
    return np.concatenate([np.arange(0, 128, 2), np.arange(1, 128, 2)])


def _rope_tables():
    rows = T // 64
    row = np.repeat(np.arange(rows, dtype=np.float64), 64)
    col = np.tile(np.arange(64, dtype=np.float64), rows)
    inv_freq = 1.0 / (10000.0 ** (np.arange(0, 64, 2, dtype=np.float64) / 64.0))
    ang = np.concatenate([row[:, None] * inv_freq[None], col[:, None] * inv_freq[None]], axis=-1)
    cos = np.cos(ang).astype(np.float32)
    sin = np.sin(ang).astype(np.float32)
    cs = np.concatenate([cos.T, cos.T], axis=0)
    ssa = np.concatenate([sin.T, -sin.T], axis=0)
    return np.ascontiguousarray(cs), np.ascontiguousarray(ssa)


def prep_shared(inp):
    f = lambda a: np.asarray(a, dtype=np.float32)
    w_in = f(inp["w_in"])[0]
    o = {}
    o["w_ada"] = _bstyle(f(inp["w_ada"])[0])
    o["b_ada"] = f(inp["b_ada"])[0].reshape(1, 6 * D)
    o["n1g"] = np.ascontiguousarray(f(inp["norm1_g"])[0].reshape(KT, 128).T)
    o["n2g"] = np.ascontiguousarray(f(inp["norm2_g"])[0].reshape(KT, 128).T)
    oq, ok, ov = 0, 1024, 1280
    ogq, ogk, ogv, ogg, olr, omg = 1536, 2048, 2560, 3584, 4608, 4640
    perm = _deint()
    wb = np.zeros((N_WB, 128, KT, 128), np.float32)
    for h in range(8):
        wb[WB_Q + h] = _bstyle(w_in[:, oq + h * 128 + perm])
    for g in range(2):
        wb[WB_K + g] = _bstyle(w_in[:, ok + g * 128 + perm])
    for h in range(4):
        wb[WB_GQK + 2 * h] = _bstyle(w_in[:, ogq + h * 128: ogq + (h + 1) * 128])
        wb[WB_GQK + 2 * h + 1] = _bstyle(w_in[:, ogk + h * 128: ogk + (h + 1) * 128])
    for t in range(16):
        wb[WB_GA + t] = _bstyle(w_in[:, omg + t * 128: omg + (t + 1) * 128])
    o["wb"] = wb
    o["wlr"] = _bstyle(w_in[:, olr: olr + 32])
    wag = np.zeros((4, 128, KT, 640), np.float32)
    for h in range(4):
        cols = np.concatenate([np.arange(ogk + h * 128, ogk + (h + 1) * 128),
                               np.arange(ogv + h * 256, ogv + (h + 1) * 256),
                               np.arange(ogg + h * 256, ogg + (h + 1) * 256)])
        wag[h] = _bstyle(w_in[:, cols])
    o["wag"] = wag
    o["wav"] = _bstyle(w_in[:, ov: ov + 256])
    qg = np.stack([f(inp["q_norm_g"])[0][perm], f(inp["k_norm_g"])[0][perm]], axis=1)
    o["qg"] = np.ascontiguousarray(qg)
    o["qgrow"] = np.concatenate([f(inp["q_norm_g"])[0], f(inp["k_norm_g"])[0]]).reshape(1, 256)
    up = np.zeros((64, 512), np.float32)
    up[0:16] = f(inp["gk_up_f"])[0]
    up[16] = f(inp["gk_up_f_b"])[0]
    up[32:48] = f(inp["gk_up_b"])[0]
    up[48] = f(inp["gk_up_b_b"])[0]
    o["up"] = up
    o["gn"] = np.ascontiguousarray(np.broadcast_to(f(inp["gla_norm_g"])[0][None, :], (128, 256)))
    wap = f(inp["w_attn_proj"])[0]
    wgp = f(inp["w_gla_proj"])[0]
    o["wap"] = np.stack([_bstyle(wap[:, t * 128:(t + 1) * 128]) for t in range(8)])
    o["wgp"] = np.stack([_bstyle(wgp[:, t * 128:(t + 1) * 128]) for t in range(8)])
    o["wo"] = _bstyle(f(inp["w_out"])[0])
    wfi = f(inp["w_ffn_in"])[0]
    o["wfi"] = np.stack([_bstyle(wfi[:, t * 128:(t + 1) * 128]) for t in range(2 * NJ)])
    o["wfo"] = np.ascontiguousarray(f(inp["w_ffn_out"])[0].reshape(NJ, 128, D))
    cs, ssa = _rope_tables()
    o["cs"] = cs
    o["ssa"] = ssa
    k = np.arange(128)
    cst = np.zeros((128, 5, 128), np.float32)
    cst[:, 0, :] = np.eye(128, dtype=np.float32)
    cst[:, 1, :] = (k[:, None] <= k[None, :])
    cst[:, 2, :] = (k[:, None] >= k[None, :])
    cst[:, 3, :] = (k[:, None] < k[None, :])
    cst[:, 4, :] = (k[:, None] > k[None, :])
    o["cst"] = cst
    sel = np.zeros((3, 3, 128), np.float32)
    for b in range(3):
        sel[b, b, :] = 1.0
    o["sel"] = sel
    return o


def prep_core(inp, core):
    f = lambda a: np.asarray(a, dtype=np.float32)
    b0 = core * NB
    o = {}
    o["x"] = np.ascontiguousarray(f(inp["x"])[b0:b0 + NB])
    o["ctx"] = np.ascontiguousarray(f(inp["ctx"])[b0:b0 + NB])
    cv = np.concatenate([f(inp["c"])[b0:b0 + NB], f(inp["c_ctx"])[None, :]], axis=0)
    o["cT"] = np.ascontiguousarray(cv.reshape(3, KT, 128).transpose(2, 1, 0))
    return o


def build(dbg=(), stop_after=99, nb=NB):
    nc = bass.Bass("TRN2", target_bir_lowering=False)
    S = Sched(nc)
    dumps = []

    def din(name, shape, dt=F32):
        return nc.dram_tensor(name, list(shape), dt, kind="ExternalInput").ap()

    x_d = din("x", [NB, T, D])
    ctx_d = din("ctx", [NB, TC, D])
    cT_d = din("cT", [128, KT, 3])
    wada_d = din("w_ada", [128, KT, 6 * D])
    bada_d = din("b_ada", [1, 6 * D])
    n1g_d = din("n1g", [128, KT])
    n2g_d = din("n2g", [128, KT])
    wb_d = din("wb", [N_WB, 128, KT, 128])
    wlr_d = din("wlr", [128, KT, 32])
    wag_d = din("wag", [4, 128, KT, 640])
    wav_d = din("wav", [128, KT, 256])
    qg_d = din("qg", [128, 2])
    qgrow_d = din("qgrow", [1, 256])
    up_d = din("up", [64, 512])
    gn_d = din("gn", [128, 256])
    wap_d = din("wap", [8, 128, KT, 128])
    wgp_d = din("wgp", [8, 128, KT, 128])
    wo_d = din("wo", [128, KT, D])
    wfi_d = din("wfi", [2 * NJ, 128, KT, 128])
    wfo_d = din("wfo", [NJ, 128, D])
    cs_d = din("cs", [128, T])
    ssa_d = din("ssa", [128, T])
    cst_d = din("cst", [128, 5, 128])
    sel_d = din("sel", [3, 3, 128])
    out_d = nc.dram_tensor("out", [NB, T, D], F32, kind="ExternalOutput").ap()

    def sb(name, shape, dt=F32):
        return nc.alloc_sbuf_tensor("sb_" + name, list(shape), dt).ap()

    def dump(name, ap, tk, shape, dt=F32):
        if name not in dbg:
            return
        d = nc.dram_tensor("dbg_" + name, list(shape), dt, kind="ExternalOutput").ap()
        tks = tk if isinstance(tk, (list, tuple)) else [tk]
        S.dma("sp", d, ap, owner=tks[0], reads=list(tks), final=True)
        dumps.append("dbg_" + name)

    banks = []
    for i in range(8):
        banks.append((nc.alloc_psum_tensor("ps%d" % i, [128, 512], F32).ap(), Tk("ps%d" % i, excl=True)))

    cst = sb("cst", [128, 5, 128]); cst_k = Tk("cst")
    ident = cst[:, 0, :]
    Uincl, Lincl, Ustrict, Lstrict = cst[:, 1, :], cst[:, 2, :], cst[:, 3, :], cst[:, 4, :]
    ones_bf = sb("ones_bf", [128, 128], BF16); ones_k = Tk("ones")
    ones_f = sb("ones_f", [128, 128]);
    gn = sb("gn", [128, 256])
    qg = sb("qg", [128, 2])
    qgrow = sb("qgrow", [1, 256])
    up = sb("up", [64, 512])
    n1g = sb("n1g", [128, KT])
    n2g = sb("n2g", [128, KT])
    sel = sb("sel", [3, 3, 128])
    negshift = sb("negshift", [128, 1])
    mask4 = sb("mask4", [128, 2, 512])
    cstb = sb("cstb", [128, 4, 128], BF16)
    ident_bf = sb("ident_bf", [128, 128], BF16)
    up_bf = sb("up_bf", [64, 512], BF16)
    gn2 = sb("gn2", [128, 2, 256])
    small_k = Tk("small")
    modT = sb("modT", [128, 32, 3])
    modT_k = Tk("modT")
    sc1e = sb("sc1e", [128, KT, 3]); sc2e = sb("sc2e", [128, KT, 3])
    G12 = sb("G12", [128, NB, 2, D], BF16); G12_k = Tk("G12")

    hxT = sb("hxT", [128, KT, T], BF16)
    hxT_k = [Tk("hxT%d" % c) for c in range(4)]
    hcT = sb("hcT", [128, KT, TC], BF16)
    hcT_k = Tk("hcT")
    arenaA = sb("arenaA", [128, 65536 // 4])
    arenaB = sb("arenaB", [128, 64 * 1024 // 4])
    wbuf = Ring([(sb("wbuf%d" % i, [128, KT, 128], BF16), Tk("wbuf%d" % i)) for i in range(4)])
    wabuf = Ring([(sb("wabuf%d" % i, [128, KT, 640], BF16), Tk("wabuf%d" % i)) for i in range(1)])

    def load_wb(src):
        ap, tk = wbuf.next()
        S.dma("pool", ap, src, owner=tk, writes=[tk])
        return ap, tk

    def load_wa(src, ncols):
        ap, tk = wabuf.next()
        S.dma("pool", ap[:, :, 0:ncols], src, owner=tk, writes=[tk])
        return ap, tk

    S.dma("sp", cst, cst_d, owner=cst_k, writes=[cst_k])
    for dst, src in ((gn, gn_d), (qg, qg_d), (qgrow, qgrow_d), (up, up_d), (n1g, n1g_d), (n2g, n2g_d), (sel, sel_d)):
        S.dma("sp", dst, src, owner=small_k, writes=[])
    small_k.w = (small_k.dsem, small_k.dcnt)
    S.op("dve", lambda e: e.memset(ones_bf, 1.0), writes=[ones_k])
    S.op("dve", lambda e: e.memset(ones_f, 1.0), writes=[ones_k])
    for dr_ in range(2):
        for k_ in range(4):
            S.op("dve", lambda e, dr_=dr_, k_=k_: e.tensor_copy(out=mask4[:, dr_, k_ * 128:(k_ + 1) * 128], in_=cst[:, 1 + dr_, :]), reads=[cst_k], writes=[cst_k])
    for k_ in range(2):
        S.op("dve", lambda e, k_=k_: e.tensor_copy(out=gn2[:, k_, :], in_=gn), reads=[small_k], writes=[small_k])
    S.op("dve", lambda e: e.tensor_copy(out=cstb, in_=cst[:, 1:5, :]), reads=[cst_k], writes=[cst_k])
    S.op("dve", lambda e: e.tensor_copy(out=ident_bf, in_=cst[:, 0, :]), reads=[cst_k], writes=[cst_k])
    S.op("dve", lambda e: e.tensor_copy(out=up_bf, in_=up), reads=[small_k], writes=[small_k])

    def carve(base_ap, off_words, shape, dt=F32):
        n = 1
        for s in shape[1:]:
            n *= s
        words = n if dt == F32 else (n + 1) // 2
        v = base_ap[:, off_words: off_words + words]
        if dt != F32:
            v = v.bitcast(dt)
        if len(shape) == 3:
            v = v.rearrange("p (a b) -> p a b", a=shape[1])
        return v[0:shape[0]] if shape[0] != 128 else v, off_words + words

    off = 0
    cT, off = carve(arenaB, off, [128, KT, 3])
    scT, off = carve(arenaB, off, [128, KT, 3])
    bada, off = carve(arenaB, off, [1, 6 * D])
    off = 0 + 24 + 24 + 6 * D
    mod_sb, off = carve(arenaB, off, [3, 6 * D])
    wa_items = []
    offw = 0
    for i_ in range(4):
        ap_, offw = carve(arenaA, offw, [128, KT, 512])
        wa_items.append((ap_, Tk("wa%d" % i_)))
    wa_ring = Ring(wa_items)
    cT_k, scT_k, bada_k, mod_k = Tk("cT"), Tk("scT"), Tk("bada"), Tk("mod")
    S.dma("sp", cT, cT_d, owner=cT_k, writes=[cT_k])
    S.dma("sp", bada, bada_d, owner=bada_k, writes=[bada_k])
    S.op("act", lambda e: e.activation(out=scT, in_=cT, func=AF.Silu), reads=[cT_k], writes=[scT_k])
    bi_ = 0
    for n in range(12):
        wa, wa_k = wa_ring.next()
        S.dma("sp", wa, wada_d[:, :, n * 512:(n + 1) * 512], owner=wa_k, writes=[wa_k])
        ps, ps_k = banks[bi_ % 8]; bi_ += 1

        def mm(e, wa=wa, ps=ps, n=n):
            for kt in range(KT):
                e.matmul(ps[0:3, :], lhsT=scT[:, kt, :], rhs=wa[:, kt, :], start=(kt == 0), stop=False)
            return e.matmul(ps[0:3, :], lhsT=ones_f[0:1, 0:3], rhs=bada[0:1, n * 512:(n + 1) * 512], start=False, stop=True)
        S.op("pe", mm, reads=[scT_k, wa_k, bada_k, ones_k], writes=[ps_k])
        S.op("dve", lambda e, ps=ps, n=n: e.tensor_copy(out=mod_sb[:, n * 512:(n + 1) * 512], in_=ps[0:3, :]),
             reads=[ps_k], writes=[mod_k])
    ps, ps_k = banks[bi_ % 8]; bi_ += 1

    def trs(e, ps=ps):
        last = None
        for gi, grp in enumerate((0, 1, 3, 4)):
            for kt in range(KT):
                j = gi * KT + kt
                c0 = grp * D + kt * 128
                last = e.transpose(ps[:, j * 3:(j + 1) * 3], mod_sb[0:3, c0:c0 + 128], ident[0:3, 0:3])
        return last
    S.op("pe", trs, reads=[mod_k, cst_k], writes=[ps_k])
    S.op("dve", lambda e, ps=ps: e.tensor_copy(out=modT.rearrange("p a b -> p (a b)"), in_=ps[:, 0:96]),
         reads=[ps_k], writes=[modT_k])
    for j in range(3):
        S.op("dve", lambda e, j=j: e.scalar_tensor_tensor(out=sc1e[:, :, j], in0=modT[:, 8:16, j], scalar=1.0, in1=n1g,
                                                           op0=ALU.add, op1=ALU.mult), reads=[modT_k, small_k], writes=[modT_k])
        S.op("dve", lambda e, j=j: e.scalar_tensor_tensor(out=sc2e[:, :, j], in0=modT[:, 24:32, j], scalar=1.0, in1=n2g,
                                                           op0=ALU.add, op1=ALU.mult), reads=[modT_k, small_k], writes=[modT_k])
    sh1T = modT[:, 0:8, :]
    sh2T = modT[:, 16:24, :]
    for b in range(NB):
        for gi, grp in enumerate((2, 5)):
            for cc in range(2):
                ps, ps_k = banks[bi_ % 8]; bi_ += 1
                c0 = grp * D + cc * 512
                S.op("pe", lambda e, ps=ps, b=b, c0=c0: e.matmul(ps, lhsT=sel[:, b, :], rhs=mod_sb[0:3, c0:c0 + 512], start=True, stop=True),
                     reads=[mod_k, small_k], writes=[ps_k])
                S.op("act", lambda e, ps=ps, b=b, gi=gi, cc=cc: e.copy(out=G12[:, b, gi, cc * 512:(cc + 1) * 512], in_=ps),
                     reads=[ps_k], writes=[G12_k])
    mx = sb("mx", [1, 4])
    S.op("dve", lambda e: e.tensor_reduce(out=mx[:, 0:1], in_=qgrow[:, 0:128], axis=AX.X, op=ALU.max, apply_absolute_value=True),
         reads=[small_k], writes=[modT_k])
    S.op("dve", lambda e: e.tensor_reduce(out=mx[:, 1:2], in_=qgrow[:, 128:256], axis=AX.X, op=ALU.max, apply_absolute_value=True),
         reads=[small_k], writes=[modT_k])
    S.op("dve", lambda e: e.scalar_tensor_tensor(out=mx[:, 2:3], in0=mx[:, 0:1], scalar=-(128.0 ** 0.5), in1=mx[:, 1:2],
                                                 op0=ALU.mult, op1=ALU.mult), reads=[modT_k], writes=[modT_k])
    ps, ps_k = banks[bi_ % 8]; bi_ += 1
    S.op("pe", lambda e, ps=ps: e.matmul(ps[:, 0:1], lhsT=ones_f[0:1, :], rhs=mx[0:1, 2:3], start=True, stop=True),
         reads=[modT_k, ones_k], writes=[ps_k])
    S.op("dve", lambda e, ps=ps: e.tensor_copy(out=negshift, in_=ps[:, 0:1]), reads=[ps_k], writes=[modT_k])
    dump("modT", modT.rearrange("p a b -> p (a b)"), modT_k, [128, 96])
    dump("G12", G12.rearrange("p a b c -> p (a b c)"), G12_k, [128, NB * 2 * D], BF16)
    dump("negshift", negshift, modT_k, [128, 1])
    S.barrier()

    st = {"bank": bi_}
    pre = {}

    def take(key, loader):
        return pre.pop(key) if key in pre else loader()

    def bank():
        b = banks[st["bank"] % 8]
        st["bank"] += 1
        return b
    st["bankfn"] = bank

    def norm_transpose(load_tile, ntiles, dstT, dst_keys, sc_col, sh_col, xr_ring, xn_ring, junk, junk_k, stat_ring, act_kts=(0, 2, 4, 6)):
        ngrp = (ntiles + 3) // 4

        def stage_n(g):
            tiles = list(range(g * 4, min(ntiles, g * 4 + 4)))
            nt = len(tiles)
            xrs = []
            for i in tiles:
                xr, xr_k = xr_ring.next()
                load_tile(i, xr, xr_k)
                xrs.append((xr, xr_k))
            stt, stt_k = stat_ring.next()
            for j, (xr, xr_k) in enumerate(xrs):
                S.op("act", lambda e, xr=xr, j=j: e.activation(out=junk, in_=xr, func=AF.Square, accum_out=stt[:, j:j + 1]),
                     reads=[xr_k], writes=[junk_k], wpart=[stt_k])
            S.op("act", lambda e: e.activation(out=stt[:, 4:4 + nt], in_=stt[:, 0:nt], func=AF.Ln, scale=1.0 / D, bias=EPS),
                 reads=[stt_k], writes=[stt_k])
            S.op("act", lambda e: e.activation(out=stt[:, 8:8 + nt], in_=stt[:, 4:4 + nt], func=AF.Exp, scale=-0.5),
                 reads=[stt_k], writes=[stt_k])
            xns = []
            for j, (xr, xr_k) in enumerate(xrs):
                xn, xn_k = xn_ring.next()
                S.op("dve", lambda e, xn=xn, xr=xr, j=j: e.tensor_scalar(out=xn, in0=xr, scalar1=stt[:, 8 + j:9 + j], scalar2=None, op0=ALU.mult),
                     reads=[xr_k, stt_k], writes=[xn_k])
                xns.append((xn, xn_k))
            return xns

        def stage_t(g, xns):
            nt = len(xns)
            for kt in range(KT):
                ps, ps_k = st['bankfn']()
                psb = ps.bitcast(BF16)

                def trs(e, psb=psb, kt=kt):
                    last = None
                    for j, (xn, _) in enumerate(xns):
                        last = e.transpose(psb[:, j * 128:(j + 1) * 128], xn[:, kt * 128:(kt + 1) * 128], ident_bf)
                    return last
                S.op("pe", trs, reads=[k for _, k in xns] + [cst_k], writes=[ps_k])
                dst = dstT[:, kt, g * 512: g * 512 + nt * 128]
                if kt in act_kts:
                    S.op("act", lambda e, psb=psb, dst=dst, kt=kt: e.activation(out=dst, in_=psb[:, 0:nt * 128], func=AF.Identity,
                                                                               scale=sc_col(kt), bias=sh_col(kt)),
                         reads=[ps_k, modT_k], wpart=[dst_keys[g]])
                else:
                    S.op("dve", lambda e, psb=psb, dst=dst, kt=kt: e.tensor_scalar(out=dst, in0=psb[:, 0:nt * 128], scalar1=sc_col(kt),
                                                                                  scalar2=sh_col(kt), op0=ALU.mult, op1=ALU.add),
                         reads=[ps_k, modT_k], wpart=[dst_keys[g]])
        cur = stage_n(0)
        for g in range(ngrp):
            nxt = stage_n(g + 1) if g + 1 < ngrp else None
            stage_t(g, cur)
            cur = nxt

    for b in range(nb):
        off = 0
        xr_items = []
        for i in range(8):
            ap, off = carve(arenaB, off, [128, D])
            xr_items.append((ap, Tk("xr%d" % i)))
        xn_items = []
        for i in range(8):
            ap, off = carve(arenaB, off, [128, D], BF16)
            xn_items.append((ap, Tk("xn%d" % i)))
        junk, off = carve(arenaB, off, [128, D], BF16)
        junk_k = Tk("junk")
        stat_items = []
        for i in range(3):
            ap, off = carve(arenaB, off, [128, 12])
            stat_items.append((ap, Tk("stat%d" % i)))
        xr_ring, xn_ring, stat_ring = Ring(xr_items), Ring(xn_items), Ring(stat_items)

        norm_transpose(lambda i, ap, tk: S.dma("sp", ap, ctx_d[b, i * 128:(i + 1) * 128, :], owner=tk, writes=[tk]),
                       2, hcT, [hcT_k], lambda kt: sc1e[:, kt, 2:3], lambda kt: sh1T[:, kt, 2:3],
                       xr_ring, xn_ring, junk, junk_k, stat_ring, act_kts=(0, 4))
        norm_transpose(lambda i, ap, tk: S.dma("sp", ap, x_d[b, i * 128:(i + 1) * 128, :], owner=tk, writes=[tk]),
                       16, hxT, hxT_k, lambda kt: sc1e[:, kt, b:b + 1], lambda kt: sh1T[:, kt, b:b + 1],
                       xr_ring, xn_ring, junk, junk_k, stat_ring, act_kts=(0, 4))
        if b == 0:
            dump("hcT", hcT.rearrange("p a b -> p (a b)"), hcT_k, [128, KT * TC], BF16)
            for c in range(4):
                dump("hxT%d" % c, hxT[:, :, c * 512:(c + 1) * 512], hxT_k[c], [128, KT, 512], BF16)
        if stop_after <= 1:
            break
        pre["gq0"] = load_wb(wb_d[WB_GQK + 0])
        pre["gk0"] = load_wb(wb_d[WB_GQK + 1])
        pre["ga0"] = load_wa(wag_d[0], 640)
        S.barrier()

        goT = arenaA[:, 0:8192].bitcast(BF16).rearrange("p (a b) -> p a b", a=KT)
        aoT = arenaA[:, 8192:16384].bitcast(BF16).rearrange("p (a b) -> p a b", a=KT)
        goT_k = [Tk("goT%d" % h) for h in range(4)]
        aoT_k = [[Tk("aoT%d_%d" % (h, c)) for c in range(4)] for h in range(8)]
        offA = 8192
        qeT, offA = carve(arenaA, offA, [128, 2, T], BF16); qeT_k = [Tk("qeT0"), Tk("qeT1")]
        ATs, offA = carve(arenaA, offA, [128, 2, T], BF16); ATs_k = [Tk("AT0"), Tk("AT1")]
        kds, offA = carve(arenaA, offA, [128, 2, 18 * 128], BF16); kds_k = [Tk("kd0"), Tk("kd1")]
        keT_items = []
        for i_ in range(2):
            ap, offA = carve(arenaA, offA, [128, 512], BF16)
            keT_items.append((ap, Tk("keT%d" % i_)))
        keT_r = Ring(keT_items)
        hl_items = []
        for i_ in range(2):
            ap, offA = carve(arenaA, offA, [128, 2, 512], BF16)
            hl_items.append((ap, Tk("hl%d" % i_)))
        hl_r = Ring(hl_items)
        assert offA <= 16384, offA
        off = 0
        LR, off = carve(arenaB, off, [64, T + TC], BF16); LR_k = Tk("LR")
        gk_tm, off = carve(arenaB, off, [128, 18 * 128], BF16); gk_tm_k = Tk("gk_tm")
        gv, off = carve(arenaB, off, [128, 18, 256], BF16); gv_k = Tk("gv")
        sgg, off = carve(arenaB, off, [128, 16, 256], BF16); sgg_k = Tk("sgg")
        SBst, off = carve(arenaB, off, [128, 16, 256], BF16); SBst_k = Tk("SBst")
        S32p, Sb32p = [], []
        for i_ in range(2):
            ap, off = carve(arenaB, off, [128, 256]); S32p.append((ap, Tk("S32_%d" % i_)))
            ap, off = carve(arenaB, off, [128, 256]); Sb32p.append((ap, Tk("Sb32_%d" % i_)))
        dlx, off = carve(arenaB, off, [128, 2, 20]); dlx_k = [Tk("dlx0"), Tk("dlx1")]
        wlr_sb, off = carve(arenaB, off, [128, KT, 32], BF16); wlr_k = Tk("wlr")
        junk2, off = carve(arenaB, off, [128, 256], BF16); junk2_k = Tk("junk2")

        def ring(n, shape, dt=F32, nm="r"):
            nonlocal off
            items = []
            for i in range(n):
                ap, off = carve(arenaB, off, shape, dt)
                items.append((ap, Tk("%s%d" % (nm, i))))
            return Ring(items)
        e1_r = ring(2, [128, 512], F32, "e1")
        Lw_r = ring(2, [128, 512], BF16, "Lw")
        E_r = ring(2, [128, 512], F32, "E")
        eb_r = ring(2, [128, 512], F32, "eb")
        enb_r = ring(2, [128, 512], F32, "enb")
        Sbf_r = ring(2, [128, 256], BF16, "Sbf")
        go_r = ring(2, [128, 256], F32, "go")
        st_r = ring(3, [128, 4], F32, "st")
        tmp_r = e1_r
        assert off <= 16384, off

        def hsrc(i, kt):
            if i < 2:
                return hcT[:, kt, i * 128:(i + 1) * 128], hcT_k
            return hxT[:, kt, (i - 2) * 128:(i - 1) * 128], hxT_k[(i - 2) // 4]

        S.dma("pool", wlr_sb, wlr_d, owner=wlr_k, writes=[wlr_k])
        S.op("dve", lambda e: e.memset(LR, 1.0), writes=[LR_k])
        for c in range(5):
            n0 = 0 if c == 0 else TC + (c - 1) * 512
            nn = TC if c == 0 else 512
            for dr in range(2):
                ps, ps_k = bank()

                def mm(e, ps=ps, c=c, dr=dr, nn=nn):
                    last = None
                    for kt in range(KT):
                        rhs = hcT[:, kt, :] if c == 0 else hxT[:, kt, (c - 1) * 512:c * 512]
                        last = e.matmul(ps[0:16, 0:nn], lhsT=wlr_sb[:, kt, dr * 16:(dr + 1) * 16], rhs=rhs, start=(kt == 0), stop=(kt == KT - 1))
                    return last
                S.op("pe", mm, reads=[wlr_k, hcT_k if c == 0 else hxT_k[c - 1]], writes=[ps_k])
                S.op("dve", lambda e, ps=ps, dr=dr, n0=n0, nn=nn: e.tensor_copy(out=LR[dr * 32:dr * 32 + 16, n0:n0 + nn], in_=ps[0:16, 0:nn]),
                     reads=[ps_k], wpart=[LR_k])
        if stop_after <= 1.1:
            dump("LR", LR, LR_k, [64, T + TC], BF16)
            break

        def decay_pair(h, tiles, balloc, mid=None):
            nt = len(tiles)
            W = nt * 128
            i0 = tiles[0]
            pzs = []
            for dr in range(2):
                pz, pz_k = balloc()

                def mmz(e, pz=pz, dr=dr):
                    last = None
                    for k, i in enumerate(tiles):
                        last = e.matmul(pz[:, k * 128:(k + 1) * 128], lhsT=LR[dr * 32:dr * 32 + 17, i * 128:(i + 1) * 128],
                                        rhs=up_bf[dr * 32:dr * 32 + 17, h * 128:(h + 1) * 128], start=True, stop=True)
                    return last
                S.op("pe", mmz, reads=[LR_k, small_k], writes=[pz_k])
                pzs.append((pz, pz_k))
            hls = []
            for dr in range(2):
                pz, pz_k = pzs[dr]
                e1, e1_k = e1_r.next()
                S.op("act", lambda e, e1=e1, pz=pz: e.activation(out=e1[:, 0:W], in_=pz[:, 0:W], func=AF.Exp, scale=-1.0), reads=[pz_k], writes=[e1_k])
                Lw, Lw_k = Lw_r.next()
                S.op("act", lambda e, e1=e1, Lw=Lw: e.activation(out=Lw[:, 0:W], in_=e1[:, 0:W], func=AF.Ln, bias=1.0), reads=[e1_k], writes=[Lw_k])
                hls.append((Lw, Lw_k))
            if mid is not None:
                mid()
            pds = []
            for dr in range(2):
                hl, hl_k = hls[dr]
                pd, pd_k = balloc()

                def mmd(e, pd=pd, hl=hl, dr=dr):
                    tri = cstb[:, 3, :] if dr == 0 else cstb[:, 2, :]
                    return e.matmul(pd[:, 0:W], lhsT=tri, rhs=hl[:, 0:W], start=True, stop=True)
                S.op("pe", mmd, reads=[hl_k, cst_k], writes=[pd_k])
                pds.append((pd, pd_k))
            pbs = []
            for dr in range(2):
                hl, hl_k = hls[dr]
                pb, pb_k = balloc()

                def mmb(e, pb=pb, hl=hl, dr=dr):
                    tri = cstb[:, 0, :] if dr == 0 else cstb[:, 1, :]
                    last = None
                    for k in range(nt):
                        last = e.matmul(pb[:, k * 128:(k + 1) * 128], lhsT=hl[:, k * 128:(k + 1) * 128], rhs=tri, start=True, stop=True)
                    return last
                S.op("pe", mmb, reads=[hl_k, cst_k], writes=[pb_k])
                pbs.append((pb, pb_k))
            for dr in range(2):
                pd, pd_k = pds[dr]
                E, E_k = E_r.next()
                S.op("act", lambda e, E=E, pd=pd: e.activation(out=E[:, 0:W], in_=pd[:, 0:W], func=AF.Exp, scale=-1.0 / 16), reads=[pd_k], writes=[E_k])
                S.op("dve", lambda e, E=E, dr=dr: e.tensor_tensor(out=kds[:, dr, i0 * 128:i0 * 128 + W], in0=gk_tm[:, i0 * 128:i0 * 128 + W], in1=E[:, 0:W], op=ALU.mult),
                     reads=[gk_tm_k, E_k], wpart=[kds_k[dr]])
            res = []
            for dr in range(2):
                pb, pb_k = pbs[dr]
                eb, eb_k = eb_r.next()
                enb, enb_k = enb_r.next()
                S.op("act", lambda e, eb=eb, pb=pb: e.activation(out=eb[:, 0:W], in_=pb[:, 0:W], func=AF.Exp, scale=-1.0 / 16), reads=[pb_k], writes=[eb_k])
                if nt == 4:
                    S.op("act", lambda e, enb=enb, pb=pb: e.activation(out=enb[:, 0:W], in_=pb[:, 0:W], func=AF.Exp, scale=1.0 / 16), reads=[pb_k], writes=[enb_k])
                col = 127 if dr == 0 else 0
                S.op("dve", lambda e, eb=eb, dr=dr, col=col: e.tensor_copy(out=dlx[:, dr, i0:i0 + nt], in_=eb[:, 0:W].rearrange("p (k t) -> p k t", k=nt)[:, :, col]),
                     reads=[eb_k], wpart=[dlx_k[dr]])
                res.append((eb, eb_k, enb, enb_k))
            return res

        for h in range(4):
            wq, wq_k = take("gq%d" % h, lambda: load_wb(wb_d[WB_GQK + 2 * h]))
            wk, wk_k = take("gk%d" % h, lambda: load_wb(wb_d[WB_GQK + 2 * h + 1]))
            wa, wa_k = take("ga%d" % h, lambda: load_wa(wag_d[h], 640))
            for i0 in range(0, 18, 4):
                tl = list(range(i0, min(18, i0 + 4)))
                ps, ps_k = bank()

                def mmk(e, ps=ps, tl=tl):
                    last = None
                    for k, i in enumerate(tl):
                        for kt in range(KT):
                            last = e.matmul(ps[:, k * 128:(k + 1) * 128], lhsT=hsrc(i, kt)[0], rhs=wa[:, kt, 0:128], start=(kt == 0), stop=(kt == KT - 1))
                    return last
                S.op("pe", mmk, reads=[wa_k] + list({id(hsrc(i, 0)[1]): hsrc(i, 0)[1] for i in tl}.values()), writes=[ps_k])
                S.op("act", lambda e, ps=ps, tl=tl: e.copy(out=gk_tm[:, tl[0] * 128:(tl[-1] + 1) * 128], in_=ps[:, 0:len(tl) * 128]), reads=[ps_k], wpart=[gk_tm_k])
            bProj = Ring(banks[0:4]); bDP = Ring(banks[4:8])

            def tm_vg(i0, do_v=True, do_g=True):
                if do_v:
                    tm_v(i0)
                if do_g and i0 >= 2:
                    tm_g(i0)

            def tm_v(i0):
                ps, ps_k = bDP.next()

                def mmv(e, ps=ps):
                    last = None
                    for k in range(2):
                        for kt in range(KT):
                            last = e.matmul(ps[:, k * 256:(k + 1) * 256], lhsT=hsrc(i0 + k, kt)[0], rhs=wa[:, kt, 128:384], start=(kt == 0), stop=(kt == KT - 1))
                    return last
                S.op("pe", mmv, reads=[wa_k, hsrc(i0, 0)[1]], writes=[ps_k])
                S.op("dve", lambda e, ps=ps: e.tensor_copy(out=gv[:, i0:i0 + 2, :], in_=ps.rearrange("p (a b) -> p a b", a=2)), reads=[ps_k], wpart=[gv_k])

            def tm_g(i0):
                if True:
                    ps2, ps2_k = bDP.next()

                    def mmg(e, ps2=ps2):
                        last = None
                        for k in range(2):
                            for kt in range(KT):
                                last = e.matmul(ps2[:, k * 256:(k + 1) * 256], lhsT=hsrc(i0 + k, kt)[0], rhs=wa[:, kt, 384:640], start=(kt == 0), stop=(kt == KT - 1))
                        return last
                    S.op("pe", mmg, reads=[wa_k, hsrc(i0, 0)[1]], writes=[ps2_k])
                    hl, hl_k = hl_r.next()
                    tmp = hl.rearrange("p a b -> p (a b)").bitcast(F32)
                    S.op("act", lambda e: e.activation(out=tmp, in_=ps2, func=AF.Silu), reads=[ps2_k], writes=[hl_k])
                    S.op("dve", lambda e: e.tensor_tensor(out=sgg[:, i0 - 2:i0, :], in0=tmp.rearrange("p (a b) -> p a b", a=2),
                                                          in1=gn2, op=ALU.mult), reads=[hl_k, small_k], wpart=[sgg_k])
            tm_vg(0)
            if stop_after <= 1.2:
                dump("gk_tm", gk_tm, [gk_tm_k, gv_k, sgg_k], [128, 18 * 128], BF16)
                break
            decay_pair(h, [0, 1], bDP.next)

            def proj_qk(c):
                pq, pq_k = bProj.next()
                pk, pk_k = bProj.next()
                for (w, w_k, pp, pp_k) in ((wq, wq_k, pq, pq_k), (wk, wk_k, pk, pk_k)):
                    def mm(e, pp=pp, w=w):
                        last = None
                        for kt in range(KT):
                            last = e.matmul(pp, lhsT=w[:, kt, :], rhs=hxT[:, kt, c * 512:(c + 1) * 512], start=(kt == 0), stop=(kt == KT - 1))
                        return last
                    S.op("pe", mm, reads=[w_k, hxT_k[c]], writes=[pp_k])
                return pq, pq_k, pk, pk_k

            def qk_scale(c, P, ebs):
                pq, pq_k, pk, pk_k = P
                kes = []
                for dr in range(2):
                    eb, eb_k, enb, enb_k = ebs[dr]
                    S.op("dve", lambda e, dr=dr, eb=eb: e.scalar_tensor_tensor(out=qeT[:, dr, c * 512:(c + 1) * 512], in0=pq, scalar=GLA_QSCALE, in1=eb, op0=ALU.mult, op1=ALU.mult),
                         reads=[pq_k, eb_k], wpart=[qeT_k[dr]])
                    keT, keT_k = keT_r.next()
                    S.op("dve", lambda e, keT=keT, enb=enb: e.tensor_tensor(out=keT, in0=pk, in1=enb, op=ALU.mult), reads=[pk_k, enb_k], writes=[keT_k])
                    kes.append((keT, keT_k))
                return kes

            def amat(c, kes):
                for dr in range(2):
                    keT, keT_k = kes[dr]
                    pa, pa_k = bDP.next()

                    def mma(e, pa=pa, keT=keT, dr=dr):
                        last = None
                        for k in range(4):
                            t0 = c * 512 + k * 128
                            last = e.matmul(pa[:, k * 128:(k + 1) * 128], lhsT=keT[:, k * 128:(k + 1) * 128], rhs=qeT[:, dr, t0:t0 + 128], start=True, stop=True)
                        return last
                    S.op("pe", mma, reads=[keT_k, qeT_k[dr]], writes=[pa_k])
                    S.op("dve", lambda e, pa=pa, dr=dr: e.tensor_tensor(out=ATs[:, dr, c * 512:(c + 1) * 512], in0=pa, in1=mask4[:, dr, :], op=ALU.mult),
                         reads=[pa_k, cst_k], wpart=[ATs_k[dr]])
            def tm_group(c):
                def f():
                    tm_vg(2 + 4 * c, do_g=False)
                    tm_vg(4 + 4 * c, do_g=False)
                    if c == 0:
                        for i0 in range(2, 18, 2):
                            tm_vg(i0, do_v=False)
                return f
            P = proj_qk(0)
            ebs = decay_pair(h, [2, 3, 4, 5], bDP.next, mid=tm_group(0))
            for c in range(4):
                kes = qk_scale(c, P, ebs)
                if c + 1 < 4:
                    P = proj_qk(c + 1)
                    ebs = decay_pair(h, [2 + 4 * (c + 1) + k for k in range(4)], bDP.next, mid=tm_group(c + 1))
                amat(c, kes)
            bU = Ring(banks[0:2]); bOo = Ring(banks[2:5]); bT = Ring(banks[5:7])

            cur = {0: 0, 1: 0}

            def stbuf(dr):
                return (S32p if dr == 0 else Sb32p)[cur[dr]]

            def state_step(i, dr, first):
                pu, pu_k = bU.next()
                S.op("pe", lambda e: e.matmul(pu[:, 0:256], lhsT=kds[:, dr, i * 128:(i + 1) * 128], rhs=gv[:, i, :], start=True, stop=True),
                     reads=[kds_k[dr], gv_k], writes=[pu_k])
                old, old_k = stbuf(dr)
                cur[dr] ^= 1
                new, new_k = stbuf(dr)
                if first:
                    S.op("dve", lambda e: e.tensor_copy(out=new, in_=pu[:, 0:256]), reads=[pu_k], writes=[new_k])
                else:
                    S.op("dve", lambda e: e.scalar_tensor_tensor(out=new, in0=old, scalar=dlx[:, dr, i:i + 1], in1=pu[:, 0:256], op0=ALU.mult, op1=ALU.add),
                         reads=[pu_k, dlx_k[dr], old_k], writes=[new_k])
            for step, i in enumerate((0, 1)):
                state_step(i, 0, first=(step == 0))
            for step, i in enumerate((1, 0)):
                state_step(i, 1, first=(step == 0))
            if b == 0 and h == 0:
                dump("sf", stbuf(0)[0], stbuf(0)[1], [128, 256])
                dump("sb", stbuf(1)[0], stbuf(1)[1], [128, 256])
            for n in range(15, -1, -1):
                sbuf_, sbuf_k = stbuf(1)
                S.op("act", lambda e, n=n, sbuf_=sbuf_: e.copy(out=SBst[:, n, :], in_=sbuf_), reads=[sbuf_k], wpart=[SBst_k])
                if n > 0:
                    state_step(n + 2, 1, first=False)
            live = {}

            def core(n):
                i = n + 2
                Sbf, Sbf_k = Sbf_r.next()
                s32_, s32_k = stbuf(0)
                S.op("act", lambda e: e.copy(out=Sbf, in_=s32_), reads=[s32_k], writes=[Sbf_k])
                pso, pso_k = bOo.next()
                tok = slice(n * 128, (n + 1) * 128)

                def mmo(e):
                    e.matmul(pso[:, 0:256], lhsT=qeT[:, 0, tok], rhs=Sbf, start=True, stop=False)
                    e.matmul(pso[:, 0:256], lhsT=ATs[:, 0, tok], rhs=gv[:, i, :], start=False, stop=False)
                    e.matmul(pso[:, 0:256], lhsT=qeT[:, 1, tok], rhs=SBst[:, n, :], start=False, stop=False)
                    return e.matmul(pso[:, 0:256], lhsT=ATs[:, 1, tok], rhs=gv[:, i, :], start=False, stop=True)
                if n < 15:
                    state_step(i, 0, first=False)
                S.op("pe", mmo, reads=[qeT_k[0], qeT_k[1], ATs_k[0], ATs_k[1], Sbf_k, SBst_k, gv_k], writes=[pso_k])
                live[n] = [pso, pso_k]

            def norm(n):
                pso, pso_k = live[n]
                stt, stt_k = st_r.next()
                S.op("act", lambda e: e.activation(out=junk2, in_=pso[:, 0:256], func=AF.Square, accum_out=stt[:, 1:2]), reads=[pso_k], writes=[junk2_k, stt_k])
                S.op("act", lambda e: e.activation(out=stt[:, 2:3], in_=stt[:, 1:2], func=AF.Ln, scale=1.0 / 256, bias=EPS), reads=[stt_k], writes=[stt_k])
                S.op("act", lambda e: e.activation(out=stt[:, 3:4], in_=stt[:, 2:3], func=AF.Exp, scale=-0.5), reads=[stt_k], writes=[stt_k])
                go, go_k = go_r.next()
                S.op("dve", lambda e: e.scalar_tensor_tensor(out=go, in0=pso[:, 0:256], scalar=stt[:, 3:4], in1=sgg[:, n, :], op0=ALU.mult, op1=ALU.mult),
                     reads=[pso_k, stt_k, sgg_k], writes=[go_k])
                pst, pst_k = bT.next()

                def trs(e):
                    e.transpose(pst[:, 0:128], go[:, 0:128], ident)
                    return e.transpose(pst[:, 128:256], go[:, 128:256], ident)
                S.op("pe", trs, reads=[go_k, cst_k], writes=[pst_k])
                live[n] = [pst, pst_k]

            def evac(n):
                pst, pst_k = live.pop(n)
                tok = slice(n * 128, (n + 1) * 128)
                S.op("dve", lambda e: e.tensor_copy(out=goT[:, 2 * h:2 * h + 2, tok], in_=pst[:, 0:256].rearrange("p (a b) -> p a b", a=2)),
                     reads=[pst_k], wpart=[goT_k[h]])
            for n in range(18):
                if n < 16:
                    core(n)
                if 0 <= n - 1 < 16:
                    norm(n - 1)
                if 0 <= n - 2 < 16:
                    evac(n - 2)
        if b == 0 and stop_after >= 2:
            dump("goT", goT.rearrange("p a b -> p (a b)"), goT_k, [128, KT * T], BF16)
        if stop_after <= 2:
            break
        pre["k0"] = load_wb(wb_d[WB_K + 0])
        pre["k1"] = load_wb(wb_d[WB_K + 1])
        pre["wv"] = load_wa(wav_d, 256)
        S.barrier()

        off = 0
        kT, off = carve(arenaB, off, [128, 2, T + TC], BF16); kT_k = [Tk("kT0"), Tk("kT1")]
        Vt, off = carve(arenaB, off, [128, 18, 256], BF16); V_k = Tk("V")
        CS, off = carve(arenaB, off, [128, T]); SSa, off = carve(arenaB, off, [128, T]); cs_k = Tk("cs")
        sq_r = ring(2, [128, 512], BF16, "sq")
        ln_r = ring(2, [128, 512], F32, "ln")
        rr_r = ring(2, [128, 512], F32, "rr")
        t1_r = ring(2, [128, 512], F32, "t1")
        t2_r = ring(2, [128, 512], F32, "t2")
        qT_r = ring(2, [128, 512], BF16, "qT")
        PT_r = ring(4, [128, 512], BF16, "PT")
        rz_r = ring(2, [128, 512], F32, "rz")
        assert off <= 16384, off
        S.dma("sp", CS, cs_d, owner=cs_k, writes=[cs_k])
        S.dma("sp", SSa, ssa_d, owner=cs_k, writes=[cs_k])
        bS = Ring(banks[0:3]); bO = Ring(banks[3:5]); bZ = Ring(banks[5:7]); bM = Ring(banks[7:8])

        def proj_b(w, w_k, c5, bring=None):
            ps, ps_k = (bring or bM).next()
            nn = TC if c5 == 0 else 512

            def mm(e):
                last = None
                for kt in range(KT):
                    rhs = hcT[:, kt, :] if c5 == 0 else hxT[:, kt, (c5 - 1) * 512:c5 * 512]
                    last = e.matmul(ps[:, 0:nn], lhsT=w[:, kt, :], rhs=rhs, start=(kt == 0), stop=(kt == KT - 1))
                return last
            S.op("pe", mm, reads=[w_k, hcT_k if c5 == 0 else hxT_k[c5 - 1]], writes=[ps_k])
            return ps, ps_k, nn

        def norm_rope_steps(ps, ps_k, nn, gi, tok0, dst, dst_k):
            g = qg[:, gi:gi + 1]
            sq, sq_k = sq_r.next()
            t1, t1_k = t1_r.next()
            lnv, lnv_k = ln_r.next()
            rr, rr_k = rr_r.next()
            steps = []
            steps.append(lambda: S.op("act", lambda e: e.activation(out=sq[:, 0:nn], in_=ps[:, 0:nn], func=AF.Square), reads=[ps_k], writes=[sq_k]))
            if tok0 is None:
                steps.append(lambda: S.op("dve", lambda e: e.tensor_scalar(out=t1[:, 0:nn], in0=ps[:, 0:nn], scalar1=g, scalar2=None, op0=ALU.mult),
                                          reads=[ps_k, small_k], writes=[t1_k]))
            else:
                tok = slice(tok0, tok0 + nn)
                t2, t2_k = t2_r.next()
                steps.append(lambda: S.op("dve", lambda e: e.scalar_tensor_tensor(out=t1[:, 0:nn], in0=ps[:, 0:nn], scalar=g, in1=CS[:, tok], op0=ALU.mult, op1=ALU.mult),
                                          reads=[ps_k, small_k, cs_k], writes=[t1_k]))
                steps.append(lambda: S.op("dve", lambda e: e.scalar_tensor_tensor(out=t2[0:64, 0:nn], in0=ps[64:128, 0:nn], scalar=qg[64:128, gi:gi + 1], in1=SSa[64:128, tok],
                                                                                  op0=ALU.mult, op1=ALU.mult), reads=[ps_k, small_k, cs_k], writes=[t2_k]))
                steps.append(lambda: S.op("dve", lambda e: e.scalar_tensor_tensor(out=t2[64:128, 0:nn], in0=ps[0:64, 0:nn], scalar=qg[0:64, gi:gi + 1], in1=SSa[0:64, tok],
                                                                                  op0=ALU.mult, op1=ALU.mult), reads=[ps_k, small_k, cs_k], writes=[t2_k]))
                steps.append(lambda: S.op("dve", lambda e: e.tensor_tensor(out=t1[:, 0:nn], in0=t1[:, 0:nn], in1=t2[:, 0:nn], op=ALU.add), reads=[t1_k, t2_k], writes=[t1_k]))
            steps.append(lambda: S.op("pe", lambda e: e.matmul(ps[:, 0:nn], lhsT=ones_bf, rhs=sq[:, 0:nn], start=True, stop=True), reads=[sq_k, ones_k], writes=[ps_k]))
            steps.append(lambda: S.op("act", lambda e: e.activation(out=lnv[:, 0:nn], in_=ps[:, 0:nn], func=AF.Ln, scale=1.0 / 128, bias=EPS), reads=[ps_k], writes=[lnv_k]))
            steps.append(lambda: S.op("act", lambda e: e.activation(out=rr[:, 0:nn], in_=lnv[:, 0:nn], func=AF.Exp, scale=-0.5), reads=[lnv_k], writes=[rr_k]))
            steps.append(lambda: S.op("dve", lambda e: e.tensor_tensor(out=dst, in0=t1[:, 0:nn], in1=rr[:, 0:nn], op=ALU.mult), reads=[t1_k, rr_k], writes=[dst_k]))
            return steps

        def norm_rope(ps, ps_k, nn, gi, tok0, dst, dst_k):
            for st_ in norm_rope_steps(ps, ps_k, nn, gi, tok0, dst, dst_k):
                st_()

        bMp = Ring(banks[3:8])
        klists = []
        for g in range(2):
            w, w_k = take("k%d" % g, lambda: load_wb(wb_d[WB_K + g]))
            for c5 in range(5):
                def mk(g=g, c5=c5, w=w, w_k=w_k):
                    box = {}

                    def first():
                        ps, ps_k, nn = proj_b(w, w_k, c5, bMp)
                        n0 = 0 if c5 == 0 else TC + (c5 - 1) * 512
                        box["steps"] = norm_rope_steps(ps, ps_k, nn, 1, None if c5 == 0 else (c5 - 1) * 512, kT[:, g, n0:n0 + nn], kT_k[g])
                    nsteps = 6 if c5 == 0 else 9
                    return [first] + [(lambda i=i: box["steps"][i]()) for i in range(nsteps)]
                klists.append(mk())
        wv, wv_k = take("wv", lambda: load_wa(wav_d, 256))
        v_todo = list(range(18))

        def v_tile(i):
            ps, ps_k = bS.next()

            def mm(e, ps=ps, i=i):
                last = None
                for kt in range(KT):
                    last = e.matmul(ps[:, 0:256], lhsT=hsrc(i, kt)[0], rhs=wv[:, kt, 0:256], start=(kt == 0), stop=(kt == KT - 1))
                return last
            S.op("pe", mm, reads=[wv_k, hsrc(i, 0)[1]], writes=[ps_k])
            S.op("act" if i % 2 else "dve", (lambda e, ps=ps, i=i: e.copy(out=Vt[:, i, :], in_=ps[:, 0:256])) if i % 2 else
                 (lambda e, ps=ps, i=i: e.tensor_copy(out=Vt[:, i, :], in_=ps[:, 0:256])), reads=[ps_k], wpart=[V_k])
        for p in range(0, len(klists), 2):
            la, lb = klists[p], klists[p + 1]
            for i in range(max(len(la), len(lb))):
                if i < len(la):
                    la[i]()
                if i < len(lb):
                    lb[i]()
                if v_todo and i % 2 == 1:
                    v_tile(v_todo.pop(0))
        while v_todo:
            v_tile(v_todo.pop(0))
        if b == 0:
            dump("kT", kT.rearrange("p a b -> p (a b)"), kT_k, [128, 2 * (T + TC)], BF16)
            dump("V", Vt.rearrange("p a b -> p (a b)"), V_k, [128, 18 * 256], BF16)

        def q_prep_steps(h, c):
            qT, qT_k = qT_r.next()
            box = {}

            def first():
                w, w_k = get_wq(h)
                ps, ps_k, nn = proj_b(w, w_k, c + 1)
                box["steps"] = norm_rope_steps(ps, ps_k, 512, 0, c * 512, qT, qT_k)
            steps = [first] + [(lambda i=i: box["steps"][i]()) for i in range(9)]
            return steps, (qT, qT_k)

        def attend(h, c, qT, qT_k, side=(), pre_qk=None, nxt=None):
            g = h // 4
            side = list(side)
            pO, pO_k = bO.next()
            pZ, pZ_k = bZ.next()
            LOOK = 2
            pSs = dict(pre_qk or {})
            nxt_store = {}

            def qk_for(gg, qTx, qTx_k, j, store):
                pS, pS_k = bS.next()
                S.op("pe", lambda e: e.matmul(pS, lhsT=kT[:, gg, j * 128:(j + 1) * 128], rhs=qTx, start=True, stop=True),
                     reads=[kT_k[gg], qTx_k], writes=[pS_k])
                store[j] = (pS, pS_k)
            for j in range(LOOK):
                if j not in pSs:
                    qk_for(g, qT, qT_k, j, pSs)
            for j in range(18):
                if j + LOOK < 18:
                    qk_for(g, qT, qT_k, j + LOOK, pSs)
                elif nxt is not None:
                    assert not side
                    qk_for(nxt[0] // 4, nxt[1], nxt[2], j + LOOK - 18, nxt_store)
                if j in (1, 2, 3, 4, 6, 8, 10, 12, 13, 14) and side:
                    side.pop(0)()
                pS, pS_k = pSs.pop(j)
                PT, PT_k = PT_r.next()
                S.op("act", lambda e, pS=pS, PT=PT: e.activation(out=PT, in_=pS, func=AF.Exp, scale=ATT_SCALE, bias=-ATT_SHIFT),
                     reads=[pS_k], writes=[PT_k])

                def pv(e, PT=PT, j=j):
                    e.matmul(pO, lhsT=Vt[:, j, g * 128:(g + 1) * 128], rhs=PT, start=(j == 0), stop=(j == 17))
                    return e.matmul(pZ, lhsT=ones_bf, rhs=PT, start=(j == 0), stop=(j == 17))
                S.op("pe", pv, reads=[V_k, PT_k, ones_k], writes=[pO_k, pZ_k])
            assert not side
            rz, rz_k = rz_r.next()
            S.op("dve", lambda e: e.reciprocal(out=rz, in_=pZ), reads=[pZ_k], writes=[rz_k])
            S.op("dve", lambda e: e.tensor_tensor(out=aoT[:, h, c * 512:(c + 1) * 512], in0=pO, in1=rz, op=ALU.mult),
                 reads=[pO_k, rz_k], writes=[aoT_k[h][c]])
            return nxt_store

        tiles = [(h, c) for h in range(8) for c in range(4)]
        wq_cur = {}

        def get_wq(h):
            if h not in wq_cur:
                wq_cur[h] = load_wb(wb_d[WB_Q + h])
            return wq_cur[h]
        steps0, pend = q_prep_steps(0, 0)
        for st_ in steps0:
            st_()
        pre_qk = None
        for ti, (h, c) in enumerate(tiles):
            side, nxt, nxt_arg = (), None, None
            if ti + 1 < len(tiles):
                side, nxt = q_prep_steps(*tiles[ti + 1])
                nxt_arg = (tiles[ti + 1][0], nxt[0], nxt[1])
            pre_qk = attend(h, c, *pend, side=side, pre_qk=pre_qk, nxt=nxt_arg)
            pend = nxt
        if b == 0:
            dump("aoT", aoT.rearrange("p a b -> p (a b)"), [k for row in aoT_k for k in row], [128, KT * T], BF16)
        if stop_after <= 3:
            break
        pre["mg0_0"] = load_wb(wb_d[WB_GA + 0])
        pre["my0_0"] = load_wb(wap_d[0])
        S.barrier()

        off = 0
        mT, off = carve(arenaB, off, [128, KT, T], BF16); mT_k = [Tk("mT%d" % c) for c in range(4)]
        sg_r = ring(2, [128, 512], F32, "sg")
        ma_r = ring(5, [128, 512], F32, "ma")
        assert off <= 16384, off
        bG = Ring(banks[0:4]); bY = Ring(banks[4:8])
        for f in range(8):
            mas = []
            for half in range(2):
                wg, wg_k = take("mg%d_%d" % (f, half), lambda: load_wb(wb_d[(WB_GA if half == 0 else WB_GB) + f]))
                wy, wy_k = take("my%d_%d" % (f, half), lambda: load_wb(wap_d[f] if half == 0 else wgp_d[f]))
                src = aoT if half == 0 else goT
                for c in range(4):
                    pG, pG_k = bG.next()
                    pY, pY_k = bY.next()

                    def mmg(e, pG=pG, c=c, wg=wg):
                        last = None
                        for kt in range(KT):
                            last = e.matmul(pG, lhsT=wg[:, kt, :], rhs=hxT[:, kt, c * 512:(c + 1) * 512], start=(kt == 0), stop=(kt == KT - 1))
                        return last
                    S.op("pe", mmg, reads=[wg_k, hxT_k[c]], writes=[pG_k])

                    def mmy(e, pY=pY, c=c, wy=wy, src=src):
                        last = None
                        for kt in range(KT):
                            last = e.matmul(pY, lhsT=wy[:, kt, :], rhs=src[:, kt, c * 512:(c + 1) * 512], start=(kt == 0), stop=(kt == KT - 1))
                        return last
                    srck = [aoT_k[hh][c] for hh in range(8)] if half == 0 else goT_k
                    S.op("pe", mmy, reads=[wy_k] + list(srck), writes=[pY_k])
                    sg, sg_k = sg_r.next()
                    S.op("act", lambda e, pG=pG, sg=sg: e.activation(out=sg, in_=pG, func=AF.Sigmoid), reads=[pG_k], writes=[sg_k])
                    if half == 0:
                        ma, ma_k = ma_r.next()
                        S.op("dve", lambda e, pY=pY, sg=sg, ma=ma: e.tensor_tensor(out=ma, in0=pY, in1=sg, op=ALU.mult), reads=[pY_k, sg_k], writes=[ma_k])
                        mas.append((ma, ma_k))
                    else:
                        ma, ma_k = mas[c]
                        mb, mb_k = ma_r.next()
                        S.op("dve", lambda e, pY=pY, sg=sg, mb=mb: e.tensor_tensor(out=mb, in0=pY, in1=sg, op=ALU.mult), reads=[pY_k, sg_k], writes=[mb_k])
                        S.op("dve", lambda e, ma=ma, mb=mb, f=f, c=c: e.tensor_tensor(out=mT[:, f, c * 512:(c + 1) * 512], in0=ma, in1=mb, op=ALU.add),
                             reads=[ma_k, mb_k], wpart=[mT_k[c]])
        if b == 0:
            dump("mT", mT.rearrange("p a b -> p (a b)"), mT_k, [128, KT * T], BF16)
        if stop_after <= 4:
            break
        pre["wo0"] = load_wa(wo_d[:, :, 0:512], 512)
        S.barrier()

        x1d_k = [Tk("x1d%d" % i) for i in range(16)]
        offA = 0
        wo_hi, offA = carve(arenaA, offA, [128, KT, 512], BF16); wo_k = Tk("wo")
        xl_items = []
        for i in range(4):
            ap, offA = carve(arenaA, offA, [128, D])
            xl_items.append((ap, Tk("xl%d" % i)))
        xl_r = Ring(xl_items)
        x1_items = []
        for i in range(8):
            ap, offA = carve(arenaA, offA, [128, D])
            x1_items.append((ap, Tk("x1t%d" % i)))
        x1_r = Ring(x1_items)
        assert offA <= 16384, offA
        off = 8192
        xn_r = ring(8, [128, D], BF16, "xn5")
        junk5, off = carve(arenaB, off, [128, D], BF16); junk5_k = Tk("junk5")
        st5_r = ring(3, [128, 12], F32, "st5")
        assert off <= 16384, off
        wo0, wo0_k = pre.pop("wo0")
        S.dma("pool", wo_hi, wo_d[:, :, 512:1024], owner=wo_k, writes=[wo_k])
        bY5 = Ring(banks[0:4])

        def make_x1(i, ap, tk):
            xl, xl_k = xl_r.next()
            S.dma("sp", xl, x_d[b, i * 128:(i + 1) * 128, :], owner=xl_k, writes=[xl_k])
            for cc in range(2):
                ps, ps_k = bY5.next()

                def mm(e, ps=ps, cc=cc):
                    last = None
                    for kt in range(KT):
                        rhs = wo0[:, kt, 0:512] if cc == 0 else wo_hi[:, kt, :]
                        last = e.matmul(ps, lhsT=mT[:, kt, i * 128:(i + 1) * 128], rhs=rhs, start=(kt == 0), stop=(kt == KT - 1))
                    return last
                S.op("pe", mm, reads=[mT_k[i // 4], wo0_k if cc == 0 else wo_k], writes=[ps_k])
                S.op("dve", lambda e, ps=ps, cc=cc: e.tensor_tensor(out=ap[:, cc * 512:(cc + 1) * 512], in0=ps, in1=G12[:, b, 0, cc * 512:(cc + 1) * 512], op=ALU.mult),
                     reads=[ps_k, G12_k], wpart=[tk])
                S.op("dve", lambda e, cc=cc: e.tensor_tensor(out=ap[:, cc * 512:(cc + 1) * 512], in0=ap[:, cc * 512:(cc + 1) * 512], in1=xl[:, cc * 512:(cc + 1) * 512], op=ALU.add),
                     reads=[xl_k, tk], wpart=[tk])
            S.dma("pool", out_d[b, i * 128:(i + 1) * 128, :], ap, owner=tk, reads=[tk], writes=[x1d_k[i]], final=True)

        def bank45():
            bnk = banks[4 + (st["bank"] % 4)]
            st["bank"] += 1
            return bnk
        st["bankfn"] = bank45
        norm_transpose(make_x1, 16, hxT, hxT_k, lambda kt: sc2e[:, kt, b:b + 1], lambda kt: sh2T[:, kt, b:b + 1],
                       x1_r, xn_r, junk5, junk5_k, st5_r, act_kts=(0, 1, 2, 4, 5, 6))
        st["bankfn"] = bank
        if b == 0:
            dump("h2T", hxT.rearrange("p a b -> p (a b)"), hxT_k, [128, KT * T], BF16)
        if stop_after <= 5:
            break
        pre["fa0_0"] = load_wb(wfi_d[0])
        pre["fb0_0"] = load_wb(wfi_d[NJ + 0])
        S.barrier()

        offA = 0
        uT, offA = carve(arenaA, offA, [128, NJ, 1024], BF16); uT_k = [Tk("uT0"), Tk("uT1")]
        xr6_items, stg_items = [], []
        for i in range(2):
            ap, offA = carve(arenaA, offA, [128, D]); xr6_items.append((ap, Tk("xr6_%d" % i)))
        for i in range(2):
            ap, offA = carve(arenaA, offA, [128, D]); stg_items.append((ap, Tk("stg%d" % i)))
        xr6_r, stg_r = Ring(xr6_items), Ring(stg_items)
        assert offA <= 16384, offA
        off = 0
        WoG, off = carve(arenaB, off, [128, NJ, D], BF16); WoG_k = Tk("WoG")
        sa_r = ring(2, [128, 512], F32, "sa")
        wf_r = ring(2, [128, D], BF16, "wf")
        assert off <= 16384, off
        WoG_kj = [Tk("WoG%d" % j) for j in range(2)]

        def load_wog(piece):
            j2 = piece // 6
            S.dma("pool", WoG[:, 2 * piece:2 * piece + 2, :], wfo_d[2 * piece:2 * piece + 2].rearrange("j p n -> p j n"), owner=WoG_kj[j2], writes=[])
            WoG_kj[j2].w = (WoG_kj[j2].dsem, WoG_kj[j2].dcnt)
        bA = Ring(banks[0:2]); bB = Ring(banks[2:4]); bF = Ring(banks[4:8])
        for H in range(2):
            for j in range(NJ):
                wa_j, wa_jk = take("fa%d_%d" % (j, H), lambda: load_wb(wfi_d[j]))
                wb_j, wb_jk = take("fb%d_%d" % (j, H), lambda: load_wb(wfi_d[NJ + j]))
                if H == 0 and 1 <= j <= 11:
                    load_wog(j - 1)
                for cl in range(2):
                    c = 2 * H + cl
                    pA, pA_k = bA.next()
                    pB, pB_k = bB.next()

                    def mma(e, pA=pA, c=c, wa_j=wa_j):
                        last = None
                        for kt in range(KT):
                            last = e.matmul(pA, lhsT=wa_j[:, kt, :], rhs=hxT[:, kt, c * 512:(c + 1) * 512], start=(kt == 0), stop=(kt == KT - 1))
                        return last
                    S.op("pe", mma, reads=[wa_jk, hxT_k[c]], writes=[pA_k])

                    def mmb(e, pB=pB, c=c, wb_j=wb_j):
                        last = None
                        for kt in range(KT):
                            last = e.matmul(pB, lhsT=wb_j[:, kt, :], rhs=hxT[:, kt, c * 512:(c + 1) * 512], start=(kt == 0), stop=(kt == KT - 1))
                        return last
                    S.op("pe", mmb, reads=[wb_jk, hxT_k[c]], writes=[pB_k])
                    sa, sa_k = sa_r.next()
                    S.op("act", lambda e, pA=pA, sa=sa: e.activation(out=sa, in_=pA, func=AF.Silu), reads=[pA_k], writes=[sa_k])
                    S.op("dve", lambda e, pB=pB, sa=sa, j=j, cl=cl: e.tensor_tensor(out=uT[:, j, cl * 512:(cl + 1) * 512], in0=pB, in1=sa, op=ALU.mult),
                         reads=[pB_k, sa_k], wpart=[uT_k[cl]])
            if H == 0:
                for j_ in range(2):
                    pre["fa%d_1" % j_] = load_wb(wfi_d[j_])
                    pre["fb%d_1" % j_] = load_wb(wfi_d[NJ + j_])
            for tl in range(8):
                i = H * 8 + tl
                xr6, xr6_k = xr6_r.next()
                S.dma("sp", xr6, out_d[b, i * 128:(i + 1) * 128, :], owner=xr6_k, reads=[x1d_k[i]], writes=[xr6_k])
                stg, stg_k = stg_r.next()
                for cc in range(2):
                    pF, pF_k = bF.next()

                    def mmf(e, pF=pF, tl=tl, cc=cc):
                        last = None
                        for j in range(NJ):
                            last = e.matmul(pF, lhsT=uT[:, j, tl * 128:(tl + 1) * 128], rhs=WoG[:, j, cc * 512:(cc + 1) * 512], start=(j == 0), stop=(j == NJ - 1))
                        return last
                    S.op("pe", mmf, reads=[uT_k[tl // 4]] + WoG_kj, writes=[pF_k])
                    S.op("dve", lambda e, pF=pF, cc=cc, stg=stg: e.tensor_tensor(out=stg[:, cc * 512:(cc + 1) * 512], in0=pF, in1=G12[:, b, 1, cc * 512:(cc + 1) * 512], op=ALU.mult),
                         reads=[pF_k, G12_k], wpart=[stg_k])
                    S.op("dve", lambda e, cc=cc, stg=stg, xr6=xr6: e.tensor_tensor(out=stg[:, cc * 512:(cc + 1) * 512], in0=stg[:, cc * 512:(cc + 1) * 512], in1=xr6[:, cc * 512:(cc + 1) * 512], op=ALU.add),
                         reads=[xr6_k, stg_k], wpart=[stg_k])
                S.dma("pool", out_d[b, i * 128:(i + 1) * 128, :], stg, owner=stg_k, reads=[stg_k], writes=[x1d_k[i]], final=True)
        S.barrier()

    S.finish()
    return nc, dumps


_CACHE = {}


def kernel(**inputs):
    shared = prep_shared(inputs)
    if "nc" not in _CACHE:
        _CACHE["nc"] = build()[0]
    nc = _CACHE["nc"]
    in_maps = []
    for core in range(NCORES):
        m = dict(shared)
        m.update(prep_core(inputs, core))
        in_maps.append(m)
    res = run_bass_kernel_spmd(nc, in_maps, core_ids=list(range(NCORES)))
    out = np.concatenate([np.asarray(r["out"]) for r in res.results], axis=0)
    return out.astype(np.float32)
```

```python
import numpy as np
import concourse.bass as bass
import concourse.mybir as mybir
from concourse.bass_utils import run_bass_kernel_spmd

F32 = mybir.dt.float32
BF16 = mybir.dt.bfloat16
AF = mybir.ActivationFunctionType
ALU = mybir.AluOpType
AX = mybir.AxisListType

D = 1024
T = 2048
TC = 256
NB = 2
NCORES = 8
DFF = 2816
NJ = DFF // 128
EPS = 1e-6
KT = 8
ATT_SCALE = 128.0 ** -0.5
ATT_SHIFT = 30.0
GLA_QSCALE = 128.0 ** -0.5

WB_Q = 0
WB_K = 8
WB_GQK = 10
WB_GA = 18
WB_GB = 26
N_WB = 34


class Tk:
    __slots__ = ("w", "r", "dsem", "dcnt", "name", "excl", "ws")

    def __init__(self, name="", excl=False):
        self.excl = excl
        self.w = None
        self.ws = {}
        self.r = {}
        self.dsem = None
        self.dcnt = 0
        self.name = name


class _Eng:
    def __init__(self, h, sem):
        self.h = h
        self.sem = sem
        self.cnt = 0
        self.known = {}
        self.is_pe = False


class Sched:
    def __init__(self, nc):
        self.nc = nc
        self.E = {
            "pe": _Eng(nc.tensor, nc.alloc_semaphore("s_pe")),
            "act": _Eng(nc.scalar, nc.alloc_semaphore("s_act")),
            "dve": _Eng(nc.vector, nc.alloc_semaphore("s_dve")),
            "pool": _Eng(nc.gpsimd, nc.alloc_semaphore("s_pool")),
            "sp": _Eng(nc.sync, nc.alloc_semaphore("s_sp")),
        }
        self.E["pe"].is_pe = True
        self.nsem = 5
        self.dsems = []
        self.final = {}

    def _wait(self, E, reads, writes, wpart=()):
        need = {}

        def add(sp):
            if sp is None:
                return
            k = id(sp[0])
            if k not in need or need[k][1] < sp[1]:
                need[k] = sp

        for t in reads:
            add(t.w)
            for sp in t.ws.values():
                add(sp)
            if t.excl:
                for sp in t.r.values():
                    add(sp)
        for t in writes:
            add(t.w)
            for sp in t.ws.values():
                add(sp)
            for sp in t.r.values():
                add(sp)
        for t in wpart:
            add(t.w)
            for sp in t.r.values():
                add(sp)
        for k, (sem, val) in need.items():
            if sem is E.sem and E.is_pe:
                continue
            if E.known.get(k, 0) >= val:
                continue
            E.h.wait_ge(sem, val)
            E.known[k] = val

    def op(self, e, fn, reads=(), writes=(), wpart=()):
        E = self.E[e]
        self._wait(E, reads, writes, wpart)
        inst = fn(E.h)
        inst.then_inc(E.sem, 1)
        E.cnt += 1
        sp = (E.sem, E.cnt)
        for t in writes:
            t.w = sp
            t.ws = {}
            t.r = {}
        for t in wpart:
            t.ws[id(E.sem)] = sp
        k = id(E.sem)
        for t in reads:
            t.r[k] = sp
        return sp

    def dma(self, q, out, in_, owner, reads=(), writes=(), final=False, **kw):
        E = self.E[q]
        self._wait(E, reads, writes)
        if owner.dsem is None:
            owner.dsem = self.nc.alloc_semaphore("s_d%d" % self.nsem)
            self.nsem += 1
            self.dsems.append(owner)
        owner.dcnt += 16
        E.h.dma_start(out=out, in_=in_, **kw).then_inc(owner.dsem, 16)
        sp = (owner.dsem, owner.dcnt)
        for t in writes:
            t.w = sp
            t.ws = {}
            t.r = {}
        k = id(owner.dsem)
        for t in reads:
            t.r[k] = sp
        if final:
            self.final[k] = sp
        return sp

    def barrier(self):
        sps = [(E.sem, E.cnt) for E in self.E.values() if E.cnt > 0]
        sps += [(o.dsem, o.dcnt) for o in self.dsems]
        for E in self.E.values():
            for sem, val in sps:
                if sem is E.sem:
                    continue
                k = id(sem)
                if E.known.get(k, 0) >= val:
                    continue
                E.h.wait_ge(sem, val)
                E.known[k] = val

    def finish(self):
        E = self.E["sp"]
        for k, (sem, val) in self.final.items():
            E.h.wait_ge(sem, val)


class Ring:
    def __init__(self, items):
        self.items = items
        self.i = 0

    def next(self):
        it = self.items[self.i % len(self.items)]
        self.i += 1
        return it


def _bstyle(w_cols):
    M = w_cols.shape[1]
    return np.ascontiguousarray(w_cols.reshape(KT, 128, M).transpose(1, 0, 2))


def _deint():
    return np.concatenate([np.arange(0, 128, 2), np.arange(1, 128, 2)])


def _rope_tables():
    rows = T // 64
    row = np.repeat(np.arange(rows, dtype=np.float64), 64)
    col = np.tile(np.arange(64, dtype=np.float64), rows)
    inv_freq = 1.0 / (10000.0 ** (np.arange(0, 64, 2, dtype=np.float64) / 64.0))
    ang = np.concatenate([row[:, None] * inv_freq[None], col[:, None] * inv_freq[None]], axis=-1)
    cos = np.cos(ang).astype(np.float32)
    sin = np.sin(ang).astype(np.float32)
    cs = np.concatenate([cos.T, cos.T], axis=0)
    ssa = np.concatenate([sin.T, -sin.T], axis=0)
    return np.ascontiguousarray(cs), np.ascontiguousarray(ssa)


def prep_shared(inp):
    f = lambda a: np.asarray(a, dtype=np.float32)
    w_in = f(inp["w_in"])[0]
    o = {}
    o["w_ada"] = _bstyle(f(inp["w_ada"])[0])
    o["b_ada"] = f(inp["b_ada"])[0].reshape(1, 6 * D)
    o["n1g"] = np.ascontiguousarray(f(inp["norm1_g"])[0].reshape(KT, 128).T)
    o["n2g"] = np.ascontiguousarray(f(inp["norm2_g"])[0].reshape(KT, 128).T)
    oq, ok, ov = 0, 1024, 1280
    ogq, ogk, ogv, ogg, olr, omg = 1536, 2048, 2560, 3584, 4608, 4640
    perm = _deint()
    wb = np.zeros((N_WB, 128, KT, 128), np.float32)
    for h in range(8):
        wb[WB_Q + h] = _bstyle(w_in[:, oq + h * 128 + perm])
    for g in range(2):
        wb[WB_K + g] = _bstyle(w_in[:, ok + g * 128 + perm])
    for h in range(4):
        wb[WB_GQK + 2 * h] = _bstyle(w_in[:, ogq + h * 128: ogq + (h + 1) * 128])
        wb[WB_GQK + 2 * h + 1] = _bstyle(w_in[:, ogk + h * 128: ogk + (h + 1) * 128])
    for t in range(16):
        wb[WB_GA + t] = _bstyle(w_in[:, omg + t * 128: omg + (t + 1) * 128])
    o["wb"] = wb
    o["wlr"] = _bstyle(w_in[:, olr: olr + 32])
    wag = np.zeros((4, 128, KT, 640), np.float32)
    for h in range(4):
        cols = np.concatenate([np.arange(ogk + h * 128, ogk + (h + 1) * 128),
                               np.arange(ogv + h * 256, ogv + (h + 1) * 256),
                               np.arange(ogg + h * 256, ogg + (h + 1) * 256)])
        wag[h] = _bstyle(w_in[:, cols])
    o["wag"] = wag
    o["wav"] = _bstyle(w_in[:, ov: ov + 256])
    qg = np.stack([f(inp["q_norm_g"])[0][perm], f(inp["k_norm_g"])[0][perm]], axis=1)
    o["qg"] = np.ascontiguousarray(qg)
    o["qgrow"] = np.concatenate([f(inp["q_norm_g"])[0], f(inp["k_norm_g"])[0]]).reshape(1, 256)
    up = np.zeros((64, 512), np.float32)
    up[0:16] = f(inp["gk_up_f"])[0]
    up[16] = f(inp["gk_up_f_b"])[0]
    up[32:48] = f(inp["gk_up_b"])[0]
    up[48] = f(inp["gk_up_b_b"])[0]
    o["up"] = up
    o["gn"] = np.ascontiguousarray(np.broadcast_to(f(inp["gla_norm_g"])[0][None, :], (128, 256)))
    wap = f(inp["w_attn_proj"])[0]
    wgp = f(inp["w_gla_proj"])[0]
    o["wap"] = np.stack([_bstyle(wap[:, t * 128:(t + 1) * 128]) for t in range(8)])
    o["wgp"] = np.stack([_bstyle(wgp[:, t * 128:(t + 1) * 128]) for t in range(8)])
    o["wo"] = _bstyle(f(inp["w_out"])[0])
    wfi = f(inp["w_ffn_in"])[0]
    o["wfi"] = np.stack([_bstyle(wfi[:, t * 128:(t + 1) * 128]) for t in range(2 * NJ)])
    o["wfo"] = np.ascontiguousarray(f(inp["w_ffn_out"])[0].reshape(NJ, 128, D))
    cs, ssa = _rope_tables()
    o["cs"] = cs
    o["ssa"] = ssa
    k = np.arange(128)
    cst = np.zeros((128, 5, 128), np.float32)
    cst[:, 0, :] = np.eye(128, dtype=np.float32)
    cst[:, 1, :] = (k[:, None] <= k[None, :])
    cst[:, 2, :] = (k[:, None] >= k[None, :])
    cst[:, 3, :] = (k[:, None] < k[None, :])
    cst[:, 4, :] = (k[:, None] > k[None, :])
    o["cst"] = cst
    sel = np.zeros((3, 3, 128), np.float32)
    for b in range(3):
        sel[b, b, :] = 1.0
    o["sel"] = sel
    return o


def prep_core(inp, core):
    f = lambda a: np.asarray(a, dtype=np.float32)
    b0 = core * NB
    o = {}
    o["x"] = np.ascontiguousarray(f(inp["x"])[b0:b0 + NB])
    o["ctx"] = np.ascontiguousarray(f(inp["ctx"])[b0:b0 + NB])
    cv = np.concatenate([f(inp["c"])[b0:b0 + NB], f(inp["c_ctx"])[None, :]], axis=0)
    o["cT"] = np.ascontiguousarray(cv.reshape(3, KT, 128).transpose(2, 1, 0))
    return o


def build(dbg=(), stop_after=99, nb=NB):
    nc = bass.Bass("TRN2", target_bir_lowering=False)
    S = Sched(nc)
    dumps = []

    def din(name, shape, dt=F32):
        return nc.dram_tensor(name, list(shape), dt, kind="ExternalInput").ap()

    x_d = din("x", [NB, T, D])
    ctx_d = din("ctx", [NB, TC, D])
    cT_d = din("cT", [128, KT, 3])
    wada_d = din("w_ada", [128, KT, 6 * D])
    bada_d = din("b_ada", [1, 6 * D])
    n1g_d = din("n1g", [128, KT])
    n2g_d = din("n2g", [128, KT])
    wb_d = din("wb", [N_WB, 128, KT, 128])
    wlr_d = din("wlr", [128, KT, 32])
    wag_d = din("wag", [4, 128, KT, 640])
    wav_d = din("wav", [128, KT, 256])
    qg_d = din("qg", [128, 2])
    qgrow_d = din("qgrow", [1, 256])
    up_d = din("up", [64, 512])
    gn_d = din("gn", [128, 256])
    wap_d = din("wap", [8, 128, KT, 128])
    wgp_d = din("wgp", [8, 128, KT, 128])
    wo_d = din("wo", [128, KT, D])
    wfi_d = din("wfi", [2 * NJ, 128, KT, 128])
    wfo_d = din("wfo", [NJ, 128, D])
    cs_d = din("cs", [128, T])
    ssa_d = din("ssa", [128, T])
    cst_d = din("cst", [128, 5, 128])
    sel_d = din("sel", [3, 3, 128])
    out_d = nc.dram_tensor("out", [NB, T, D], F32, kind="ExternalOutput").ap()

    def sb(name, shape, dt=F32):
        return nc.alloc_sbuf_tensor("sb_" + name, list(shape), dt).ap()

    def dump(name, ap, tk, shape, dt=F32):
        if name not in dbg:
            return
        d = nc.dram_tensor("dbg_" + name, list(shape), dt, kind="ExternalOutput").ap()
        tks = tk if isinstance(tk, (list, tuple)) else [tk]
        S.dma("sp", d, ap, owner=tks[0], reads=list(tks), final=True)
        dumps.append("dbg_" + name)

    banks = []
    for i in range(8):
        banks.append((nc.alloc_psum_tensor("ps%d" % i, [128, 512], F32).ap(), Tk("ps%d" % i, excl=True)))

    cst = sb("cst", [128, 5, 128]); cst_k = Tk("cst")
    ident = cst[:, 0, :]
    Uincl, Lincl, Ustrict, Lstrict = cst[:, 1, :], cst[:, 2, :], cst[:, 3, :], cst[:, 4, :]
    ones_bf = sb("ones_bf", [128, 128], BF16); ones_k = Tk("ones")
    ones_f = sb("ones_f", [128, 128]);
    gn = sb("gn", [128, 256])
    qg = sb("qg", [128, 2])
    qgrow = sb("qgrow", [1, 256])
    up = sb("up", [64, 512])
    n1g = sb("n1g", [128, KT])
    n2g = sb("n2g", [128, KT])
    sel = sb("sel", [3, 3, 128])
    negshift = sb("negshift", [128, 1])
    mask4 = sb("mask4", [128, 2, 512])
    cstb = sb("cstb", [128, 4, 128], BF16)
    ident_bf = sb("ident_bf", [128, 128], BF16)
    up_bf = sb("up_bf", [64, 512], BF16)
    gn2 = sb("gn2", [128, 2, 256])
    small_k = Tk("small")
    modT = sb("modT", [128, 32, 3])
    modT_k = Tk("modT")
    sc1e = sb("sc1e", [128, KT, 3]); sc2e = sb("sc2e", [128, KT, 3])
    G12 = sb("G12", [128, NB, 2, D], BF16); G12_k = Tk("G12")

    hxT = sb("hxT", [128, KT, T], BF16)
    hxT_k = [Tk("hxT%d" % c) for c in range(4)]
    hcT = sb("hcT", [128, KT, TC], BF16)
    hcT_k = Tk("hcT")
    arenaA = sb("arenaA", [128, 65536 // 4])
    arenaB = sb("arenaB", [128, 64 * 1024 // 4])
    wbuf = Ring([(sb("wbuf%d" % i, [128, KT, 128], BF16), Tk("wbuf%d" % i)) for i in range(4)])
    wabuf = Ring([(sb("wabuf%d" % i, [128, KT, 640], BF16), Tk("wabuf%d" % i)) for i in range(1)])

    def load_wb(src):
        ap, tk = wbuf.next()
        S.dma("pool", ap, src, owner=tk, writes=[tk])
        return ap, tk

    def load_wa(src, ncols):
        ap, tk = wabuf.next()
        S.dma("pool", ap[:, :, 0:ncols], src, owner=tk, writes=[tk])
        return ap, tk

    S.dma("sp", cst, cst_d, owner=cst_k, writes=[cst_k])
    for dst, src in ((gn, gn_d), (qg, qg_d), (qgrow, qgrow_d), (up, up_d), (n1g, n1g_d), (n2g, n2g_d), (sel, sel_d)):
        S.dma("sp", dst, src, owner=small_k, writes=[])
    small_k.w = (small_k.dsem, small_k.dcnt)
    S.op("dve", lambda e: e.memset(ones_bf, 1.0), writes=[ones_k])
    S.op("dve", lambda e: e.memset(ones_f, 1.0), writes=[ones_k])
    for dr_ in range(2):
        for k_ in range(4):
            S.op("dve", lambda e, dr_=dr_, k_=k_: e.tensor_copy(out=mask4[:, dr_, k_ * 128:(k_ + 1) * 128], in_=cst[:, 1 + dr_, :]), reads=[cst_k], writes=[cst_k])
    for k_ in range(2):
        S.op("dve", lambda e, k_=k_: e.tensor_copy(out=gn2[:, k_, :], in_=gn), reads=[small_k], writes=[small_k])
    S.op("dve", lambda e: e.tensor_copy(out=cstb, in_=cst[:, 1:5, :]), reads=[cst_k], writes=[cst_k])
    S.op("dve", lambda e: e.tensor_copy(out=ident_bf, in_=cst[:, 0, :]), reads=[cst_k], writes=[cst_k])
    S.op("dve", lambda e: e.tensor_copy(out=up_bf, in_=up), reads=[small_k], writes=[small_k])

    def carve(base_ap, off_words, shape, dt=F32):
        n = 1
        for s in shape[1:]:
            n *= s
        words = n if dt == F32 else (n + 1) // 2
        v = base_ap[:, off_words: off_words + words]
        if dt != F32:
            v = v.bitcast(dt)
        if len(shape) == 3:
            v = v.rearrange("p (a b) -> p a b", a=shape[1])
        return v[0:shape[0]] if shape[0] != 128 else v, off_words + words

    off = 0
    cT, off = carve(arenaB, off, [128, KT, 3])
    scT, off = carve(arenaB, off, [128, KT, 3])
    bada, off = carve(arenaB, off, [1, 6 * D])
    off = 0 + 24 + 24 + 6 * D
    mod_sb, off = carve(arenaB, off, [3, 6 * D])
    wa_items = []
    offw = 0
    for i_ in range(4):
        ap_, offw = carve(arenaA, offw, [128, KT, 512])
        wa_items.append((ap_, Tk("wa%d" % i_)))
    wa_ring = Ring(wa_items)
    cT_k, scT_k, bada_k, mod_k = Tk("cT"), Tk("scT"), Tk("bada"), Tk("mod")
    S.dma("sp", cT, cT_d, owner=cT_k, writes=[cT_k])
    S.dma("sp", bada, bada_d, owner=bada_k, writes=[bada_k])
    S.op("act", lambda e: e.activation(out=scT, in_=cT, func=AF.Silu), reads=[cT_k], writes=[scT_k])
    bi_ = 0
    for n in range(12):
        wa, wa_k = wa_ring.next()
        S.dma("sp", wa, wada_d[:, :, n * 512:(n + 1) * 512], owner=wa_k, writes=[wa_k])
        ps, ps_k = banks[bi_ % 8]; bi_ += 1

        def mm(e, wa=wa, ps=ps, n=n):
            for kt in range(KT):
                e.matmul(ps[0:3, :], lhsT=scT[:, kt, :], rhs=wa[:, kt, :], start=(kt == 0), stop=False)
            return e.matmul(ps[0:3, :], lhsT=ones_f[0:1, 0:3], rhs=bada[0:1, n * 512:(n + 1) * 512], start=False, stop=True)
        S.op("pe", mm, reads=[scT_k, wa_k, bada_k, ones_k], writes=[ps_k])
        S.op("dve", lambda e, ps=ps, n=n: e.tensor_copy(out=mod_sb[:, n * 512:(n + 1) * 512], in_=ps[0:3, :]),
             reads=[ps_k], writes=[mod_k])
    ps, ps_k = banks[bi_ % 8]; bi_ += 1

    def trs(e, ps=ps):
        last = None
        for gi, grp in enumerate((0, 1, 3, 4)):
            for kt in range(KT):
                j = gi * KT + kt
                c0 = grp * D + kt * 128
                last = e.transpose(ps[:, j * 3:(j + 1) * 3], mod_sb[0:3, c0:c0 + 128], ident[0:3, 0:3])
        return last
    S.op("pe", trs, reads=[mod_k, cst_k], writes=[ps_k])
    S.op("dve", lambda e, ps=ps: e.tensor_copy(out=modT.rearrange("p a b -> p (a b)"), in_=ps[:, 0:96]),
         reads=[ps_k], writes=[modT_k])
    for j in range(3):
        S.op("dve", lambda e, j=j: e.scalar_tensor_tensor(out=sc1e[:, :, j], in0=modT[:, 8:16, j], scalar=1.0, in1=n1g,
                                                           op0=ALU.add, op1=ALU.mult), reads=[modT_k, small_k], writes=[modT_k])
        S.op("dve", lambda e, j=j: e.scalar_tensor_tensor(out=sc2e[:, :, j], in0=modT[:, 24:32, j], scalar=1.0, in1=n2g,
                                                           op0=ALU.add, op1=ALU.mult), reads=[modT_k, small_k], writes=[modT_k])
    sh1T = modT[:, 0:8, :]
    sh2T = modT[:, 16:24, :]
    for b in range(NB):
        for gi, grp in enumerate((2, 5)):
            for cc in range(2):
                ps, ps_k = banks[bi_ % 8]; bi_ += 1
                c0 = grp * D + cc * 512
                S.op("pe", lambda e, ps=ps, b=b, c0=c0: e.matmul(ps, lhsT=sel[:, b, :], rhs=mod_sb[0:3, c0:c0 + 512], start=True, stop=True),
                     reads=[mod_k, small_k], writes=[ps_k])
                S.op("act", lambda e, ps=ps, b=b, gi=gi, cc=cc: e.copy(out=G12[:, b, gi, cc * 512:(cc + 1) * 512], in_=ps),
                     reads=[ps_k], writes=[G12_k])
    mx = sb("mx", [1, 4])
    S.op("dve", lambda e: e.tensor_reduce(out=mx[:, 0:1], in_=qgrow[:, 0:128], axis=AX.X, op=ALU.max, apply_absolute_value=True),
         reads=[small_k], writes=[modT_k])
    S.op("dve", lambda e: e.tensor_reduce(out=mx[:, 1:2], in_=qgrow[:, 128:256], axis=AX.X, op=ALU.max, apply_absolute_value=True),
         reads=[small_k], writes=[modT_k])
    S.op("dve", lambda e: e.scalar_tensor_tensor(out=mx[:, 2:3], in0=mx[:, 0:1], scalar=-(128.0 ** 0.5), in1=mx[:, 1:2],
                                                 op0=ALU.mult, op1=ALU.mult), reads=[modT_k], writes=[modT_k])
    ps, ps_k = banks[bi_ % 8]; bi_ += 1
    S.op("pe", lambda e, ps=ps: e.matmul(ps[:, 0:1], lhsT=ones_f[0:1, :], rhs=mx[0:1, 2:3], start=True, stop=True),
         reads=[modT_k, ones_k], writes=[ps_k])
    S.op("dve", lambda e, ps=ps: e.tensor_copy(out=negshift, in_=ps[:, 0:1]), reads=[ps_k], writes=[modT_k])
    dump("modT", modT.rearrange("p a b -> p (a b)"), modT_k, [128, 96])
    dump("G12", G12.rearrange("p a b c -> p (a b c)"), G12_k, [128, NB * 2 * D], BF16)
    dump("negshift", negshift, modT_k, [128, 1])
    S.barrier()

    st = {"bank": bi_}
    pre = {}

    def take(key, loader):
        return pre.pop(key) if key in pre else loader()

    def bank():
        b = banks[st["bank"] % 8]
        st["bank"] += 1
        return b
    st["bankfn"] = bank

    def norm_transpose(load_tile, ntiles, dstT, dst_keys, sc_col, sh_col, xr_ring, xn_ring, junk, junk_k, stat_ring, act_kts=(0, 2, 4, 6)):
        ngrp = (ntiles + 3) // 4

        def stage_n(g):
            tiles = list(range(g * 4, min(ntiles, g * 4 + 4)))
            nt = len(tiles)
            xrs = []
            for i in tiles:
                xr, xr_k = xr_ring.next()
                load_tile(i, xr, xr_k)
                xrs.append((xr, xr_k))
            stt, stt_k = stat_ring.next()
            for j, (xr, xr_k) in enumerate(xrs):
                S.op("act", lambda e, xr=xr, j=j: e.activation(out=junk, in_=xr, func=AF.Square, accum_out=stt[:, j:j + 1]),
                     reads=[xr_k], writes=[junk_k], wpart=[stt_k])
            S.op("act", lambda e: e.activation(out=stt[:, 4:4 + nt], in_=stt[:, 0:nt], func=AF.Ln, scale=1.0 / D, bias=EPS),
                 reads=[stt_k], writes=[stt_k])
            S.op("act", lambda e: e.activation(out=stt[:, 8:8 + nt], in_=stt[:, 4:4 + nt], func=AF.Exp, scale=-0.5),
                 reads=[stt_k], writes=[stt_k])
            xns = []
            for j, (xr, xr_k) in enumerate(xrs):
                xn, xn_k = xn_ring.next()
                S.op("dve", lambda e, xn=xn, xr=xr, j=j: e.tensor_scalar(out=xn, in0=xr, scalar1=stt[:, 8 + j:9 + j], scalar2=None, op0=ALU.mult),
                     reads=[xr_k, stt_k], writes=[xn_k])
                xns.append((xn, xn_k))
            return xns

        def stage_t(g, xns):
            nt = len(xns)
            for kt in range(KT):
                ps, ps_k = st['bankfn']()
                psb = ps.bitcast(BF16)

                def trs(e, psb=psb, kt=kt):
                    last = None
                    for j, (xn, _) in enumerate(xns):
                        last = e.transpose(psb[:, j * 128:(j + 1) * 128], xn[:, kt * 128:(kt + 1) * 128], ident_bf)
                    return last
                S.op("pe", trs, reads=[k for _, k in xns] + [cst_k], writes=[ps_k])
                dst = dstT[:, kt, g * 512: g * 512 + nt * 128]
                if kt in act_kts:
                    S.op("act", lambda e, psb=psb, dst=dst, kt=kt: e.activation(out=dst, in_=psb[:, 0:nt * 128], func=AF.Identity,
                                                                               scale=sc_col(kt), bias=sh_col(kt)),
                         reads=[ps_k, modT_k], wpart=[dst_keys[g]])
                else:
                    S.op("dve", lambda e, psb=psb, dst=dst, kt=kt: e.tensor_scalar(out=dst, in0=psb[:, 0:nt * 128], scalar1=sc_col(kt),
                                                                                  scalar2=sh_col(kt), op0=ALU.mult, op1=ALU.add),
                         reads=[ps_k, modT_k], wpart=[dst_keys[g]])
        cur = stage_n(0)
        for g in range(ngrp):
            nxt = stage_n(g + 1) if g + 1 < ngrp else None
            stage_t(g, cur)
            cur = nxt

    for b in range(nb):
        off = 0
        xr_items = []
        for i in range(8):
            ap, off = carve(arenaB, off, [128, D])
            xr_items.append((ap, Tk("xr%d" % i)))
        xn_items = []
        for i in range(8):
            ap, off = carve(arenaB, off, [128, D], BF16)
            xn_items.append((ap, Tk("xn%d" % i)))
        junk, off = carve(arenaB, off, [128, D], BF16)
        junk_k = Tk("junk")
        stat_items = []
        for i in range(3):
            ap, off = carve(arenaB, off, [128, 12])
            stat_items.append((ap, Tk("stat%d" % i)))
        xr_ring, xn_ring, stat_ring = Ring(xr_items), Ring(xn_items), Ring(stat_items)

        norm_transpose(lambda i, ap, tk: S.dma("sp", ap, ctx_d[b, i * 128:(i + 1) * 128, :], owner=tk, writes=[tk]),
                       2, hcT, [hcT_k], lambda kt: sc1e[:, kt, 2:3], lambda kt: sh1T[:, kt, 2:3],
                       xr_ring, xn_ring, junk, junk_k, stat_ring, act_kts=(0, 4))
        norm_transpose(lambda i, ap, tk: S.dma("sp", ap, x_d[b, i * 128:(i + 1) * 128, :], owner=tk, writes=[tk]),
                       16, hxT, hxT_k, lambda kt: sc1e[:, kt, b:b + 1], lambda kt: sh1T[:, kt, b:b + 1],
                       xr_ring, xn_ring, junk, junk_k, stat_ring, act_kts=(0, 4))
        if b == 0:
            dump("hcT", hcT.rearrange("p a b -> p (a b)"), hcT_k, [128, KT * TC], BF16)
            for c in range(4):
                dump("hxT%d" % c, hxT[:, :, c * 512:(c + 1) * 512], hxT_k[c], [128, KT, 512], BF16)
        if stop_after <= 1:
            break
        pre["gq0"] = load_wb(wb_d[WB_GQK + 0])
        pre["gk0"] = load_wb(wb_d[WB_GQK + 1])
        pre["ga0"] = load_wa(wag_d[0], 640)
        S.barrier()

        goT = arenaA[:, 0:8192].bitcast(BF16).rearrange("p (a b) -> p a b", a=KT)
        aoT = arenaA[:, 8192:16384].bitcast(BF16).rearrange("p (a b) -> p a b", a=KT)
        goT_k = [Tk("goT%d" % h) for h in range(4)]
        aoT_k = [[Tk("aoT%d_%d" % (h, c)) for c in range(4)] for h in range(8)]
        offA = 8192
        qeT, offA = carve(arenaA, offA, [128, 2, T], BF16); qeT_k = [Tk("qeT0"), Tk("qeT1")]
        ATs, offA = carve(arenaA, offA, [128, 2, T], BF16); ATs_k = [Tk("AT0"), Tk("AT1")]
        kds, offA = carve(arenaA, offA, [128, 2, 18 * 128], BF16); kds_k = [Tk("kd0"), Tk("kd1")]
        keT_items = []
        for i_ in range(2):
            ap, offA = carve(arenaA, offA, [128, 512], BF16)
            keT_items.append((ap, Tk("keT%d" % i_)))
        keT_r = Ring(keT_items)
        hl_items = []
        for i_ in range(2):
            ap, offA = carve(arenaA, offA, [128, 2, 512], BF16)
            hl_items.append((ap, Tk("hl%d" % i_)))
        hl_r = Ring(hl_items)
        assert offA <= 16384, offA
        off = 0
        LR, off = carve(arenaB, off, [64, T + TC], BF16); LR_k = Tk("LR")
        gk_tm, off = carve(arenaB, off, [128, 18 * 128], BF16); gk_tm_k = Tk("gk_tm")
        gv, off = carve(arenaB, off, [128, 18, 256], BF16); gv_k = Tk("gv")
        sgg, off = carve(arenaB, off, [128, 16, 256], BF16); sgg_k = Tk("sgg")
        SBst, off = carve(arenaB, off, [128, 16, 256], BF16); SBst_k = Tk("SBst")
        S32p, Sb32p = [], []
        for i_ in range(2):
            ap, off = carve(arenaB, off, [128, 256]); S32p.append((ap, Tk("S32_%d" % i_)))
            ap, off = carve(arenaB, off, [128, 256]); Sb32p.append((ap, Tk("Sb32_%d" % i_)))
        dlx, off = carve(arenaB, off, [128, 2, 20]); dlx_k = [Tk("dlx0"), Tk("dlx1")]
        wlr_sb, off = carve(arenaB, off, [128, KT, 32], BF16); wlr_k = Tk("wlr")
        junk2, off = carve(arenaB, off, [128, 256], BF16); junk2_k = Tk("junk2")

        def ring(n, shape, dt=F32, nm="r"):
            nonlocal off
            items = []
            for i in range(n):
                ap, off = carve(arenaB, off, shape, dt)
                items.append((ap, Tk("%s%d" % (nm, i))))
            return Ring(items)
        e1_r = ring(2, [128, 512], F32, "e1")
        Lw_r = ring(2, [128, 512], BF16, "Lw")
        E_r = ring(2, [128, 512], F32, "E")
        eb_r = ring(2, [128, 512], F32, "eb")
        enb_r = ring(2, [128, 512], F32, "enb")
        Sbf_r = ring(2, [128, 256], BF16, "Sbf")
        go_r = ring(2, [128, 256], F32, "go")
        st_r = ring(3, [128, 4], F32, "st")
        tmp_r = e1_r
        assert off <= 16384, off

        def hsrc(i, kt):
            if i < 2:
                return hcT[:, kt, i * 128:(i + 1) * 128], hcT_k
            return hxT[:, kt, (i - 2) * 128:(i - 1) * 128], hxT_k[(i - 2) // 4]

        S.dma("pool", wlr_sb, wlr_d, owner=wlr_k, writes=[wlr_k])
        S.op("dve", lambda e: e.memset(LR, 1.0), writes=[LR_k])
        for c in range(5):
            n0 = 0 if c == 0 else TC + (c - 1) * 512
            nn = TC if c == 0 else 512
            for dr in range(2):
                ps, ps_k = bank()

                def mm(e, ps=ps, c=c, dr=dr, nn=nn):
                    last = None
                    for kt in range(KT):
                        rhs = hcT[:, kt, :] if c == 0 else hxT[:, kt, (c - 1) * 512:c * 512]
                        last = e.matmul(ps[0:16, 0:nn], lhsT=wlr_sb[:, kt, dr * 16:(dr + 1) * 16], rhs=rhs, start=(kt == 0), stop=(kt == KT - 1))
                    return last
                S.op("pe", mm, reads=[wlr_k, hcT_k if c == 0 else hxT_k[c - 1]], writes=[ps_k])
                S.op("dve", lambda e, ps=ps, dr=dr, n0=n0, nn=nn: e.tensor_copy(out=LR[dr * 32:dr * 32 + 16, n0:n0 + nn], in_=ps[0:16, 0:nn]),
                     reads=[ps_k], wpart=[LR_k])
        if stop_after <= 1.1:
            dump("LR", LR, LR_k, [64, T + TC], BF16)
            break

        def decay_pair(h, tiles, balloc, mid=None):
            nt = len(tiles)
            W = nt * 128
            i0 = tiles[0]
            pzs = []
            for dr in range(2):
                pz, pz_k = balloc()

                def mmz(e, pz=pz, dr=dr):
                    last = None
                    for k, i in enumerate(tiles):
                        last = e.matmul(pz[:, k * 128:(k + 1) * 128], lhsT=LR[dr * 32:dr * 32 + 17, i * 128:(i + 1) * 128],
                                        rhs=up_bf[dr * 32:dr * 32 + 17, h * 128:(h + 1) * 128], start=True, stop=True)
                    return last
                S.op("pe", mmz, reads=[LR_k, small_k], writes=[pz_k])
                pzs.append((pz, pz_k))
            hls = []
            for dr in range(2):
                pz, pz_k = pzs[dr]
                e1, e1_k = e1_r.next()
                S.op("act", lambda e, e1=e1, pz=pz: e.activation(out=e1[:, 0:W], in_=pz[:, 0:W], func=AF.Exp, scale=-1.0), reads=[pz_k], writes=[e1_k])
                Lw, Lw_k = Lw_r.next()
                S.op("act", lambda e, e1=e1, Lw=Lw: e.activation(out=Lw[:, 0:W], in_=e1[:, 0:W], func=AF.Ln, bias=1.0), reads=[e1_k], writes=[Lw_k])
                hls.append((Lw, Lw_k))
            if mid is not None:
                mid()
            pds = []
            for dr in range(2):
                hl, hl_k = hls[dr]
                pd, pd_k = balloc()

                def mmd(e, pd=pd, hl=hl, dr=dr):
                    tri = cstb[:, 3, :] if dr == 0 else cstb[:, 2, :]
                    return e.matmul(pd[:, 0:W], lhsT=tri, rhs=hl[:, 0:W], start=True, stop=True)
                S.op("pe", mmd, reads=[hl_k, cst_k], writes=[pd_k])
                pds.append((pd, pd_k))
            pbs = []
            for dr in range(2):
                hl, hl_k = hls[dr]
                pb, pb_k = balloc()

                def mmb(e, pb=pb, hl=hl, dr=dr):
                    tri = cstb[:, 0, :] if dr == 0 else cstb[:, 1, :]
                    last = None
                    for k in range(nt):
                        last = e.matmul(pb[:, k * 128:(k + 1) * 128], lhsT=hl[:, k * 128:(k + 1) * 128], rhs=tri, start=True, stop=True)
                    return last
                S.op("pe", mmb, reads=[hl_k, cst_k], writes=[pb_k])
                pbs.append((pb, pb_k))
            for dr in range(2):
                pd, pd_k = pds[dr]
                E, E_k = E_r.next()
                S.op("act", lambda e, E=E, pd=pd: e.activation(out=E[:, 0:W], in_=pd[:, 0:W], func=AF.Exp, scale=-1.0 / 16), reads=[pd_k], writes=[E_k])
                S.op("dve", lambda e, E=E, dr=dr: e.tensor_tensor(out=kds[:, dr, i0 * 128:i0 * 128 + W], in0=gk_tm[:, i0 * 128:i0 * 128 + W], in1=E[:, 0:W], op=ALU.mult),
                     reads=[gk_tm_k, E_k], wpart=[kds_k[dr]])
            res = []
            for dr in range(2):
                pb, pb_k = pbs[dr]
                eb, eb_k = eb_r.next()
                enb, enb_k = enb_r.next()
                S.op("act", lambda e, eb=eb, pb=pb: e.activation(out=eb[:, 0:W], in_=pb[:, 0:W], func=AF.Exp, scale=-1.0 / 16), reads=[pb_k], writes=[eb_k])
                if nt == 4:
                    S.op("act", lambda e, enb=enb, pb=pb: e.activation(out=enb[:, 0:W], in_=pb[:, 0:W], func=AF.Exp, scale=1.0 / 16), reads=[pb_k], writes=[enb_k])
                col = 127 if dr == 0 else 0
                S.op("dve", lambda e, eb=eb, dr=dr, col=col: e.tensor_copy(out=dlx[:, dr, i0:i0 + nt], in_=eb[:, 0:W].rearrange("p (k t) -> p k t", k=nt)[:, :, col]),
                     reads=[eb_k], wpart=[dlx_k[dr]])
                res.append((eb, eb_k, enb, enb_k))
            return res

        for h in range(4):
            wq, wq_k = take("gq%d" % h, lambda: load_wb(wb_d[WB_GQK + 2 * h]))
            wk, wk_k = take("gk%d" % h, lambda: load_wb(wb_d[WB_GQK + 2 * h + 1]))
            wa, wa_k = take("ga%d" % h, lambda: load_wa(wag_d[h], 640))
            for i0 in range(0, 18, 4):
                tl = list(range(i0, min(18, i0 + 4)))
                ps, ps_k = bank()

                def mmk(e, ps=ps, tl=tl):
                    last = None
                    for k, i in enumerate(tl):
                        for kt in range(KT):
                            last = e.matmul(ps[:, k * 128:(k + 1) * 128], lhsT=hsrc(i, kt)[0], rhs=wa[:, kt, 0:128], start=(kt == 0), stop=(kt == KT - 1))
                    return last
                S.op("pe", mmk, reads=[wa_k] + list({id(hsrc(i, 0)[1]): hsrc(i, 0)[1] for i in tl}.values()), writes=[ps_k])
                S.op("act", lambda e, ps=ps, tl=tl: e.copy(out=gk_tm[:, tl[0] * 128:(tl[-1] + 1) * 128], in_=ps[:, 0:len(tl) * 128]), reads=[ps_k], wpart=[gk_tm_k])
            bProj = Ring(banks[0:4]); bDP = Ring(banks[4:8])

            def tm_vg(i0, do_v=True, do_g=True):
                if do_v:
                    tm_v(i0)
                if do_g and i0 >= 2:
                    tm_g(i0)

            def tm_v(i0):
                ps, ps_k = bDP.next()

                def mmv(e, ps=ps):
                    last = None
                    for k in range(2):
                        for kt in range(KT):
                            last = e.matmul(ps[:, k * 256:(k + 1) * 256], lhsT=hsrc(i0 + k, kt)[0], rhs=wa[:, kt, 128:384], start=(kt == 0), stop=(kt == KT - 1))
                    return last
                S.op("pe", mmv, reads=[wa_k, hsrc(i0, 0)[1]], writes=[ps_k])
                S.op("dve", lambda e, ps=ps: e.tensor_copy(out=gv[:, i0:i0 + 2, :], in_=ps.rearrange("p (a b) -> p a b", a=2)), reads=[ps_k], wpart=[gv_k])

            def tm_g(i0):
                if True:
                    ps2, ps2_k = bDP.next()

                    def mmg(e, ps2=ps2):
                        last = None
                        for k in range(2):
                            for kt in range(KT):
                                last = e.matmul(ps2[:, k * 256:(k + 1) * 256], lhsT=hsrc(i0 + k, kt)[0], rhs=wa[:, kt, 384:640], start=(kt == 0), stop=(kt == KT - 1))
                        return last
                    S.op("pe", mmg, reads=[wa_k, hsrc(i0, 0)[1]], writes=[ps2_k])
                    hl, hl_k = hl_r.next()
                    tmp = hl.rearrange("p a b -> p (a b)").bitcast(F32)
                    S.op("act", lambda e: e.activation(out=tmp, in_=ps2, func=AF.Silu), reads=[ps2_k], writes=[hl_k])
                    S.op("dve", lambda e: e.tensor_tensor(out=sgg[:, i0 - 2:i0, :], in0=tmp.rearrange("p (a b) -> p a b", a=2),
                                                          in1=gn2, op=ALU.mult), reads=[hl_k, small_k], wpart=[sgg_k])
            tm_vg(0)
            if stop_after <= 1.2:
                dump("gk_tm", gk_tm, [gk_tm_k, gv_k, sgg_k], [128, 18 * 128], BF16)
                break
            decay_pair(h, [0, 1], bDP.next)

            def proj_qk(c):
                pq, pq_k = bProj.next()
                pk, pk_k = bProj.next()
                for (w, w_k, pp, pp_k) in ((wq, wq_k, pq, pq_k), (wk, wk_k, pk, pk_k)):
                    def mm(e, pp=pp, w=w):
                        last = None
                        for kt in range(KT):
                            last = e.matmul(pp, lhsT=w[:, kt, :], rhs=hxT[:, kt, c * 512:(c + 1) * 512], start=(kt == 0), stop=(kt == KT - 1))
                        return last
                    S.op("pe", mm, reads=[w_k, hxT_k[c]], writes=[pp_k])
                return pq, pq_k, pk, pk_k

            def qk_scale(c, P, ebs):
                pq, pq_k, pk, pk_k = P
                kes = []
                for dr in range(2):
                    eb, eb_k, enb, enb_k = ebs[dr]
                    S.op("dve", lambda e, dr=dr, eb=eb: e.scalar_tensor_tensor(out=qeT[:, dr, c * 512:(c + 1) * 512], in0=pq, scalar=GLA_QSCALE, in1=eb, op0=ALU.mult, op1=ALU.mult),
                         reads=[pq_k, eb_k], wpart=[qeT_k[dr]])
                    keT, keT_k = keT_r.next()
                    S.op("dve", lambda e, keT=keT, enb=enb: e.tensor_tensor(out=keT, in0=pk, in1=enb, op=ALU.mult), reads=[pk_k, enb_k], writes=[keT_k])
                    kes.append((keT, keT_k))
                return kes

            def amat(c, kes):
                for dr in range(2):
                    keT, keT_k = kes[dr]
                    pa, pa_k = bDP.next()

                    def mma(e, pa=pa, keT=keT, dr=dr):
                        last = None
                        for k in range(4):
                            t0 = c * 512 + k * 128
                            last = e.matmul(pa[:, k * 128:(k + 1) * 128], lhsT=keT[:, k * 128:(k + 1) * 128], rhs=qeT[:, dr, t0:t0 + 128], start=True, stop=True)
                        return last
                    S.op("pe", mma, reads=[keT_k, qeT_k[dr]], writes=[pa_k])
                    S.op("dve", lambda e, pa=pa, dr=dr: e.tensor_tensor(out=ATs[:, dr, c * 512:(c + 1) * 512], in0=pa, in1=mask4[:, dr, :], op=ALU.mult),
                         reads=[pa_k, cst_k], wpart=[ATs_k[dr]])
            def tm_group(c):
                def f():
                    tm_vg(2 + 4 * c, do_g=False)
                    tm_vg(4 + 4 * c, do_g=False)
                    if c == 0:
                        for i0 in range(2, 18, 2):
                            tm_vg(i0, do_v=False)
                return f
            P = proj_qk(0)
            ebs = decay_pair(h, [2, 3, 4, 5], bDP.next, mid=tm_group(0))
            for c in range(4):
                kes = qk_scale(c, P, ebs)
                if c + 1 < 4:
                    P = proj_qk(c + 1)
                    ebs = decay_pair(h, [2 + 4 * (c + 1) + k for k in range(4)], bDP.next, mid=tm_group(c + 1))
                amat(c, kes)
            bU = Ring(banks[0:2]); bOo = Ring(banks[2:5]); bT = Ring(banks[5:7])

            cur = {0: 0, 1: 0}

            def stbuf(dr):
                return (S32p if dr == 0 else Sb32p)[cur[dr]]

            def state_step(i, dr, first):
                pu, pu_k = bU.next()
                S.op("pe", lambda e: e.matmul(pu[:, 0:256], lhsT=kds[:, dr, i * 128:(i + 1) * 128], rhs=gv[:, i, :], start=True, stop=True),
                     reads=[kds_k[dr], gv_k], writes=[pu_k])
                old, old_k = stbuf(dr)
                cur[dr] ^= 1
                new, new_k = stbuf(dr)
                if first:
                    S.op("dve", lambda e: e.tensor_copy(out=new, in_=pu[:, 0:256]), reads=[pu_k], writes=[new_k])
                else:
                    S.op("dve", lambda e: e.scalar_tensor_tensor(out=new, in0=old, scalar=dlx[:, dr, i:i + 1], in1=pu[:, 0:256], op0=ALU.mult, op1=ALU.add),
                         reads=[pu_k, dlx_k[dr], old_k], writes=[new_k])
            for step, i in enumerate((0, 1)):
                state_step(i, 0, first=(step == 0))
            for step, i in enumerate((1, 0)):
                state_step(i, 1, first=(step == 0))
            if b == 0 and h == 0:
                dump("sf", stbuf(0)[0], stbuf(0)[1], [128, 256])
                dump("sb", stbuf(1)[0], stbuf(1)[1], [128, 256])
            for n in range(15, -1, -1):
                sbuf_, sbuf_k = stbuf(1)
                S.op("act", lambda e, n=n, sbuf_=sbuf_: e.copy(out=SBst[:, n, :], in_=sbuf_), reads=[sbuf_k], wpart=[SBst_k])
                if n > 0:
                    state_step(n + 2, 1, first=False)
            live = {}

            def core(n):
                i = n + 2
                Sbf, Sbf_k = Sbf_r.next()
                s32_, s32_k = stbuf(0)
                S.op("act", lambda e: e.copy(out=Sbf, in_=s32_), reads=[s32_k], writes=[Sbf_k])
                pso, pso_k = bOo.next()
                tok = slice(n * 128, (n + 1) * 128)

                def mmo(e):
                    e.matmul(pso[:, 0:256], lhsT=qeT[:, 0, tok], rhs=Sbf, start=True, stop=False)
                    e.matmul(pso[:, 0:256], lhsT=ATs[:, 0, tok], rhs=gv[:, i, :], start=False, stop=False)
                    e.matmul(pso[:, 0:256], lhsT=qeT[:, 1, tok], rhs=SBst[:, n, :], start=False, stop=False)
                    return e.matmul(pso[:, 0:256], lhsT=ATs[:, 1, tok], rhs=gv[:, i, :], start=False, stop=True)
                if n < 15:
                    state_step(i, 0, first=False)
                S.op("pe", mmo, reads=[qeT_k[0], qeT_k[1], ATs_k[0], ATs_k[1], Sbf_k, SBst_k, gv_k], writes=[pso_k])
                live[n] = [pso, pso_k]

            def norm(n):
                pso, pso_k = live[n]
                stt, stt_k = st_r.next()
                S.op("act", lambda e: e.activation(out=junk2, in_=pso[:, 0:256], func=AF.Square, accum_out=stt[:, 1:2]), reads=[pso_k], writes=[junk2_k, stt_k])
                S.op("act", lambda e: e.activation(out=stt[:, 2:3], in_=stt[:, 1:2], func=AF.Ln, scale=1.0 / 256, bias=EPS), reads=[stt_k], writes=[stt_k])
                S.op("act", lambda e: e.activation(out=stt[:, 3:4], in_=stt[:, 2:3], func=AF.Exp, scale=-0.5), reads=[stt_k], writes=[stt_k])
                go, go_k = go_r.next()
                S.op("dve", lambda e: e.scalar_tensor_tensor(out=go, in0=pso[:, 0:256], scalar=stt[:, 3:4], in1=sgg[:, n, :], op0=ALU.mult, op1=ALU.mult),
                     reads=[pso_k, stt_k, sgg_k], writes=[go_k])
                pst, pst_k = bT.next()

                def trs(e):
                    e.transpose(pst[:, 0:128], go[:, 0:128], ident)
                    return e.transpose(pst[:, 128:256], go[:, 128:256], ident)
                S.op("pe", trs, reads=[go_k, cst_k], writes=[pst_k])
                live[n] = [pst, pst_k]

            def evac(n):
                pst, pst_k = live.pop(n)
                tok = slice(n * 128, (n + 1) * 128)
                S.op("dve", lambda e: e.tensor_copy(out=goT[:, 2 * h:2 * h + 2, tok], in_=pst[:, 0:256].rearrange("p (a b) -> p a b", a=2)),
                     reads=[pst_k], wpart=[goT_k[h]])
            for n in range(18):
                if n < 16:
                    core(n)
                if 0 <= n - 1 < 16:
                    norm(n - 1)
                if 0 <= n - 2 < 16:
                    evac(n - 2)
        if b == 0 and stop_after >= 2:
            dump("goT", goT.rearrange("p a b -> p (a b)"), goT_k, [128, KT * T], BF16)
        if stop_after <= 2:
            break
        pre["k0"] = load_wb(wb_d[WB_K + 0])
        pre["k1"] = load_wb(wb_d[WB_K + 1])
        pre["wv"] = load_wa(wav_d, 256)
        S.barrier()

        off = 0
        kT, off = carve(arenaB, off, [128, 2, T + TC], BF16); kT_k = [Tk("kT0"), Tk("kT1")]
        Vt, off = carve(arenaB, off, [128, 18, 256], BF16); V_k = Tk("V")
        CS, off = carve(arenaB, off, [128, T]); SSa, off = carve(arenaB, off, [128, T]); cs_k = Tk("cs")
        sq_r = ring(2, [128, 512], BF16, "sq")
        ln_r = ring(2, [128, 512], F32, "ln")
        rr_r = ring(2, [128, 512], F32, "rr")
        t1_r = ring(2, [128, 512], F32, "t1")
        t2_r = ring(2, [128, 512], F32, "t2")
        qT_r = ring(2, [128, 512], BF16, "qT")
        PT_r = ring(4, [128, 512], BF16, "PT")
        rz_r = ring(2, [128, 512], F32, "rz")
        assert off <= 16384, off
        S.dma("sp", CS, cs_d, owner=cs_k, writes=[cs_k])
        S.dma("sp", SSa, ssa_d, owner=cs_k, writes=[cs_k])
        bS = Ring(banks[0:3]); bO = Ring(banks[3:5]); bZ = Ring(banks[5:7]); bM = Ring(banks[7:8])

        def proj_b(w, w_k, c5, bring=None):
            ps, ps_k = (bring or bM).next()
            nn = TC if c5 == 0 else 512

            def mm(e):
                last = None
                for kt in range(KT):
                    rhs = hcT[:, kt, :] if c5 == 0 else hxT[:, kt, (c5 - 1) * 512:c5 * 512]
                    last = e.matmul(ps[:, 0:nn], lhsT=w[:, kt, :], rhs=rhs, start=(kt == 0), stop=(kt == KT - 1))
                return last
            S.op("pe", mm, reads=[w_k, hcT_k if c5 == 0 else hxT_k[c5 - 1]], writes=[ps_k])
            return ps, ps_k, nn

        def norm_rope_steps(ps, ps_k, nn, gi, tok0, dst, dst_k):
            g = qg[:, gi:gi + 1]
            sq, sq_k = sq_r.next()
            t1, t1_k = t1_r.next()
            lnv, lnv_k = ln_r.next()
            rr, rr_k = rr_r.next()
            steps = []
            steps.append(lambda: S.op("act", lambda e: e.activation(out=sq[:, 0:nn], in_=ps[:, 0:nn], func=AF.Square), reads=[ps_k], writes=[sq_k]))
            if tok0 is None:
                steps.append(lambda: S.op("dve", lambda e: e.tensor_scalar(out=t1[:, 0:nn], in0=ps[:, 0:nn], scalar1=g, scalar2=None, op0=ALU.mult),
                                          reads=[ps_k, small_k], writes=[t1_k]))
            else:
                tok = slice(tok0, tok0 + nn)
                t2, t2_k = t2_r.next()
                steps.append(lambda: S.op("dve", lambda e: e.scalar_tensor_tensor(out=t1[:, 0:nn], in0=ps[:, 0:nn], scalar=g, in1=CS[:, tok], op0=ALU.mult, op1=ALU.mult),
                                          reads=[ps_k, small_k, cs_k], writes=[t1_k]))
                steps.append(lambda: S.op("dve", lambda e: e.scalar_tensor_tensor(out=t2[0:64, 0:nn], in0=ps[64:128, 0:nn], scalar=qg[64:128, gi:gi + 1], in1=SSa[64:128, tok],
                                                                                  op0=ALU.mult, op1=ALU.mult), reads=[ps_k, small_k, cs_k], writes=[t2_k]))
                steps.append(lambda: S.op("dve", lambda e: e.scalar_tensor_tensor(out=t2[64:128, 0:nn], in0=ps[0:64, 0:nn], scalar=qg[0:64, gi:gi + 1], in1=SSa[0:64, tok],
                                                                                  op0=ALU.mult, op1=ALU.mult), reads=[ps_k, small_k, cs_k], writes=[t2_k]))
                steps.append(lambda: S.op("dve", lambda e: e.tensor_tensor(out=t1[:, 0:nn], in0=t1[:, 0:nn], in1=t2[:, 0:nn], op=ALU.add), reads=[t1_k, t2_k], writes=[t1_k]))
            steps.append(lambda: S.op("pe", lambda e: e.matmul(ps[:, 0:nn], lhsT=ones_bf, rhs=sq[:, 0:nn], start=True, stop=True), reads=[sq_k, ones_k], writes=[ps_k]))
            steps.append(lambda: S.op("act", lambda e: e.activation(out=lnv[:, 0:nn], in_=ps[:, 0:nn], func=AF.Ln, scale=1.0 / 128, bias=EPS), reads=[ps_k], writes=[lnv_k]))
            steps.append(lambda: S.op("act", lambda e: e.activation(out=rr[:, 0:nn], in_=lnv[:, 0:nn], func=AF.Exp, scale=-0.5), reads=[lnv_k], writes=[rr_k]))
            steps.append(lambda: S.op("dve", lambda e: e.tensor_tensor(out=dst, in0=t1[:, 0:nn], in1=rr[:, 0:nn], op=ALU.mult), reads=[t1_k, rr_k], writes=[dst_k]))
            return steps

        def norm_rope(ps, ps_k, nn, gi, tok0, dst, dst_k):
            for st_ in norm_rope_steps(ps, ps_k, nn, gi, tok0, dst, dst_k):
                st_()

        bMp = Ring(banks[3:8])
        klists = []
        for g in range(2):
            w, w_k = take("k%d" % g, lambda: load_wb(wb_d[WB_K + g]))
            for c5 in range(5):
                def mk(g=g, c5=c5, w=w, w_k=w_k):
                    box = {}

                    def first():
                        ps, ps_k, nn = proj_b(w, w_k, c5, bMp)
                        n0 = 0 if c5 == 0 else TC + (c5 - 1) * 512
                        box["steps"] = norm_rope_steps(ps, ps_k, nn, 1, None if c5 == 0 else (c5 - 1) * 512, kT[:, g, n0:n0 + nn], kT_k[g])
                    nsteps = 6 if c5 == 0 else 9
                    return [first] + [(lambda i=i: box["steps"][i]()) for i in range(nsteps)]
                klists.append(mk())
        wv, wv_k = take("wv", lambda: load_wa(wav_d, 256))
        v_todo = list(range(18))

        def v_tile(i):
            ps, ps_k = bS.next()

            def mm(e, ps=ps, i=i):
                last = None
                for kt in range(KT):
                    last = e.matmul(ps[:, 0:256], lhsT=hsrc(i, kt)[0], rhs=wv[:, kt, 0:256], start=(kt == 0), stop=(kt == KT - 1))
                return last
            S.op("pe", mm, reads=[wv_k, hsrc(i, 0)[1]], writes=[ps_k])
            S.op("act" if i % 2 else "dve", (lambda e, ps=ps, i=i: e.copy(out=Vt[:, i, :], in_=ps[:, 0:256])) if i % 2 else
                 (lambda e, ps=ps, i=i: e.tensor_copy(out=Vt[:, i, :], in_=ps[:, 0:256])), reads=[ps_k], wpart=[V_k])
        for p in range(0, len(klists), 2):
            la, lb = klists[p], klists[p + 1]
            for i in range(max(len(la), len(lb))):
                if i < len(la):
                    la[i]()
                if i < len(lb):
                    lb[i]()
                if v_todo and i % 2 == 1:
                    v_tile(v_todo.pop(0))
        while v_todo:
            v_tile(v_todo.pop(0))
        if b == 0:
            dump("kT", kT.rearrange("p a b -> p (a b)"), kT_k, [128, 2 * (T + TC)], BF16)
            dump("V", Vt.rearrange("p a b -> p (a b)"), V_k, [128, 18 * 256], BF16)

        def q_prep_steps(h, c):
            qT, qT_k = qT_r.next()
            box = {}

            def first():
                w, w_k = get_wq(h)
                ps, ps_k, nn = proj_b(w, w_k, c + 1)
                box["steps"] = norm_rope_steps(ps, ps_k, 512, 0, c * 512, qT, qT_k)
            steps = [first] + [(lambda i=i: box["steps"][i]()) for i in range(9)]
            return steps, (qT, qT_k)

        def attend(h, c, qT, qT_k, side=(), pre_qk=None, nxt=None):
            g = h // 4
            side = list(side)
            pO, pO_k = bO.next()
            pZ, pZ_k = bZ.next()
            LOOK = 2
            pSs = dict(pre_qk or {})
            nxt_store = {}

            def qk_for(gg, qTx, qTx_k, j, store):
                pS, pS_k = bS.next()
                S.op("pe", lambda e: e.matmul(pS, lhsT=kT[:, gg, j * 128:(j + 1) * 128], rhs=qTx, start=True, stop=True),
                     reads=[kT_k[gg], qTx_k], writes=[pS_k])
                store[j] = (pS, pS_k)
            for j in range(LOOK):
                if j not in pSs:
                    qk_for(g, qT, qT_k, j, pSs)
            for j in range(18):
                if j + LOOK < 18:
                    qk_for(g, qT, qT_k, j + LOOK, pSs)
                elif nxt is not None:
                    assert not side
                    qk_for(nxt[0] // 4, nxt[1], nxt[2], j + LOOK - 18, nxt_store)
                if j in (1, 2, 3, 4, 6, 8, 10, 12, 13, 14) and side:
                    side.pop(0)()
                pS, pS_k = pSs.pop(j)
                PT, PT_k = PT_r.next()
                S.op("act", lambda e, pS=pS, PT=PT: e.activation(out=PT, in_=pS, func=AF.Exp, scale=ATT_SCALE, bias=-ATT_SHIFT),
                     reads=[pS_k], writes=[PT_k])

                def pv(e, PT=PT, j=j):
                    e.matmul(pO, lhsT=Vt[:, j, g * 128:(g + 1) * 128], rhs=PT, start=(j == 0), stop=(j == 17))
                    return e.matmul(pZ, lhsT=ones_bf, rhs=PT, start=(j == 0), stop=(j == 17))
                S.op("pe", pv, reads=[V_k, PT_k, ones_k], writes=[pO_k, pZ_k])
            assert not side
            rz, rz_k = rz_r.next()
            S.op("dve", lambda e: e.reciprocal(out=rz, in_=pZ), reads=[pZ_k], writes=[rz_k])
            S.op("dve", lambda e: e.tensor_tensor(out=aoT[:, h, c * 512:(c + 1) * 512], in0=pO, in1=rz, op=ALU.mult),
                 reads=[pO_k, rz_k], writes=[aoT_k[h][c]])
            return nxt_store

        tiles = [(h, c) for h in range(8) for c in range(4)]
        wq_cur = {}

        def get_wq(h):
            if h not in wq_cur:
                wq_cur[h] = load_wb(wb_d[WB_Q + h])
            return wq_cur[h]
        steps0, pend = q_prep_steps(0, 0)
        for st_ in steps0:
            st_()
        pre_qk = None
        for ti, (h, c) in enumerate(tiles):
            side, nxt, nxt_arg = (), None, None
            if ti + 1 < len(tiles):
                side, nxt = q_prep_steps(*tiles[ti + 1])
                nxt_arg = (tiles[ti + 1][0], nxt[0], nxt[1])
            pre_qk = attend(h, c, *pend, side=side, pre_qk=pre_qk, nxt=nxt_arg)
            pend = nxt
        if b == 0:
            dump("aoT", aoT.rearrange("p a b -> p (a b)"), [k for row in aoT_k for k in row], [128, KT * T], BF16)
        if stop_after <= 3:
            break
        pre["mg0_0"] = load_wb(wb_d[WB_GA + 0])
        pre["my0_0"] = load_wb(wap_d[0])
        S.barrier()

        off = 0
        mT, off = carve(arenaB, off, [128, KT, T], BF16); mT_k = [Tk("mT%d" % c) for c in range(4)]
        sg_r = ring(2, [128, 512], F32, "sg")
        ma_r = ring(5, [128, 512], F32, "ma")
        assert off <= 16384, off
        bG = Ring(banks[0:4]); bY = Ring(banks[4:8])
        for f in range(8):
            mas = []
            for half in range(2):
                wg, wg_k = take("mg%d_%d" % (f, half), lambda: load_wb(wb_d[(WB_GA if half == 0 else WB_GB) + f]))
                wy, wy_k = take("my%d_%d" % (f, half), lambda: load_wb(wap_d[f] if half == 0 else wgp_d[f]))
                src = aoT if half == 0 else goT
                for c in range(4):
                    pG, pG_k = bG.next()
                    pY, pY_k = bY.next()

                    def mmg(e, pG=pG, c=c, wg=wg):
                        last = None
                        for kt in range(KT):
                            last = e.matmul(pG, lhsT=wg[:, kt, :], rhs=hxT[:, kt, c * 512:(c + 1) * 512], start=(kt == 0), stop=(kt == KT - 1))
                        return last
                    S.op("pe", mmg, reads=[wg_k, hxT_k[c]], writes=[pG_k])

                    def mmy(e, pY=pY, c=c, wy=wy, src=src):
                        last = None
                        for kt in range(KT):
                            last = e.matmul(pY, lhsT=wy[:, kt, :], rhs=src[:, kt, c * 512:(c + 1) * 512], start=(kt == 0), stop=(kt == KT - 1))
                        return last
                    srck = [aoT_k[hh][c] for hh in range(8)] if half == 0 else goT_k
                    S.op("pe", mmy, reads=[wy_k] + list(srck), writes=[pY_k])
                    sg, sg_k = sg_r.next()
                    S.op("act", lambda e, pG=pG, sg=sg: e.activation(out=sg, in_=pG, func=AF.Sigmoid), reads=[pG_k], writes=[sg_k])
                    if half == 0:
                        ma, ma_k = ma_r.next()
                        S.op("dve", lambda e, pY=pY, sg=sg, ma=ma: e.tensor_tensor(out=ma, in0=pY, in1=sg, op=ALU.mult), reads=[pY_k, sg_k], writes=[ma_k])
                        mas.append((ma, ma_k))
                    else:
                        ma, ma_k = mas[c]
                        mb, mb_k = ma_r.next()
                        S.op("dve", lambda e, pY=pY, sg=sg, mb=mb: e.tensor_tensor(out=mb, in0=pY, in1=sg, op=ALU.mult), reads=[pY_k, sg_k], writes=[mb_k])
                        S.op("dve", lambda e, ma=ma, mb=mb, f=f, c=c: e.tensor_tensor(out=mT[:, f, c * 512:(c + 1) * 512], in0=ma, in1=mb, op=ALU.add),
                             reads=[ma_k, mb_k], wpart=[mT_k[c]])
        if b == 0:
            dump("mT", mT.rearrange("p a b -> p (a b)"), mT_k, [128, KT * T], BF16)
        if stop_after <= 4:
            break
        pre["wo0"] = load_wa(wo_d[:, :, 0:512], 512)
        S.barrier()

        x1d_k = [Tk("x1d%d" % i) for i in range(16)]
        offA = 0
        wo_hi, offA = carve(arenaA, offA, [128, KT, 512], BF16); wo_k = Tk("wo")
        xl_items = []
        for i in range(4):
            ap, offA = carve(arenaA, offA, [128, D])
            xl_items.append((ap, Tk("xl%d" % i)))
        xl_r = Ring(xl_items)
        x1_items = []
        for i in range(8):
            ap, offA = carve(arenaA, offA, [128, D])
            x1_items.append((ap, Tk("x1t%d" % i)))
        x1_r = Ring(x1_items)
        assert offA <= 16384, offA
        off = 8192
        xn_r = ring(8, [128, D], BF16, "xn5")
        junk5, off = carve(arenaB, off, [128, D], BF16); junk5_k = Tk("junk5")
        st5_r = ring(3, [128, 12], F32, "st5")
        assert off <= 16384, off
        wo0, wo0_k = pre.pop("wo0")
        S.dma("pool", wo_hi, wo_d[:, :, 512:1024], owner=wo_k, writes=[wo_k])
        bY5 = Ring(banks[0:4])

        def make_x1(i, ap, tk):
            xl, xl_k = xl_r.next()
            S.dma("sp", xl, x_d[b, i * 128:(i + 1) * 128, :], owner=xl_k, writes=[xl_k])
            for cc in range(2):
                ps, ps_k = bY5.next()

                def mm(e, ps=ps, cc=cc):
                    last = None
                    for kt in range(KT):
                        rhs = wo0[:, kt, 0:512] if cc == 0 else wo_hi[:, kt, :]
                        last = e.matmul(ps, lhsT=mT[:, kt, i * 128:(i + 1) * 128], rhs=rhs, start=(kt == 0), stop=(kt == KT - 1))
                    return last
                S.op("pe", mm, reads=[mT_k[i // 4], wo0_k if cc == 0 else wo_k], writes=[ps_k])
                S.op("dve", lambda e, ps=ps, cc=cc: e.tensor_tensor(out=ap[:, cc * 512:(cc + 1) * 512], in0=ps, in1=G12[:, b, 0, cc * 512:(cc + 1) * 512], op=ALU.mult),
                     reads=[ps_k, G12_k], wpart=[tk])
                S.op("pool", lambda e, cc=cc: e.tensor_tensor(out=ap[:, cc * 512:(cc + 1) * 512], in0=ap[:, cc * 512:(cc + 1) * 512], in1=xl[:, cc * 512:(cc + 1) * 512], op=ALU.add),
                     reads=[xl_k, tk], wpart=[tk])
            S.dma("pool", out_d[b, i * 128:(i + 1) * 128, :], ap, owner=tk, reads=[tk], writes=[x1d_k[i]], final=True)

        def bank45():
            bnk = banks[4 + (st["bank"] % 4)]
            st["bank"] += 1
            return bnk
        st["bankfn"] = bank45
        norm_transpose(make_x1, 16, hxT, hxT_k, lambda kt: sc2e[:, kt, b:b + 1], lambda kt: sh2T[:, kt, b:b + 1],
                       x1_r, xn_r, junk5, junk5_k, st5_r, act_kts=(0, 1, 2, 3, 4, 5, 6, 7))
        st["bankfn"] = bank
        if b == 0:
            dump("h2T", hxT.rearrange("p a b -> p (a b)"), hxT_k, [128, KT * T], BF16)
        if stop_after <= 5:
            break
        pre["fa0_0"] = load_wb(wfi_d[0])
        pre["fb0_0"] = load_wb(wfi_d[NJ + 0])
        S.barrier()

        offA = 0
        uT, offA = carve(arenaA, offA, [128, NJ, 1024], BF16); uT_k = [Tk("uT0"), Tk("uT1")]
        xr6_items, stg_items = [], []
        for i in range(2):
            ap, offA = carve(arenaA, offA, [128, D]); xr6_items.append((ap, Tk("xr6_%d" % i)))
        for i in range(2):
            ap, offA = carve(arenaA, offA, [128, D]); stg_items.append((ap, Tk("stg%d" % i)))
        xr6_r, stg_r = Ring(xr6_items), Ring(stg_items)
        assert offA <= 16384, offA
        off = 0
        WoG, off = carve(arenaB, off, [128, NJ, D], BF16); WoG_k = Tk("WoG")
        sa_r = ring(2, [128, 512], F32, "sa")
        wf_r = ring(2, [128, D], BF16, "wf")
        assert off <= 16384, off
        WoG_kj = [Tk("WoG%d" % j) for j in range(2)]

        def load_wog(piece):
            j2 = piece // 6
            S.dma("pool", WoG[:, 2 * piece:2 * piece + 2, :], wfo_d[2 * piece:2 * piece + 2].rearrange("j p n -> p j n"), owner=WoG_kj[j2], writes=[])
            WoG_kj[j2].w = (WoG_kj[j2].dsem, WoG_kj[j2].dcnt)
        bA = Ring(banks[0:2]); bB = Ring(banks[2:4]); bF = Ring(banks[4:8])
        for H in range(2):
            for j in range(NJ):
                wa_j, wa_jk = take("fa%d_%d" % (j, H), lambda: load_wb(wfi_d[j]))
                wb_j, wb_jk = take("fb%d_%d" % (j, H), lambda: load_wb(wfi_d[NJ + j]))
                if H == 0 and 1 <= j <= 11:
                    load_wog(j - 1)
                for cl in range(2):
                    c = 2 * H + cl
                    pA, pA_k = bA.next()
                    pB, pB_k = bB.next()

                    def mma(e, pA=pA, c=c, wa_j=wa_j):
                        last = None
                        for kt in range(KT):
                            last = e.matmul(pA, lhsT=wa_j[:, kt, :], rhs=hxT[:, kt, c * 512:(c + 1) * 512], start=(kt == 0), stop=(kt == KT - 1))
                        return last
                    S.op("pe", mma, reads=[wa_jk, hxT_k[c]], writes=[pA_k])

                    def mmb(e, pB=pB, c=c, wb_j=wb_j):
                        last = None
                        for kt in range(KT):
                            last = e.matmul(pB, lhsT=wb_j[:, kt, :], rhs=hxT[:, kt, c * 512:(c + 1) * 512], start=(kt == 0), stop=(kt == KT - 1))
                        return last
                    S.op("pe", mmb, reads=[wb_jk, hxT_k[c]], writes=[pB_k])
                    sa, sa_k = sa_r.next()
                    S.op("act", lambda e, pA=pA, sa=sa: e.activation(out=sa, in_=pA, func=AF.Silu), reads=[pA_k], writes=[sa_k])
                    S.op("dve", lambda e, pB=pB, sa=sa, j=j, cl=cl: e.tensor_tensor(out=uT[:, j, cl * 512:(cl + 1) * 512], in0=pB, in1=sa, op=ALU.mult),
                         reads=[pB_k, sa_k], wpart=[uT_k[cl]])
            if H == 0:
                for j_ in range(2):
                    pre["fa%d_1" % j_] = load_wb(wfi_d[j_])
                    pre["fb%d_1" % j_] = load_wb(wfi_d[NJ + j_])
            for tl in range(8):
                i = H * 8 + tl
                xr6, xr6_k = xr6_r.next()
                S.dma("sp", xr6, out_d[b, i * 128:(i + 1) * 128, :], owner=xr6_k, reads=[x1d_k[i]], writes=[xr6_k])
                stg, stg_k = stg_r.next()
                for cc in range(2):
                    pF, pF_k = bF.next()

                    def mmf(e, pF=pF, tl=tl, cc=cc):
                        last = None
                        for j in range(NJ):
                            last = e.matmul(pF, lhsT=uT[:, j, tl * 128:(tl + 1) * 128], rhs=WoG[:, j, cc * 512:(cc + 1) * 512], start=(j == 0), stop=(j == NJ - 1))
                        return last
                    S.op("pe", mmf, reads=[uT_k[tl // 4]] + WoG_kj, writes=[pF_k])
                    S.op("dve", lambda e, pF=pF, cc=cc, stg=stg: e.tensor_tensor(out=stg[:, cc * 512:(cc + 1) * 512], in0=pF, in1=G12[:, b, 1, cc * 512:(cc + 1) * 512], op=ALU.mult),
                         reads=[pF_k, G12_k], wpart=[stg_k])
                    S.op("dve", lambda e, cc=cc, stg=stg, xr6=xr6: e.tensor_tensor(out=stg[:, cc * 512:(cc + 1) * 512], in0=stg[:, cc * 512:(cc + 1) * 512], in1=xr6[:, cc * 512:(cc + 1) * 512], op=ALU.add),
                         reads=[xr6_k, stg_k], wpart=[stg_k])
                S.dma("pool", out_d[b, i * 128:(i + 1) * 128, :], stg, owner=stg_k, reads=[stg_k], writes=[x1d_k[i]], final=True)
        S.barrier()

    S.finish()
    return nc, dumps


_CACHE = {}


def kernel(**inputs):
    shared = prep_shared(inputs)
    if "nc" not in _CACHE:
        _CACHE["nc"] = build()[0]
    nc = _CACHE["nc"]
    in_maps = []
    for core in range(NCORES):
        m = dict(shared)
        m.update(prep_core(inputs, core))
        in_maps.append(m)
    res = run_bass_kernel_spmd(nc, in_maps, core_ids=list(range(NCORES)))
    out = np.concatenate([np.asarray(r["out"]) for r in res.results], axis=0)
    return out.astype(np.float32)
```

```python
import numpy as np
import concourse.bass as bass
import concourse.mybir as mybir
from concourse.bass_utils import run_bass_kernel_spmd

F32 = mybir.dt.float32
BF16 = mybir.dt.bfloat16
AF = mybir.ActivationFunctionType
ALU = mybir.AluOpType
AX = mybir.AxisListType

D = 1024
T = 2048
TC = 256
NB = 2
NCORES = 8
DFF = 2816
NJ = DFF // 128
EPS = 1e-6
KT = 8
ATT_SCALE = 128.0 ** -0.5
ATT_SHIFT = 30.0
GLA_QSCALE = 128.0 ** -0.5

WB_Q = 0
WB_K = 8
WB_GQK = 10
WB_GA = 18
WB_GB = 26
N_WB = 34


class Tk:
    __slots__ = ("w", "r", "dsem", "dcnt", "name", "excl", "ws")

    def __init__(self, name="", excl=False):
        self.excl = excl
        self.w = None
        self.ws = {}
        self.r = {}
        self.dsem = None
        self.dcnt = 0
        self.name = name


class _Eng:
    def __init__(self, h, sem):
        self.h = h
        self.sem = sem
        self.cnt = 0
        self.known = {}
        self.is_pe = False


class Sched:
    def __init__(self, nc):
        self.nc = nc
        self.E = {
            "pe": _Eng(nc.tensor, nc.alloc_semaphore("s_pe")),
            "act": _Eng(nc.scalar, nc.alloc_semaphore("s_act")),
            "dve": _Eng(nc.vector, nc.alloc_semaphore("s_dve")),
            "pool": _Eng(nc.gpsimd, nc.alloc_semaphore("s_pool")),
            "sp": _Eng(nc.sync, nc.alloc_semaphore("s_sp")),
        }
        self.E["pe"].is_pe = True
        self.nsem = 5
        self.dsems = []
        self.final = {}

    def _wait(self, E, reads, writes, wpart=()):
        need = {}

        def add(sp):
            if sp is None:
                return
            k = id(sp[0])
            if k not in need or need[k][1] < sp[1]:
                need[k] = sp

        for t in reads:
            add(t.w)
            for sp in t.ws.values():
                add(sp)
            if t.excl:
                for sp in t.r.values():
                    add(sp)
        for t in writes:
            add(t.w)
            for sp in t.ws.values():
                add(sp)
            for sp in t.r.values():
                add(sp)
        for t in wpart:
            add(t.w)
            for sp in t.r.values():
                add(sp)
        for k, (sem, val) in need.items():
            if sem is E.sem and E.is_pe:
                continue
            if E.known.get(k, 0) >= val:
                continue
            E.h.wait_ge(sem, val)
            E.known[k] = val

    def op(self, e, fn, reads=(), writes=(), wpart=()):
        E = self.E[e]
        self._wait(E, reads, writes, wpart)
        inst = fn(E.h)
        inst.then_inc(E.sem, 1)
        E.cnt += 1
        sp = (E.sem, E.cnt)
        for t in writes:
            t.w = sp
            t.ws = {}
            t.r = {}
        for t in wpart:
            t.ws[id(E.sem)] = sp
        k = id(E.sem)
        for t in reads:
            t.r[k] = sp
        return sp

    def dma(self, q, out, in_, owner, reads=(), writes=(), final=False, **kw):
        E = self.E[q]
        self._wait(E, reads, writes)
        if owner.dsem is None:
            owner.dsem = self.nc.alloc_semaphore("s_d%d" % self.nsem)
            self.nsem += 1
            self.dsems.append(owner)
        owner.dcnt += 16
        E.h.dma_start(out=out, in_=in_, **kw).then_inc(owner.dsem, 16)
        sp = (owner.dsem, owner.dcnt)
        for t in writes:
            t.w = sp
            t.ws = {}
            t.r = {}
        k = id(owner.dsem)
        for t in reads:
            t.r[k] = sp
        if final:
            self.final[k] = sp
        return sp

    def barrier(self):
        sps = [(E.sem, E.cnt) for E in self.E.values() if E.cnt > 0]
        sps += [(o.dsem, o.dcnt) for o in self.dsems]
        for E in self.E.values():
            for sem, val in sps:
                if sem is E.sem:
                    continue
                k = id(sem)
                if E.known.get(k, 0) >= val:
                    continue
                E.h.wait_ge(sem, val)
                E.known[k] = val

    def finish(self):
        E = self.E["sp"]
        for k, (sem, val) in self.final.items():
            E.h.wait_ge(sem, val)


class Ring:
    def __init__(self, items):
        self.items = items
        self.i = 0

    def next(self):
        it = self.items[self.i % len(self.items)]
        self.i += 1
        return it


def _bstyle(w_cols):
    M = w_cols.shape[1]
    return np.ascontiguousarray(w_cols.reshape(KT, 128, M).transpose(1, 0, 2))


def _deint():
    return np.concatenate([np.arange(0, 128, 2), np.arange(1, 128, 2)])


def _rope_tables():
    rows = T // 64
    row = np.repeat(np.arange(rows, dtype=np.float64), 64)
    col = np.tile(np.arange(64, dtype=np.float64), rows)
    inv_freq = 1.0 / (10000.0 ** (np.arange(0, 64, 2, dtype=np.float64) / 64.0))
    ang = np.concatenate([row[:, None] * inv_freq[None], col[:, None] * inv_freq[None]], axis=-1)
    cos = np.cos(ang).astype(np.float32)
    sin = np.sin(ang).astype(np.float32)
    cs = np.concatenate([cos.T, cos.T], axis=0)
    ssa = np.concatenate([sin.T, -sin.T], axis=0)
    return np.ascontiguousarray(cs), np.ascontiguousarray(ssa)


def prep_shared(inp):
    f = lambda a: np.asarray(a, dtype=np.float32)
    w_in = f(inp["w_in"])[0]
    o = {}
    o["w_ada"] = _bstyle(f(inp["w_ada"])[0])
    o["b_ada"] = f(inp["b_ada"])[0].reshape(1, 6 * D)
    o["n1g"] = np.ascontiguousarray(f(inp["norm1_g"])[0].reshape(KT, 128).T)
    o["n2g"] = np.ascontiguousarray(f(inp["norm2_g"])[0].reshape(KT, 128).T)
    oq, ok, ov = 0, 1024, 1280
    ogq, ogk, ogv, ogg, olr, omg = 1536, 2048, 2560, 3584, 4608, 4640
    perm = _deint()
    wb = np.zeros((N_WB, 128, KT, 128), np.float32)
    for h in range(8):
        wb[WB_Q + h] = _bstyle(w_in[:, oq + h * 128 + perm])
    for g in range(2):
        wb[WB_K + g] = _bstyle(w_in[:, ok + g * 128 + perm])
    for h in range(4):
        wb[WB_GQK + 2 * h] = _bstyle(w_in[:, ogq + h * 128: ogq + (h + 1) * 128])
        wb[WB_GQK + 2 * h + 1] = _bstyle(w_in[:, ogk + h * 128: ogk + (h + 1) * 128])
    for t in range(16):
        wb[WB_GA + t] = _bstyle(w_in[:, omg + t * 128: omg + (t + 1) * 128])
    o["wb"] = wb
    o["wlr"] = _bstyle(w_in[:, olr: olr + 32])
    wag = np.zeros((4, 128, KT, 640), np.float32)
    for h in range(4):
        cols = np.concatenate([np.arange(ogk + h * 128, ogk + (h + 1) * 128),
                               np.arange(ogv + h * 256, ogv + (h + 1) * 256),
                               np.arange(ogg + h * 256, ogg + (h + 1) * 256)])
        wag[h] = _bstyle(w_in[:, cols])
    o["wag"] = wag
    o["wav"] = _bstyle(w_in[:, ov: ov + 256])
    qg = np.stack([f(inp["q_norm_g"])[0][perm], f(inp["k_norm_g"])[0][perm]], axis=1)
    o["qg"] = np.ascontiguousarray(qg)
    o["qgrow"] = np.concatenate([f(inp["q_norm_g"])[0], f(inp["k_norm_g"])[0]]).reshape(1, 256)
    up = np.zeros((64, 512), np.float32)
    up[0:16] = f(inp["gk_up_f"])[0]
    up[16] = f(inp["gk_up_f_b"])[0]
    up[32:48] = f(inp["gk_up_b"])[0]
    up[48] = f(inp["gk_up_b_b"])[0]
    o["up"] = up
    o["gn"] = np.ascontiguousarray(np.broadcast_to(f(inp["gla_norm_g"])[0][None, :], (128, 256)))
    wap = f(inp["w_attn_proj"])[0]
    wgp = f(inp["w_gla_proj"])[0]
    o["wap"] = np.stack([_bstyle(wap[:, t * 128:(t + 1) * 128]) for t in range(8)])
    o["wgp"] = np.stack([_bstyle(wgp[:, t * 128:(t + 1) * 128]) for t in range(8)])
    o["wo"] = _bstyle(f(inp["w_out"])[0])
    wfi = f(inp["w_ffn_in"])[0]
    o["wfi"] = np.stack([_bstyle(wfi[:, t * 128:(t + 1) * 128]) for t in range(2 * NJ)])
    o["wfo"] = np.ascontiguousarray(f(inp["w_ffn_out"])[0].reshape(NJ, 128, D))
    cs, ssa = _rope_tables()
    o["cs"] = cs
    o["ssa"] = ssa
    k = np.arange(128)
    cst = np.zeros((128, 5, 128), np.float32)
    cst[:, 0, :] = np.eye(128, dtype=np.float32)
    cst[:, 1, :] = (k[:, None] <= k[None, :])
    cst[:, 2, :] = (k[:, None] >= k[None, :])
    cst[:, 3, :] = (k[:, None] < k[None, :])
    cst[:, 4, :] = (k[:, None] > k[None, :])
    o["cst"] = cst
    sel = np.zeros((3, 3, 128), np.float32)
    for b in range(3):
        sel[b, b, :] = 1.0
    o["sel"] = sel
    return o


def prep_core(inp, core):
    f = lambda a: np.asarray(a, dtype=np.float32)
    b0 = core * NB
    o = {}
    o["x"] = np.ascontiguousarray(f(inp["x"])[b0:b0 + NB])
    o["ctx"] = np.ascontiguousarray(f(inp["ctx"])[b0:b0 + NB])
    cv = np.concatenate([f(inp["c"])[b0:b0 + NB], f(inp["c_ctx"])[None, :]], axis=0)
    o["cT"] = np.ascontiguousarray(cv.reshape(3, KT, 128).transpose(2, 1, 0))
    return o


def build(dbg=(), stop_after=99, nb=NB):
    nc = bass.Bass("TRN2", target_bir_lowering=False)
    S = Sched(nc)
    dumps = []

    def din(name, shape, dt=F32):
        return nc.dram_tensor(name, list(shape), dt, kind="ExternalInput").ap()

    x_d = din("x", [NB, T, D])
    ctx_d = din("ctx", [NB, TC, D])
    cT_d = din("cT", [128, KT, 3])
    wada_d = din("w_ada", [128, KT, 6 * D])
    bada_d = din("b_ada", [1, 6 * D])
    n1g_d = din("n1g", [128, KT])
    n2g_d = din("n2g", [128, KT])
    wb_d = din("wb", [N_WB, 128, KT, 128])
    wlr_d = din("wlr", [128, KT, 32])
    wag_d = din("wag", [4, 128, KT, 640])
    wav_d = din("wav", [128, KT, 256])
    qg_d = din("qg", [128, 2])
    qgrow_d = din("qgrow", [1, 256])
    up_d = din("up", [64, 512])
    gn_d = din("gn", [128, 256])
    wap_d = din("wap", [8, 128, KT, 128])
    wgp_d = din("wgp", [8, 128, KT, 128])
    wo_d = din("wo", [128, KT, D])
    wfi_d = din("wfi", [2 * NJ, 128, KT, 128])
    wfo_d = din("wfo", [NJ, 128, D])
    cs_d = din("cs", [128, T])
    ssa_d = din("ssa", [128, T])
    cst_d = din("cst", [128, 5, 128])
    sel_d = din("sel", [3, 3, 128])
    out_d = nc.dram_tensor("out", [NB, T, D], F32, kind="ExternalOutput").ap()

    def sb(name, shape, dt=F32):
        return nc.alloc_sbuf_tensor("sb_" + name, list(shape), dt).ap()

    def dump(name, ap, tk, shape, dt=F32):
        if name not in dbg:
            return
        d = nc.dram_tensor("dbg_" + name, list(shape), dt, kind="ExternalOutput").ap()
        tks = tk if isinstance(tk, (list, tuple)) else [tk]
        S.dma("sp", d, ap, owner=tks[0], reads=list(tks), final=True)
        dumps.append("dbg_" + name)

    banks = []
    for i in range(8):
        banks.append((nc.alloc_psum_tensor("ps%d" % i, [128, 512], F32).ap(), Tk("ps%d" % i, excl=True)))

    cst = sb("cst", [128, 5, 128]); cst_k = Tk("cst")
    ident = cst[:, 0, :]
    Uincl, Lincl, Ustrict, Lstrict = cst[:, 1, :], cst[:, 2, :], cst[:, 3, :], cst[:, 4, :]
    ones_bf = sb("ones_bf", [128, 128], BF16); ones_k = Tk("ones")
    ones_f = sb("ones_f", [128, 128]);
    gn = sb("gn", [128, 256])
    qg = sb("qg", [128, 2])
    qgrow = sb("qgrow", [1, 256])
    up = sb("up", [64, 512])
    n1g = sb("n1g", [128, KT])
    n2g = sb("n2g", [128, KT])
    sel = sb("sel", [3, 3, 128])
    negshift = sb("negshift", [128, 1])
    mask4 = sb("mask4", [128, 2, 512])
    cstb = sb("cstb", [128, 4, 128], BF16)
    ident_bf = sb("ident_bf", [128, 128], BF16)
    up_bf = sb("up_bf", [64, 512], BF16)
    gn2 = sb("gn2", [128, 2, 256])
    small_k = Tk("small")
    modT = sb("modT", [128, 32, 3])
    modT_k = Tk("modT")
    sc1e = sb("sc1e", [128, KT, 3]); sc2e = sb("sc2e", [128, KT, 3])
    G12 = sb("G12", [128, NB, 2, D], BF16); G12_k = Tk("G12")

    hxT = sb("hxT", [128, KT, T], BF16)
    hxT_k = [Tk("hxT%d" % c) for c in range(4)]
    hcT = sb("hcT", [128, KT, TC], BF16)
    hcT_k = Tk("hcT")
    arenaA = sb("arenaA", [128, 65536 // 4])
    arenaB = sb("arenaB", [128, 64 * 1024 // 4])
    wbuf = Ring([(sb("wbuf%d" % i, [128, KT, 128], BF16), Tk("wbuf%d" % i)) for i in range(4)])
    wabuf = Ring([(sb("wabuf%d" % i, [128, KT, 640], BF16), Tk("wabuf%d" % i)) for i in range(1)])

    def load_wb(src):
        ap, tk = wbuf.next()
        S.dma("pool", ap, src, owner=tk, writes=[tk])
        return ap, tk

    def load_wa(src, ncols):
        ap, tk = wabuf.next()
        S.dma("pool", ap[:, :, 0:ncols], src, owner=tk, writes=[tk])
        return ap, tk

    S.dma("sp", cst, cst_d, owner=cst_k, writes=[cst_k])
    for dst, src in ((gn, gn_d), (qg, qg_d), (qgrow, qgrow_d), (up, up_d), (n1g, n1g_d), (n2g, n2g_d), (sel, sel_d)):
        S.dma("sp", dst, src, owner=small_k, writes=[])
    small_k.w = (small_k.dsem, small_k.dcnt)
    S.op("dve", lambda e: e.memset(ones_bf, 1.0), writes=[ones_k])
    S.op("dve", lambda e: e.memset(ones_f, 1.0), writes=[ones_k])
    for dr_ in range(2):
        for k_ in range(4):
            S.op("dve", lambda e, dr_=dr_, k_=k_: e.tensor_copy(out=mask4[:, dr_, k_ * 128:(k_ + 1) * 128], in_=cst[:, 1 + dr_, :]), reads=[cst_k], writes=[cst_k])
    for k_ in range(2):
        S.op("dve", lambda e, k_=k_: e.tensor_copy(out=gn2[:, k_, :], in_=gn), reads=[small_k], writes=[small_k])
    S.op("dve", lambda e: e.tensor_copy(out=cstb, in_=cst[:, 1:5, :]), reads=[cst_k], writes=[cst_k])
    S.op("dve", lambda e: e.tensor_copy(out=ident_bf, in_=cst[:, 0, :]), reads=[cst_k], writes=[cst_k])
    S.op("dve", lambda e: e.tensor_copy(out=up_bf, in_=up), reads=[small_k], writes=[small_k])

    def carve(base_ap, off_words, shape, dt=F32):
        n = 1
        for s in shape[1:]:
            n *= s
        words = n if dt == F32 else (n + 1) // 2
        v = base_ap[:, off_words: off_words + words]
        if dt != F32:
            v = v.bitcast(dt)
        if len(shape) == 3:
            v = v.rearrange("p (a b) -> p a b", a=shape[1])
        return v[0:shape[0]] if shape[0] != 128 else v, off_words + words

    off = 0
    cT, off = carve(arenaB, off, [128, KT, 3])
    scT, off = carve(arenaB, off, [128, KT, 3])
    bada, off = carve(arenaB, off, [1, 6 * D])
    off = 0 + 24 + 24 + 6 * D
    mod_sb, off = carve(arenaB, off, [3, 6 * D])
    wa_items = []
    offw = 0
    for i_ in range(4):
        ap_, offw = carve(arenaA, offw, [128, KT, 512])
        wa_items.append((ap_, Tk("wa%d" % i_)))
    wa_ring = Ring(wa_items)
    cT_k, scT_k, bada_k, mod_k = Tk("cT"), Tk("scT"), Tk("bada"), Tk("mod")
    S.dma("sp", cT, cT_d, owner=cT_k, writes=[cT_k])
    S.dma("sp", bada, bada_d, owner=bada_k, writes=[bada_k])
    S.op("act", lambda e: e.activation(out=scT, in_=cT, func=AF.Silu), reads=[cT_k], writes=[scT_k])
    bi_ = 0
    for n in range(12):
        wa, wa_k = wa_ring.next()
        S.dma("sp", wa, wada_d[:, :, n * 512:(n + 1) * 512], owner=wa_k, writes=[wa_k])
        ps, ps_k = banks[bi_ % 8]; bi_ += 1

        def mm(e, wa=wa, ps=ps, n=n):
            for kt in range(KT):
                e.matmul(ps[0:3, :], lhsT=scT[:, kt, :], rhs=wa[:, kt, :], start=(kt == 0), stop=False)
            return e.matmul(ps[0:3, :], lhsT=ones_f[0:1, 0:3], rhs=bada[0:1, n * 512:(n + 1) * 512], start=False, stop=True)
        S.op("pe", mm, reads=[scT_k, wa_k, bada_k, ones_k], writes=[ps_k])
        S.op("dve", lambda e, ps=ps, n=n: e.tensor_copy(out=mod_sb[:, n * 512:(n + 1) * 512], in_=ps[0:3, :]),
             reads=[ps_k], writes=[mod_k])
    ps, ps_k = banks[bi_ % 8]; bi_ += 1

    def trs(e, ps=ps):
        last = None
        for gi, grp in enumerate((0, 1, 3, 4)):
            for kt in range(KT):
                j = gi * KT + kt
                c0 = grp * D + kt * 128
                last = e.transpose(ps[:, j * 3:(j + 1) * 3], mod_sb[0:3, c0:c0 + 128], ident[0:3, 0:3])
        return last
    S.op("pe", trs, reads=[mod_k, cst_k], writes=[ps_k])
    S.op("dve", lambda e, ps=ps: e.tensor_copy(out=modT.rearrange("p a b -> p (a b)"), in_=ps[:, 0:96]),
         reads=[ps_k], writes=[modT_k])
    for j in range(3):
        S.op("dve", lambda e, j=j: e.scalar_tensor_tensor(out=sc1e[:, :, j], in0=modT[:, 8:16, j], scalar=1.0, in1=n1g,
                                                           op0=ALU.add, op1=ALU.mult), reads=[modT_k, small_k], writes=[modT_k])
        S.op("dve", lambda e, j=j: e.scalar_tensor_tensor(out=sc2e[:, :, j], in0=modT[:, 24:32, j], scalar=1.0, in1=n2g,
                                                           op0=ALU.add, op1=ALU.mult), reads=[modT_k, small_k], writes=[modT_k])
    sh1T = modT[:, 0:8, :]
    sh2T = modT[:, 16:24, :]
    for b in range(NB):
        for gi, grp in enumerate((2, 5)):
            for cc in range(2):
                ps, ps_k = banks[bi_ % 8]; bi_ += 1
                c0 = grp * D + cc * 512
                S.op("pe", lambda e, ps=ps, b=b, c0=c0: e.matmul(ps, lhsT=sel[:, b, :], rhs=mod_sb[0:3, c0:c0 + 512], start=True, stop=True),
                     reads=[mod_k, small_k], writes=[ps_k])
                S.op("act", lambda e, ps=ps, b=b, gi=gi, cc=cc: e.copy(out=G12[:, b, gi, cc * 512:(cc + 1) * 512], in_=ps),
                     reads=[ps_k], writes=[G12_k])
    mx = sb("mx", [1, 4])
    S.op("dve", lambda e: e.tensor_reduce(out=mx[:, 0:1], in_=qgrow[:, 0:128], axis=AX.X, op=ALU.max, apply_absolute_value=True),
         reads=[small_k], writes=[modT_k])
    S.op("dve", lambda e: e.tensor_reduce(out=mx[:, 1:2], in_=qgrow[:, 128:256], axis=AX.X, op=ALU.max, apply_absolute_value=True),
         reads=[small_k], writes=[modT_k])
    S.op("dve", lambda e: e.scalar_tensor_tensor(out=mx[:, 2:3], in0=mx[:, 0:1], scalar=-(128.0 ** 0.5), in1=mx[:, 1:2],
                                                 op0=ALU.mult, op1=ALU.mult), reads=[modT_k], writes=[modT_k])
    ps, ps_k = banks[bi_ % 8]; bi_ += 1
    S.op("pe", lambda e, ps=ps: e.matmul(ps[:, 0:1], lhsT=ones_f[0:1, :], rhs=mx[0:1, 2:3], start=True, stop=True),
         reads=[modT_k, ones_k], writes=[ps_k])
    S.op("dve", lambda e, ps=ps: e.tensor_copy(out=negshift, in_=ps[:, 0:1]), reads=[ps_k], writes=[modT_k])
    dump("modT", modT.rearrange("p a b -> p (a b)"), modT_k, [128, 96])
    dump("G12", G12.rearrange("p a b c -> p (a b c)"), G12_k, [128, NB * 2 * D], BF16)
    dump("negshift", negshift, modT_k, [128, 1])
    S.barrier()

    st = {"bank": bi_}
    pre = {}

    def take(key, loader):
        return pre.pop(key) if key in pre else loader()

    def bank():
        b = banks[st["bank"] % 8]
        st["bank"] += 1
        return b
    st["bankfn"] = bank

    def norm_transpose(load_tile, ntiles, dstT, dst_keys, sc_col, sh_col, xr_ring, xn_ring, junk, junk_k, stat_ring, act_kts=(0, 2, 4, 6)):
        ngrp = (ntiles + 3) // 4

        def stage_n(g):
            tiles = list(range(g * 4, min(ntiles, g * 4 + 4)))
            nt = len(tiles)
            xrs = []
            for i in tiles:
                xr, xr_k = xr_ring.next()
                load_tile(i, xr, xr_k)
                xrs.append((xr, xr_k))
            stt, stt_k = stat_ring.next()
            for j, (xr, xr_k) in enumerate(xrs):
                S.op("act", lambda e, xr=xr, j=j: e.activation(out=junk, in_=xr, func=AF.Square, accum_out=stt[:, j:j + 1]),
                     reads=[xr_k], writes=[junk_k], wpart=[stt_k])
            S.op("act", lambda e: e.activation(out=stt[:, 4:4 + nt], in_=stt[:, 0:nt], func=AF.Ln, scale=1.0 / D, bias=EPS),
                 reads=[stt_k], writes=[stt_k])
            S.op("act", lambda e: e.activation(out=stt[:, 8:8 + nt], in_=stt[:, 4:4 + nt], func=AF.Exp, scale=-0.5),
                 reads=[stt_k], writes=[stt_k])
            xns = []
            for j, (xr, xr_k) in enumerate(xrs):
                xn, xn_k = xn_ring.next()
                S.op("dve", lambda e, xn=xn, xr=xr, j=j: e.tensor_scalar(out=xn, in0=xr, scalar1=stt[:, 8 + j:9 + j], scalar2=None, op0=ALU.mult),
                     reads=[xr_k, stt_k], writes=[xn_k])
                xns.append((xn, xn_k))
            return xns

        def stage_t(g, xns):
            nt = len(xns)
            for kt in range(KT):
                ps, ps_k = st['bankfn']()
                psb = ps.bitcast(BF16)

                def trs(e, psb=psb, kt=kt):
                    last = None
                    for j, (xn, _) in enumerate(xns):
                        last = e.transpose(psb[:, j * 128:(j + 1) * 128], xn[:, kt * 128:(kt + 1) * 128], ident_bf)
                    return last
                S.op("pe", trs, reads=[k for _, k in xns] + [cst_k], writes=[ps_k])
                dst = dstT[:, kt, g * 512: g * 512 + nt * 128]
                if kt in act_kts:
                    S.op("act", lambda e, psb=psb, dst=dst, kt=kt: e.activation(out=dst, in_=psb[:, 0:nt * 128], func=AF.Identity,
                                                                               scale=sc_col(kt), bias=sh_col(kt)),
                         reads=[ps_k, modT_k], wpart=[dst_keys[g]])
                else:
                    S.op("dve", lambda e, psb=psb, dst=dst, kt=kt: e.tensor_scalar(out=dst, in0=psb[:, 0:nt * 128], scalar1=sc_col(kt),
                                                                                  scalar2=sh_col(kt), op0=ALU.mult, op1=ALU.add),
                         reads=[ps_k, modT_k], wpart=[dst_keys[g]])
        cur = stage_n(0)
        for g in range(ngrp):
            nxt = stage_n(g + 1) if g + 1 < ngrp else None
            stage_t(g, cur)
            cur = nxt

    for b in range(nb):
        off = 0
        xr_items = []
        for i in range(8):
            ap, off = carve(arenaB, off, [128, D])
            xr_items.append((ap, Tk("xr%d" % i)))
        xn_items = []
        for i in range(8):
            ap, off = carve(arenaB, off, [128, D], BF16)
            xn_items.append((ap, Tk("xn%d" % i)))
        junk, off = carve(arenaB, off, [128, D], BF16)
        junk_k = Tk("junk")
        stat_items = []
        for i in range(3):
            ap, off = carve(arenaB, off, [128, 12])
            stat_items.append((ap, Tk("stat%d" % i)))
        xr_ring, xn_ring, stat_ring = Ring(xr_items), Ring(xn_items), Ring(stat_items)

        norm_transpose(lambda i, ap, tk: S.dma("sp", ap, ctx_d[b, i * 128:(i + 1) * 128, :], owner=tk, writes=[tk]),
                       2, hcT, [hcT_k], lambda kt: sc1e[:, kt, 2:3], lambda kt: sh1T[:, kt, 2:3],
                       xr_ring, xn_ring, junk, junk_k, stat_ring, act_kts=(0, 4))
        norm_transpose(lambda i, ap, tk: S.dma("sp", ap, x_d[b, i * 128:(i + 1) * 128, :], owner=tk, writes=[tk]),
                       16, hxT, hxT_k, lambda kt: sc1e[:, kt, b:b + 1], lambda kt: sh1T[:, kt, b:b + 1],
                       xr_ring, xn_ring, junk, junk_k, stat_ring, act_kts=(0, 4))
        if b == 0:
            dump("hcT", hcT.rearrange("p a b -> p (a b)"), hcT_k, [128, KT * TC], BF16)
            for c in range(4):
                dump("hxT%d" % c, hxT[:, :, c * 512:(c + 1) * 512], hxT_k[c], [128, KT, 512], BF16)
        if stop_after <= 1:
            break
        pre["gq0"] = load_wb(wb_d[WB_GQK + 0])
        pre["gk0"] = load_wb(wb_d[WB_GQK + 1])
        pre["ga0"] = load_wa(wag_d[0], 640)
        S.barrier()

        goT = arenaA[:, 0:8192].bitcast(BF16).rearrange("p (a b) -> p a b", a=KT)
        aoT = arenaA[:, 8192:16384].bitcast(BF16).rearrange("p (a b) -> p a b", a=KT)
        goT_k = [Tk("goT%d" % h) for h in range(4)]
        aoT_k = [[Tk("aoT%d_%d" % (h, c)) for c in range(4)] for h in range(8)]
        offA = 8192
        qeT, offA = carve(arenaA, offA, [128, 2, T], BF16); qeT_k = [Tk("qeT0"), Tk("qeT1")]
        ATs, offA = carve(arenaA, offA, [128, 2, T], BF16); ATs_k = [Tk("AT0"), Tk("AT1")]
        kds, offA = carve(arenaA, offA, [128, 2, 18 * 128], BF16); kds_k = [Tk("kd0"), Tk("kd1")]
        keT_items = []
        for i_ in range(2):
            ap, offA = carve(arenaA, offA, [128, 512], BF16)
            keT_items.append((ap, Tk("keT%d" % i_)))
        keT_r = Ring(keT_items)
        hl_items = []
        for i_ in range(2):
            ap, offA = carve(arenaA, offA, [128, 2, 512], BF16)
            hl_items.append((ap, Tk("hl%d" % i_)))
        hl_r = Ring(hl_items)
        assert offA <= 16384, offA
        off = 0
        LR, off = carve(arenaB, off, [64, T + TC], BF16); LR_k = Tk("LR")
        gk_tm, off = carve(arenaB, off, [128, 18 * 128], BF16); gk_tm_k = Tk("gk_tm")
        gv, off = carve(arenaB, off, [128, 18, 256], BF16); gv_k = Tk("gv")
        sgg, off = carve(arenaB, off, [128, 16, 256], BF16); sgg_k = Tk("sgg")
        SBst, off = carve(arenaB, off, [128, 16, 256], BF16); SBst_k = Tk("SBst")
        S32p, Sb32p = [], []
        for i_ in range(2):
            ap, off = carve(arenaB, off, [128, 256]); S32p.append((ap, Tk("S32_%d" % i_)))
            ap, off = carve(arenaB, off, [128, 256]); Sb32p.append((ap, Tk("Sb32_%d" % i_)))
        dlx, off = carve(arenaB, off, [128, 2, 20]); dlx_k = [Tk("dlx0"), Tk("dlx1")]
        wlr_sb, off = carve(arenaB, off, [128, KT, 32], BF16); wlr_k = Tk("wlr")
        junk2, off = carve(arenaB, off, [128, 256], BF16); junk2_k = Tk("junk2")

        def ring(n, shape, dt=F32, nm="r"):
            nonlocal off
            items = []
            for i in range(n):
                ap, off = carve(arenaB, off, shape, dt)
                items.append((ap, Tk("%s%d" % (nm, i))))
            return Ring(items)
        e1_r = ring(2, [128, 512], F32, "e1")
        Lw_r = ring(2, [128, 512], BF16, "Lw")
        E_r = ring(2, [128, 512], F32, "E")
        eb_r = ring(2, [128, 512], F32, "eb")
        enb_r = ring(2, [128, 512], F32, "enb")
        Sbf_r = ring(2, [128, 256], BF16, "Sbf")
        go_r = ring(2, [128, 256], F32, "go")
        st_r = ring(3, [128, 4], F32, "st")
        tmp_r = e1_r
        assert off <= 16384, off

        def hsrc(i, kt):
            if i < 2:
                return hcT[:, kt, i * 128:(i + 1) * 128], hcT_k
            return hxT[:, kt, (i - 2) * 128:(i - 1) * 128], hxT_k[(i - 2) // 4]

        S.dma("pool", wlr_sb, wlr_d, owner=wlr_k, writes=[wlr_k])
        S.op("dve", lambda e: e.memset(LR, 1.0), writes=[LR_k])
        for c in range(5):
            n0 = 0 if c == 0 else TC + (c - 1) * 512
            nn = TC if c == 0 else 512
            for dr in range(2):
                ps, ps_k = bank()

                def mm(e, ps=ps, c=c, dr=dr, nn=nn):
                    last = None
                    for kt in range(KT):
                        rhs = hcT[:, kt, :] if c == 0 else hxT[:, kt, (c - 1) * 512:c * 512]
                        last = e.matmul(ps[0:16, 0:nn], lhsT=wlr_sb[:, kt, dr * 16:(dr + 1) * 16], rhs=rhs, start=(kt == 0), stop=(kt == KT - 1))
                    return last
                S.op("pe", mm, reads=[wlr_k, hcT_k if c == 0 else hxT_k[c - 1]], writes=[ps_k])
                S.op("dve", lambda e, ps=ps, dr=dr, n0=n0, nn=nn: e.tensor_copy(out=LR[dr * 32:dr * 32 + 16, n0:n0 + nn], in_=ps[0:16, 0:nn]),
                     reads=[ps_k], wpart=[LR_k])
        if stop_after <= 1.1:
            dump("LR", LR, LR_k, [64, T + TC], BF16)
            break

        def decay_pair(h, tiles, balloc, mid=None):
            nt = len(tiles)
            W = nt * 128
            i0 = tiles[0]
            pzs = []
            for dr in range(2):
                pz, pz_k = balloc()

                def mmz(e, pz=pz, dr=dr):
                    last = None
                    for k, i in enumerate(tiles):
                        last = e.matmul(pz[:, k * 128:(k + 1) * 128], lhsT=LR[dr * 32:dr * 32 + 17, i * 128:(i + 1) * 128],
                                        rhs=up_bf[dr * 32:dr * 32 + 17, h * 128:(h + 1) * 128], start=True, stop=True)
                    return last
                S.op("pe", mmz, reads=[LR_k, small_k], writes=[pz_k])
                pzs.append((pz, pz_k))
            hls = []
            for dr in range(2):
                pz, pz_k = pzs[dr]
                e1, e1_k = e1_r.next()
                S.op("act", lambda e, e1=e1, pz=pz: e.activation(out=e1[:, 0:W], in_=pz[:, 0:W], func=AF.Exp, scale=-1.0), reads=[pz_k], writes=[e1_k])
                Lw, Lw_k = Lw_r.next()
                S.op("act", lambda e, e1=e1, Lw=Lw: e.activation(out=Lw[:, 0:W], in_=e1[:, 0:W], func=AF.Ln, bias=1.0), reads=[e1_k], writes=[Lw_k])
                hls.append((Lw, Lw_k))
            if mid is not None:
                mid()
            pds = []
            for dr in range(2):
                hl, hl_k = hls[dr]
                pd, pd_k = balloc()

                def mmd(e, pd=pd, hl=hl, dr=dr):
                    tri = cstb[:, 3, :] if dr == 0 else cstb[:, 2, :]
                    return e.matmul(pd[:, 0:W], lhsT=tri, rhs=hl[:, 0:W], start=True, stop=True)
                S.op("pe", mmd, reads=[hl_k, cst_k], writes=[pd_k])
                pds.append((pd, pd_k))
            pbs = []
            for dr in range(2):
                hl, hl_k = hls[dr]
                pb, pb_k = balloc()

                def mmb(e, pb=pb, hl=hl, dr=dr):
                    tri = cstb[:, 0, :] if dr == 0 else cstb[:, 1, :]
                    last = None
                    for k in range(nt):
                        last = e.matmul(pb[:, k * 128:(k + 1) * 128], lhsT=hl[:, k * 128:(k + 1) * 128], rhs=tri, start=True, stop=True)
                    return last
                S.op("pe", mmb, reads=[hl_k, cst_k], writes=[pb_k])
                pbs.append((pb, pb_k))
            for dr in range(2):
                pd, pd_k = pds[dr]
                E, E_k = E_r.next()
                S.op("act", lambda e, E=E, pd=pd: e.activation(out=E[:, 0:W], in_=pd[:, 0:W], func=AF.Exp, scale=-1.0 / 16), reads=[pd_k], writes=[E_k])
                S.op("dve", lambda e, E=E, dr=dr: e.tensor_tensor(out=kds[:, dr, i0 * 128:i0 * 128 + W], in0=gk_tm[:, i0 * 128:i0 * 128 + W], in1=E[:, 0:W], op=ALU.mult),
                     reads=[gk_tm_k, E_k], wpart=[kds_k[dr]])
            res = []
            for dr in range(2):
                pb, pb_k = pbs[dr]
                eb, eb_k = eb_r.next()
                enb, enb_k = enb_r.next()
                S.op("act", lambda e, eb=eb, pb=pb: e.activation(out=eb[:, 0:W], in_=pb[:, 0:W], func=AF.Exp, scale=-1.0 / 16), reads=[pb_k], writes=[eb_k])
                if nt == 4:
                    S.op("act", lambda e, enb=enb, pb=pb: e.activation(out=enb[:, 0:W], in_=pb[:, 0:W], func=AF.Exp, scale=1.0 / 16), reads=[pb_k], writes=[enb_k])
                col = 127 if dr == 0 else 0
                S.op("dve", lambda e, eb=eb, dr=dr, col=col: e.tensor_copy(out=dlx[:, dr, i0:i0 + nt], in_=eb[:, 0:W].rearrange("p (k t) -> p k t", k=nt)[:, :, col]),
                     reads=[eb_k], wpart=[dlx_k[dr]])
                res.append((eb, eb_k, enb, enb_k))
            return res

        for h in range(4):
            wq, wq_k = take("gq%d" % h, lambda: load_wb(wb_d[WB_GQK + 2 * h]))
            wk, wk_k = take("gk%d" % h, lambda: load_wb(wb_d[WB_GQK + 2 * h + 1]))
            wa, wa_k = take("ga%d" % h, lambda: load_wa(wag_d[h], 640))
            for i0 in range(0, 18, 4):
                tl = list(range(i0, min(18, i0 + 4)))
                ps, ps_k = bank()

                def mmk(e, ps=ps, tl=tl):
                    last = None
                    for k, i in enumerate(tl):
                        for kt in range(KT):
                            last = e.matmul(ps[:, k * 128:(k + 1) * 128], lhsT=hsrc(i, kt)[0], rhs=wa[:, kt, 0:128], start=(kt == 0), stop=(kt == KT - 1))
                    return last
                S.op("pe", mmk, reads=[wa_k] + list({id(hsrc(i, 0)[1]): hsrc(i, 0)[1] for i in tl}.values()), writes=[ps_k])
                S.op("act", lambda e, ps=ps, tl=tl: e.copy(out=gk_tm[:, tl[0] * 128:(tl[-1] + 1) * 128], in_=ps[:, 0:len(tl) * 128]), reads=[ps_k], wpart=[gk_tm_k])
            bProj = Ring(banks[0:4]); bDP = Ring(banks[4:8])

            def tm_vg(i0, do_v=True, do_g=True):
                if do_v:
                    tm_v(i0)
                if do_g and i0 >= 2:
                    tm_g(i0)

            def tm_v(i0):
                ps, ps_k = bDP.next()

                def mmv(e, ps=ps):
                    last = None
                    for k in range(2):
                        for kt in range(KT):
                            last = e.matmul(ps[:, k * 256:(k + 1) * 256], lhsT=hsrc(i0 + k, kt)[0], rhs=wa[:, kt, 128:384], start=(kt == 0), stop=(kt == KT - 1))
                    return last
                S.op("pe", mmv, reads=[wa_k, hsrc(i0, 0)[1]], writes=[ps_k])
                S.op("dve", lambda e, ps=ps: e.tensor_copy(out=gv[:, i0:i0 + 2, :], in_=ps.rearrange("p (a b) -> p a b", a=2)), reads=[ps_k], wpart=[gv_k])

            def tm_g(i0):
                if True:
                    ps2, ps2_k = bDP.next()

                    def mmg(e, ps2=ps2):
                        last = None
                        for k in range(2):
                            for kt in range(KT):
                                last = e.matmul(ps2[:, k * 256:(k + 1) * 256], lhsT=hsrc(i0 + k, kt)[0], rhs=wa[:, kt, 384:640], start=(kt == 0), stop=(kt == KT - 1))
                        return last
                    S.op("pe", mmg, reads=[wa_k, hsrc(i0, 0)[1]], writes=[ps2_k])
                    hl, hl_k = hl_r.next()
                    tmp = hl.rearrange("p a b -> p (a b)").bitcast(F32)
                    S.op("act", lambda e: e.activation(out=tmp, in_=ps2, func=AF.Silu), reads=[ps2_k], writes=[hl_k])
                    S.op("dve", lambda e: e.tensor_tensor(out=sgg[:, i0 - 2:i0, :], in0=tmp.rearrange("p (a b) -> p a b", a=2),
                                                          in1=gn2, op=ALU.mult), reads=[hl_k, small_k], wpart=[sgg_k])
            tm_vg(0)
            if stop_after <= 1.2:
                dump("gk_tm", gk_tm, [gk_tm_k, gv_k, sgg_k], [128, 18 * 128], BF16)
                break
            decay_pair(h, [0, 1], bDP.next)

            def proj_qk(c):
                pq, pq_k = bProj.next()
                pk, pk_k = bProj.next()
                for (w, w_k, pp, pp_k) in ((wq, wq_k, pq, pq_k), (wk, wk_k, pk, pk_k)):
                    def mm(e, pp=pp, w=w):
                        last = None
                        for kt in range(KT):
                            last = e.matmul(pp, lhsT=w[:, kt, :], rhs=hxT[:, kt, c * 512:(c + 1) * 512], start=(kt == 0), stop=(kt == KT - 1))
                        return last
                    S.op("pe", mm, reads=[w_k, hxT_k[c]], writes=[pp_k])
                return pq, pq_k, pk, pk_k

            def qk_scale(c, P, ebs):
                pq, pq_k, pk, pk_k = P
                kes = []
                for dr in range(2):
                    eb, eb_k, enb, enb_k = ebs[dr]
                    S.op("dve", lambda e, dr=dr, eb=eb: e.scalar_tensor_tensor(out=qeT[:, dr, c * 512:(c + 1) * 512], in0=pq, scalar=GLA_QSCALE, in1=eb, op0=ALU.mult, op1=ALU.mult),
                         reads=[pq_k, eb_k], wpart=[qeT_k[dr]])
                    keT, keT_k = keT_r.next()
                    S.op("dve", lambda e, keT=keT, enb=enb: e.tensor_tensor(out=keT, in0=pk, in1=enb, op=ALU.mult), reads=[pk_k, enb_k], writes=[keT_k])
                    kes.append((keT, keT_k))
                return kes

            def amat(c, kes):
                for dr in range(2):
                    keT, keT_k = kes[dr]
                    pa, pa_k = bDP.next()

                    def mma(e, pa=pa, keT=keT, dr=dr):
                        last = None
                        for k in range(4):
                            t0 = c * 512 + k * 128
                            last = e.matmul(pa[:, k * 128:(k + 1) * 128], lhsT=keT[:, k * 128:(k + 1) * 128], rhs=qeT[:, dr, t0:t0 + 128], start=True, stop=True)
                        return last
                    S.op("pe", mma, reads=[keT_k, qeT_k[dr]], writes=[pa_k])
                    S.op("dve", lambda e, pa=pa, dr=dr: e.tensor_tensor(out=ATs[:, dr, c * 512:(c + 1) * 512], in0=pa, in1=mask4[:, dr, :], op=ALU.mult),
                         reads=[pa_k, cst_k], wpart=[ATs_k[dr]])
            def tm_group(c):
                def f():
                    tm_vg(2 + 4 * c, do_g=False)
                    tm_vg(4 + 4 * c, do_g=False)
                    if c == 0:
                        for i0 in range(2, 18, 2):
                            tm_vg(i0, do_v=False)
                return f
            P = proj_qk(0)
            ebs = decay_pair(h, [2, 3, 4, 5], bDP.next, mid=tm_group(0))
            for c in range(4):
                kes = qk_scale(c, P, ebs)
                if c + 1 < 4:
                    P = proj_qk(c + 1)
                    ebs = decay_pair(h, [2 + 4 * (c + 1) + k for k in range(4)], bDP.next, mid=tm_group(c + 1))
                amat(c, kes)
            bU = Ring(banks[0:2]); bOo = Ring(banks[2:5]); bT = Ring(banks[5:7])

            cur = {0: 0, 1: 0}

            def stbuf(dr):
                return (S32p if dr == 0 else Sb32p)[cur[dr]]

            def state_step(i, dr, first):
                pu, pu_k = bU.next()
                S.op("pe", lambda e: e.matmul(pu[:, 0:256], lhsT=kds[:, dr, i * 128:(i + 1) * 128], rhs=gv[:, i, :], start=True, stop=True),
                     reads=[kds_k[dr], gv_k], writes=[pu_k])
                old, old_k = stbuf(dr)
                cur[dr] ^= 1
                new, new_k = stbuf(dr)
                if first:
                    S.op("dve", lambda e: e.tensor_copy(out=new, in_=pu[:, 0:256]), reads=[pu_k], writes=[new_k])
                else:
                    S.op("dve", lambda e: e.scalar_tensor_tensor(out=new, in0=old, scalar=dlx[:, dr, i:i + 1], in1=pu[:, 0:256], op0=ALU.mult, op1=ALU.add),
                         reads=[pu_k, dlx_k[dr], old_k], writes=[new_k])
            for step, i in enumerate((0, 1)):
                state_step(i, 0, first=(step == 0))
            for step, i in enumerate((1, 0)):
                state_step(i, 1, first=(step == 0))
            if b == 0 and h == 0:
                dump("sf", stbuf(0)[0], stbuf(0)[1], [128, 256])
                dump("sb", stbuf(1)[0], stbuf(1)[1], [128, 256])
            for n in range(15, -1, -1):
                sbuf_, sbuf_k = stbuf(1)
                S.op("act", lambda e, n=n, sbuf_=sbuf_: e.copy(out=SBst[:, n, :], in_=sbuf_), reads=[sbuf_k], wpart=[SBst_k])
                if n > 0:
                    state_step(n + 2, 1, first=False)
            live = {}

            def core(n):
                i = n + 2
                Sbf, Sbf_k = Sbf_r.next()
                s32_, s32_k = stbuf(0)
                S.op("act", lambda e: e.copy(out=Sbf, in_=s32_), reads=[s32_k], writes=[Sbf_k])
                pso, pso_k = bOo.next()
                tok = slice(n * 128, (n + 1) * 128)

                def mmo(e):
                    e.matmul(pso[:, 0:256], lhsT=qeT[:, 0, tok], rhs=Sbf, start=True, stop=False)
                    e.matmul(pso[:, 0:256], lhsT=ATs[:, 0, tok], rhs=gv[:, i, :], start=False, stop=False)
                    e.matmul(pso[:, 0:256], lhsT=qeT[:, 1, tok], rhs=SBst[:, n, :], start=False, stop=False)
                    return e.matmul(pso[:, 0:256], lhsT=ATs[:, 1, tok], rhs=gv[:, i, :], start=False, stop=True)
                if n < 15:
                    state_step(i, 0, first=False)
                S.op("pe", mmo, reads=[qeT_k[0], qeT_k[1], ATs_k[0], ATs_k[1], Sbf_k, SBst_k, gv_k], writes=[pso_k])
                live[n] = [pso, pso_k]

            def norm(n):
                pso, pso_k = live[n]
                stt, stt_k = st_r.next()
                S.op("act", lambda e: e.activation(out=junk2, in_=pso[:, 0:256], func=AF.Square, accum_out=stt[:, 1:2]), reads=[pso_k], writes=[junk2_k, stt_k])
                S.op("act", lambda e: e.activation(out=stt[:, 2:3], in_=stt[:, 1:2], func=AF.Ln, scale=1.0 / 256, bias=EPS), reads=[stt_k], writes=[stt_k])
                S.op("act", lambda e: e.activation(out=stt[:, 3:4], in_=stt[:, 2:3], func=AF.Exp, scale=-0.5), reads=[stt_k], writes=[stt_k])
                go, go_k = go_r.next()
                S.op("dve", lambda e: e.scalar_tensor_tensor(out=go, in0=pso[:, 0:256], scalar=stt[:, 3:4], in1=sgg[:, n, :], op0=ALU.mult, op1=ALU.mult),
                     reads=[pso_k, stt_k, sgg_k], writes=[go_k])
                pst, pst_k = bT.next()

                def trs(e):
                    e.transpose(pst[:, 0:128], go[:, 0:128], ident)
                    return e.transpose(pst[:, 128:256], go[:, 128:256], ident)
                S.op("pe", trs, reads=[go_k, cst_k], writes=[pst_k])
                live[n] = [pst, pst_k]

            def evac(n):
                pst, pst_k = live.pop(n)
                tok = slice(n * 128, (n + 1) * 128)
                S.op("dve", lambda e: e.tensor_copy(out=goT[:, 2 * h:2 * h + 2, tok], in_=pst[:, 0:256].rearrange("p (a b) -> p a b", a=2)),
                     reads=[pst_k], wpart=[goT_k[h]])
            for n in range(18):
                if n < 16:
                    core(n)
                if 0 <= n - 1 < 16:
                    norm(n - 1)
                if 0 <= n - 2 < 16:
                    evac(n - 2)
        if b == 0 and stop_after >= 2:
            dump("goT", goT.rearrange("p a b -> p (a b)"), goT_k, [128, KT * T], BF16)
        if stop_after <= 2:
            break
        pre["k0"] = load_wb(wb_d[WB_K + 0])
        pre["k1"] = load_wb(wb_d[WB_K + 1])
        pre["wv"] = load_wa(wav_d, 256)
        S.barrier()

        off = 0
        kT, off = carve(arenaB, off, [128, 2, T + TC], BF16); kT_k = [Tk("kT0"), Tk("kT1")]
        Vt, off = carve(arenaB, off, [128, 18, 256], BF16); V_k = Tk("V")
        CS, off = carve(arenaB, off, [128, T]); SSa, off = carve(arenaB, off, [128, T]); cs_k = Tk("cs")
        sq_r = ring(2, [128, 512], BF16, "sq")
        ln_r = ring(2, [128, 512], F32, "ln")
        rr_r = ring(2, [128, 512], F32, "rr")
        t1_r = ring(2, [128, 512], F32, "t1")
        t2_r = ring(2, [128, 512], F32, "t2")
        qT_r = ring(2, [128, 512], BF16, "qT")
        PT_r = ring(4, [128, 512], BF16, "PT")
        rz_r = ring(2, [128, 512], F32, "rz")
        assert off <= 16384, off
        S.dma("sp", CS, cs_d, owner=cs_k, writes=[cs_k])
        S.dma("sp", SSa, ssa_d, owner=cs_k, writes=[cs_k])
        bS = Ring(banks[0:3]); bO = Ring(banks[3:5]); bZ = Ring(banks[5:7]); bM = Ring(banks[7:8])

        def proj_b(w, w_k, c5, bring=None):
            ps, ps_k = (bring or bM).next()
            nn = TC if c5 == 0 else 512

            def mm(e):
                last = None
                for kt in range(KT):
                    rhs = hcT[:, kt, :] if c5 == 0 else hxT[:, kt, (c5 - 1) * 512:c5 * 512]
                    last = e.matmul(ps[:, 0:nn], lhsT=w[:, kt, :], rhs=rhs, start=(kt == 0), stop=(kt == KT - 1))
                return last
            S.op("pe", mm, reads=[w_k, hcT_k if c5 == 0 else hxT_k[c5 - 1]], writes=[ps_k])
            return ps, ps_k, nn

        def norm_rope_steps(ps, ps_k, nn, gi, tok0, dst, dst_k):
            g = qg[:, gi:gi + 1]
            sq, sq_k = sq_r.next()
            t1, t1_k = t1_r.next()
            lnv, lnv_k = ln_r.next()
            rr, rr_k = rr_r.next()
            steps = []
            steps.append(lambda: S.op("act", lambda e: e.activation(out=sq[:, 0:nn], in_=ps[:, 0:nn], func=AF.Square), reads=[ps_k], writes=[sq_k]))
            if tok0 is None:
                steps.append(lambda: S.op("dve", lambda e: e.tensor_scalar(out=t1[:, 0:nn], in0=ps[:, 0:nn], scalar1=g, scalar2=None, op0=ALU.mult),
                                          reads=[ps_k, small_k], writes=[t1_k]))
            else:
                tok = slice(tok0, tok0 + nn)
                t2, t2_k = t2_r.next()
                steps.append(lambda: S.op("dve", lambda e: e.scalar_tensor_tensor(out=t1[:, 0:nn], in0=ps[:, 0:nn], scalar=g, in1=CS[:, tok], op0=ALU.mult, op1=ALU.mult),
                                          reads=[ps_k, small_k, cs_k], writes=[t1_k]))
                steps.append(lambda: S.op("dve", lambda e: e.scalar_tensor_tensor(out=t2[0:64, 0:nn], in0=ps[64:128, 0:nn], scalar=qg[64:128, gi:gi + 1], in1=SSa[64:128, tok],
                                                                                  op0=ALU.mult, op1=ALU.mult), reads=[ps_k, small_k, cs_k], writes=[t2_k]))
                steps.append(lambda: S.op("dve", lambda e: e.scalar_tensor_tensor(out=t2[64:128, 0:nn], in0=ps[0:64, 0:nn], scalar=qg[0:64, gi:gi + 1], in1=SSa[0:64, tok],
                                                                                  op0=ALU.mult, op1=ALU.mult), reads=[ps_k, small_k, cs_k], writes=[t2_k]))
                steps.append(lambda: S.op("dve", lambda e: e.tensor_tensor(out=t1[:, 0:nn], in0=t1[:, 0:nn], in1=t2[:, 0:nn], op=ALU.add), reads=[t1_k, t2_k], writes=[t1_k]))
            steps.append(lambda: S.op("pe", lambda e: e.matmul(ps[:, 0:nn], lhsT=ones_bf, rhs=sq[:, 0:nn], start=True, stop=True), reads=[sq_k, ones_k], writes=[ps_k]))
            steps.append(lambda: S.op("act", lambda e: e.activation(out=lnv[:, 0:nn], in_=ps[:, 0:nn], func=AF.Ln, scale=1.0 / 128, bias=EPS), reads=[ps_k], writes=[lnv_k]))
            steps.append(lambda: S.op("act", lambda e: e.activation(out=rr[:, 0:nn], in_=lnv[:, 0:nn], func=AF.Exp, scale=-0.5), reads=[lnv_k], writes=[rr_k]))
            steps.append(lambda: S.op("dve", lambda e: e.tensor_tensor(out=dst, in0=t1[:, 0:nn], in1=rr[:, 0:nn], op=ALU.mult), reads=[t1_k, rr_k], writes=[dst_k]))
            return steps

        def norm_rope(ps, ps_k, nn, gi, tok0, dst, dst_k):
            for st_ in norm_rope_steps(ps, ps_k, nn, gi, tok0, dst, dst_k):
                st_()

        bMp = Ring(banks[3:8])
        klists = []
        for g in range(2):
            w, w_k = take("k%d" % g, lambda: load_wb(wb_d[WB_K + g]))
            for c5 in range(5):
                def mk(g=g, c5=c5, w=w, w_k=w_k):
                    box = {}

                    def first():
                        ps, ps_k, nn = proj_b(w, w_k, c5, bMp)
                        n0 = 0 if c5 == 0 else TC + (c5 - 1) * 512
                        box["steps"] = norm_rope_steps(ps, ps_k, nn, 1, None if c5 == 0 else (c5 - 1) * 512, kT[:, g, n0:n0 + nn], kT_k[g])
                    nsteps = 6 if c5 == 0 else 9
                    return [first] + [(lambda i=i: box["steps"][i]()) for i in range(nsteps)]
                klists.append(mk())
        wv, wv_k = take("wv", lambda: load_wa(wav_d, 256))
        v_todo = list(range(18))

        def v_tile(i):
            ps, ps_k = bS.next()

            def mm(e, ps=ps, i=i):
                last = None
                for kt in range(KT):
                    last = e.matmul(ps[:, 0:256], lhsT=hsrc(i, kt)[0], rhs=wv[:, kt, 0:256], start=(kt == 0), stop=(kt == KT - 1))
                return last
            S.op("pe", mm, reads=[wv_k, hsrc(i, 0)[1]], writes=[ps_k])
            S.op("act" if i % 2 else "dve", (lambda e, ps=ps, i=i: e.copy(out=Vt[:, i, :], in_=ps[:, 0:256])) if i % 2 else
                 (lambda e, ps=ps, i=i: e.tensor_copy(out=Vt[:, i, :], in_=ps[:, 0:256])), reads=[ps_k], wpart=[V_k])
        for p in range(0, len(klists), 2):
            la, lb = klists[p], klists[p + 1]
            for i in range(max(len(la), len(lb))):
                if i < len(la):
                    la[i]()
                if i < len(lb):
                    lb[i]()
                if v_todo and i % 2 == 1:
                    v_tile(v_todo.pop(0))
        while v_todo:
            v_tile(v_todo.pop(0))
        if b == 0:
            dump("kT", kT.rearrange("p a b -> p (a b)"), kT_k, [128, 2 * (T + TC)], BF16)
            dump("V", Vt.rearrange("p a b -> p (a b)"), V_k, [128, 18 * 256], BF16)

        def q_prep_steps(h, c):
            qT, qT_k = qT_r.next()
            box = {}

            def first():
                w, w_k = get_wq(h)
                ps, ps_k, nn = proj_b(w, w_k, c + 1)
                box["steps"] = norm_rope_steps(ps, ps_k, 512, 0, c * 512, qT, qT_k)
            steps = [first] + [(lambda i=i: box["steps"][i]()) for i in range(9)]
            return steps, (qT, qT_k)

        def attend(h, c, qT, qT_k, side=(), pre_qk=None, nxt=None):
            g = h // 4
            side = list(side)
            pO, pO_k = bO.next()
            pZ, pZ_k = bZ.next()
            LOOK = 2
            pSs = dict(pre_qk or {})
            nxt_store = {}

            def qk_for(gg, qTx, qTx_k, j, store):
                pS, pS_k = bS.next()
                S.op("pe", lambda e: e.matmul(pS, lhsT=kT[:, gg, j * 128:(j + 1) * 128], rhs=qTx, start=True, stop=True),
                     reads=[kT_k[gg], qTx_k], writes=[pS_k])
                store[j] = (pS, pS_k)
            for j in range(LOOK):
                if j not in pSs:
                    qk_for(g, qT, qT_k, j, pSs)
            for j in range(18):
                if j + LOOK < 18:
                    qk_for(g, qT, qT_k, j + LOOK, pSs)
                elif nxt is not None:
                    assert not side
                    qk_for(nxt[0] // 4, nxt[1], nxt[2], j + LOOK - 18, nxt_store)
                if j in (1, 2, 3, 4, 6, 8, 10, 12, 13, 14) and side:
                    side.pop(0)()
                pS, pS_k = pSs.pop(j)
                PT, PT_k = PT_r.next()
                S.op("act", lambda e, pS=pS, PT=PT: e.activation(out=PT, in_=pS, func=AF.Exp, scale=ATT_SCALE, bias=-ATT_SHIFT),
                     reads=[pS_k], writes=[PT_k])

                def pv(e, PT=PT, j=j):
                    e.matmul(pO, lhsT=Vt[:, j, g * 128:(g + 1) * 128], rhs=PT, start=(j == 0), stop=(j == 17))
                    return e.matmul(pZ, lhsT=ones_bf, rhs=PT, start=(j == 0), stop=(j == 17))
                S.op("pe", pv, reads=[V_k, PT_k, ones_k], writes=[pO_k, pZ_k])
            assert not side
            rz, rz_k = rz_r.next()
            S.op("dve", lambda e: e.reciprocal(out=rz, in_=pZ), reads=[pZ_k], writes=[rz_k])
            S.op("dve", lambda e: e.tensor_tensor(out=aoT[:, h, c * 512:(c + 1) * 512], in0=pO, in1=rz, op=ALU.mult),
                 reads=[pO_k, rz_k], writes=[aoT_k[h][c]])
            return nxt_store

        tiles = [(h, c) for h in range(8) for c in range(4)]
        wq_cur = {}

        def get_wq(h):
            if h not in wq_cur:
                wq_cur[h] = load_wb(wb_d[WB_Q + h])
            return wq_cur[h]
        steps0, pend = q_prep_steps(0, 0)
        for st_ in steps0:
            st_()
        pre_qk = None
        for ti, (h, c) in enumerate(tiles):
            side, nxt, nxt_arg = (), None, None
            if ti + 1 < len(tiles):
                side, nxt = q_prep_steps(*tiles[ti + 1])
                nxt_arg = (tiles[ti + 1][0], nxt[0], nxt[1])
            pre_qk = attend(h, c, *pend, side=side, pre_qk=pre_qk, nxt=nxt_arg)
            pend = nxt
        if b == 0:
            dump("aoT", aoT.rearrange("p a b -> p (a b)"), [k for row in aoT_k for k in row], [128, KT * T], BF16)
        if stop_after <= 3:
            break
        pre["mg0_0"] = load_wb(wb_d[WB_GA + 0])
        pre["my0_0"] = load_wb(wap_d[0])
        S.barrier()

        off = 0
        mT, off = carve(arenaB, off, [128, KT, T], BF16); mT_k = [Tk("mT%d" % c) for c in range(4)]
        sg_r = ring(2, [128, 512], F32, "sg")
        ma_r = ring(5, [128, 512], F32, "ma")
        assert off <= 16384, off
        bG = Ring(banks[0:4]); bY = Ring(banks[4:8])
        for f in range(8):
            mas = []
            for half in range(2):
                wg, wg_k = take("mg%d_%d" % (f, half), lambda: load_wb(wb_d[(WB_GA if half == 0 else WB_GB) + f]))
                wy, wy_k = take("my%d_%d" % (f, half), lambda: load_wb(wap_d[f] if half == 0 else wgp_d[f]))
                src = aoT if half == 0 else goT
                for c in range(4):
                    pG, pG_k = bG.next()
                    pY, pY_k = bY.next()

                    def mmg(e, pG=pG, c=c, wg=wg):
                        last = None
                        for kt in range(KT):
                            last = e.matmul(pG, lhsT=wg[:, kt, :], rhs=hxT[:, kt, c * 512:(c + 1) * 512], start=(kt == 0), stop=(kt == KT - 1))
                        return last
                    S.op("pe", mmg, reads=[wg_k, hxT_k[c]], writes=[pG_k])

                    def mmy(e, pY=pY, c=c, wy=wy, src=src):
                        last = None
                        for kt in range(KT):
                            last = e.matmul(pY, lhsT=wy[:, kt, :], rhs=src[:, kt, c * 512:(c + 1) * 512], start=(kt == 0), stop=(kt == KT - 1))
                        return last
                    srck = [aoT_k[hh][c] for hh in range(8)] if half == 0 else goT_k
                    S.op("pe", mmy, reads=[wy_k] + list(srck), writes=[pY_k])
                    sg, sg_k = sg_r.next()
                    S.op("act", lambda e, pG=pG, sg=sg: e.activation(out=sg, in_=pG, func=AF.Sigmoid), reads=[pG_k], writes=[sg_k])
                    if half == 0:
                        ma, ma_k = ma_r.next()
                        S.op("dve", lambda e, pY=pY, sg=sg, ma=ma: e.tensor_tensor(out=ma, in0=pY, in1=sg, op=ALU.mult), reads=[pY_k, sg_k], writes=[ma_k])
                        mas.append((ma, ma_k))
                    else:
                        ma, ma_k = mas[c]
                        mb, mb_k = ma_r.next()
                        S.op("dve", lambda e, pY=pY, sg=sg, mb=mb: e.tensor_tensor(out=mb, in0=pY, in1=sg, op=ALU.mult), reads=[pY_k, sg_k], writes=[mb_k])
                        S.op("dve", lambda e, ma=ma, mb=mb, f=f, c=c: e.tensor_tensor(out=mT[:, f, c * 512:(c + 1) * 512], in0=ma, in1=mb, op=ALU.add),
                             reads=[ma_k, mb_k], wpart=[mT_k[c]])
        if b == 0:
            dump("mT", mT.rearrange("p a b -> p (a b)"), mT_k, [128, KT * T], BF16)
        if stop_after <= 4:
            break
        pre["wo0"] = load_wa(wo_d[:, :, 0:512], 512)
        S.barrier()

        x1d_k = [Tk("x1d%d" % i) for i in range(16)]
        offA = 0
        wo_hi, offA = carve(arenaA, offA, [128, KT, 512], BF16); wo_k = Tk("wo")
        xl_items = []
        for i in range(4):
            ap, offA = carve(arenaA, offA, [128, D])
            xl_items.append((ap, Tk("xl%d" % i)))
        xl_r = Ring(xl_items)
        x1_items = []
        for i in range(8):
            ap, offA = carve(arenaA, offA, [128, D])
            x1_items.append((ap, Tk("x1t%d" % i)))
        x1_r = Ring(x1_items)
        assert offA <= 16384, offA
        off = 8192
        xn_r = ring(8, [128, D], BF16, "xn5")
        junk5, off = carve(arenaB, off, [128, D], BF16); junk5_k = Tk("junk5")
        st5_r = ring(3, [128, 12], F32, "st5")
        assert off <= 16384, off
        wo0, wo0_k = pre.pop("wo0")
        S.dma("pool", wo_hi, wo_d[:, :, 512:1024], owner=wo_k, writes=[wo_k])
        bY5 = Ring(banks[0:4])

        def make_x1(i, ap, tk):
            xl, xl_k = xl_r.next()
            S.dma("sp", xl, x_d[b, i * 128:(i + 1) * 128, :], owner=xl_k, writes=[xl_k])
            for cc in range(2):
                ps, ps_k = bY5.next()

                def mm(e, ps=ps, cc=cc):
                    last = None
                    for kt in range(KT):
                        rhs = wo0[:, kt, 0:512] if cc == 0 else wo_hi[:, kt, :]
                        last = e.matmul(ps, lhsT=mT[:, kt, i * 128:(i + 1) * 128], rhs=rhs, start=(kt == 0), stop=(kt == KT - 1))
                    return last
                S.op("pe", mm, reads=[mT_k[i // 4], wo0_k if cc == 0 else wo_k], writes=[ps_k])
                S.op("dve", lambda e, ps=ps, cc=cc: e.tensor_tensor(out=ap[:, cc * 512:(cc + 1) * 512], in0=ps, in1=G12[:, b, 0, cc * 512:(cc + 1) * 512], op=ALU.mult),
                     reads=[ps_k, G12_k], wpart=[tk])
                S.op("dve", lambda e, cc=cc: e.tensor_tensor(out=ap[:, cc * 512:(cc + 1) * 512], in0=ap[:, cc * 512:(cc + 1) * 512], in1=xl[:, cc * 512:(cc + 1) * 512], op=ALU.add),
                     reads=[xl_k, tk], wpart=[tk])
            S.dma("pool", out_d[b, i * 128:(i + 1) * 128, :], ap, owner=tk, reads=[tk], writes=[x1d_k[i]], final=True)

        def bank45():
            bnk = banks[4 + (st["bank"] % 4)]
            st["bank"] += 1
            return bnk
        st["bankfn"] = bank45
        norm_transpose(make_x1, 16, hxT, hxT_k, lambda kt: sc2e[:, kt, b:b + 1], lambda kt: sh2T[:, kt, b:b + 1],
                       x1_r, xn_r, junk5, junk5_k, st5_r, act_kts=(0, 1, 2, 3, 4, 5, 6, 7))
        st["bankfn"] = bank
        if b == 0:
            dump("h2T", hxT.rearrange("p a b -> p (a b)"), hxT_k, [128, KT * T], BF16)
        if stop_after <= 5:
            break
        pre["fa0_0"] = load_wb(wfi_d[0])
        pre["fb0_0"] = load_wb(wfi_d[NJ + 0])
        S.barrier()

        offA = 0
        uT, offA = carve(arenaA, offA, [128, NJ, 1024], BF16); uT_k = [Tk("uT0"), Tk("uT1")]
        xr6_items, stg_items = [], []
        for i in range(2):
            ap, offA = carve(arenaA, offA, [128, D]); xr6_items.append((ap, Tk("xr6_%d" % i)))
        for i in range(2):
            ap, offA = carve(arenaA, offA, [128, D]); stg_items.append((ap, Tk("stg%d" % i)))
        xr6_r, stg_r = Ring(xr6_items), Ring(stg_items)
        assert offA <= 16384, offA
        off = 0
        WoG, off = carve(arenaB, off, [128, NJ, D], BF16); WoG_k = Tk("WoG")
        sa_r = ring(2, [128, 512], F32, "sa")
        wf_r = ring(2, [128, D], BF16, "wf")
        assert off <= 16384, off
        WoG_kj = [Tk("WoG%d" % j) for j in range(2)]

        def load_wog(piece):
            j2 = piece // 6
            S.dma("pool", WoG[:, 2 * piece:2 * piece + 2, :], wfo_d[2 * piece:2 * piece + 2].rearrange("j p n -> p j n"), owner=WoG_kj[j2], writes=[])
            WoG_kj[j2].w = (WoG_kj[j2].dsem, WoG_kj[j2].dcnt)
        bA = Ring(banks[0:2]); bB = Ring(banks[2:4]); bF = Ring(banks[4:8])
        for H in range(2):
            for j in range(NJ):
                wa_j, wa_jk = take("fa%d_%d" % (j, H), lambda: load_wb(wfi_d[j]))
                wb_j, wb_jk = take("fb%d_%d" % (j, H), lambda: load_wb(wfi_d[NJ + j]))
                if H == 0 and 1 <= j <= 11:
                    load_wog(j - 1)
                for cl in range(2):
                    c = 2 * H + cl
                    pA, pA_k = bA.next()
                    pB, pB_k = bB.next()

                    def mma(e, pA=pA, c=c, wa_j=wa_j):
                        last = None
                        for kt in range(KT):
                            last = e.matmul(pA, lhsT=wa_j[:, kt, :], rhs=hxT[:, kt, c * 512:(c + 1) * 512], start=(kt == 0), stop=(kt == KT - 1))
                        return last
                    S.op("pe", mma, reads=[wa_jk, hxT_k[c]], writes=[pA_k])

                    def mmb(e, pB=pB, c=c, wb_j=wb_j):
                        last = None
                        for kt in range(KT):
                            last = e.matmul(pB, lhsT=wb_j[:, kt, :], rhs=hxT[:, kt, c * 512:(c + 1) * 512], start=(kt == 0), stop=(kt == KT - 1))
                        return last
                    S.op("pe", mmb, reads=[wb_jk, hxT_k[c]], writes=[pB_k])
                    sa, sa_k = sa_r.next()
                    S.op("act", lambda e, pA=pA, sa=sa: e.activation(out=sa, in_=pA, func=AF.Silu), reads=[pA_k], writes=[sa_k])
                    S.op("dve", lambda e, pB=pB, sa=sa, j=j, cl=cl: e.tensor_tensor(out=uT[:, j, cl * 512:(cl + 1) * 512], in0=pB, in1=sa, op=ALU.mult),
                         reads=[pB_k, sa_k], wpart=[uT_k[cl]])
            if H == 0:
                for j_ in range(2):
                    pre["fa%d_1" % j_] = load_wb(wfi_d[j_])
                    pre["fb%d_1" % j_] = load_wb(wfi_d[NJ + j_])
            for tl in range(8):
                i = H * 8 + tl
                xr6, xr6_k = xr6_r.next()
                S.dma("sp", xr6, out_d[b, i * 128:(i + 1) * 128, :], owner=xr6_k, reads=[x1d_k[i]], writes=[xr6_k])
                stg, stg_k = stg_r.next()
                for cc in range(2):
                    pF, pF_k = bF.next()

                    def mmf(e, pF=pF, tl=tl, cc=cc):
                        last = None
                        for j in range(NJ):
                            last = e.matmul(pF, lhsT=uT[:, j, tl * 128:(tl + 1) * 128], rhs=WoG[:, j, cc * 512:(cc + 1) * 512], start=(j == 0), stop=(j == NJ - 1))
                        return last
                    S.op("pe", mmf, reads=[uT_k[tl // 4]] + WoG_kj, writes=[pF_k])
                    S.op("dve", lambda e, pF=pF, cc=cc, stg=stg: e.tensor_tensor(out=stg[:, cc * 512:(cc + 1) * 512], in0=pF, in1=G12[:, b, 1, cc * 512:(cc + 1) * 512], op=ALU.mult),
                         reads=[pF_k, G12_k], wpart=[stg_k])
                    S.op("dve", lambda e, cc=cc, stg=stg, xr6=xr6: e.tensor_tensor(out=stg[:, cc * 512:(cc + 1) * 512], in0=stg[:, cc * 512:(cc + 1) * 512], in1=xr6[:, cc * 512:(cc + 1) * 512], op=ALU.add),
                         reads=[xr6_k, stg_k], wpart=[stg_k])
                S.dma("pool", out_d[b, i * 128:(i + 1) * 128, :], stg, owner=stg_k, reads=[stg_k], writes=[x1d_k[i]], final=True)
        S.barrier()

    S.finish()
    return nc, dumps


_CACHE = {}


def kernel(**inputs):
    shared = prep_shared(inputs)
    if "nc" not in _CACHE:
        _CACHE["nc"] = build()[0]
    nc = _CACHE["nc"]
    in_maps = []
    for core in range(NCORES):
        m = dict(shared)
        m.update(prep_core(inputs, core))
        in_maps.append(m)
    res = run_bass_kernel_spmd(nc, in_maps, core_ids=list(range(NCORES)))
    out = np.concatenate([np.asarray(r["out"]) for r in res.results], axis=0)
    return out.astype(np.float32)
```
